# Optimizing a Trainium2 kernel written in Bass

```python
import math
import jax, jax.numpy as jnp
from jax import lax
import numpy as np

D_MODEL = 1024
BATCH = 1
SEQ = 16384
DEPTH = 4

N_MIXERS = 3
ALPHA = (2.0 * DEPTH) ** 0.25
BETA = (8.0 * DEPTH) ** -0.25
LN_EPS = 1e-5
HEAD_NORM_EPS = 1e-6

HGRN_EXPAND = 128
HGRN_HEADS = D_MODEL // HGRN_EXPAND
HGRN_DK = HGRN_EXPAND
HGRN_DV = D_MODEL // HGRN_HEADS
HGRN_CHUNK = 64

MLSTM_HEADS = 8
MLSTM_DV = D_MODEL // MLSTM_HEADS
MLSTM_DQK = MLSTM_DV // 2
MLSTM_CONV = 4
MLSTM_CHUNK = 64
MLSTM_QK_W = MLSTM_HEADS * MLSTM_DQK
MLSTM_IN_W = 2 * MLSTM_QK_W + 2 * D_MODEL + 2 * MLSTM_HEADS

S5_GROUP_CH = 16
S5_GROUPS = D_MODEL // S5_GROUP_CH
S5_STATE = 64
S5_CHUNK = 1024
S5_DT_MIN = 1e-3
S5_DT_MAX = 1e-1

FFN_HIDDEN = -(-8 * D_MODEL // (3 * 256)) * 256

N_HGRN = (DEPTH + 2) // 3
N_MLSTM = (DEPTH + 1) // 3
N_S5 = DEPTH // 3

kernel_name = "hybrid_hgrn2_mlstm_s5_deepnorm"


def _layernorm(x, g, b):
    xf = x.astype(jnp.float32)
    mu = jnp.mean(xf, axis=-1, keepdims=True)
    xc = xf - mu
    var = jnp.mean(xc * xc, axis=-1, keepdims=True)
    return (xc * lax.rsqrt(var + LN_EPS) * g + b).astype(x.dtype)


def _head_rmsnorm(h, w):
    return h * lax.rsqrt(jnp.mean(h * h, axis=-1, keepdims=True) + HEAD_NORM_EPS) * w


def _to_chunks(t, n_heads, chunk):
    bsz, seq, width = t.shape
    return t.reshape(bsz, seq // chunk, chunk, n_heads, width // n_heads).transpose(1, 0, 3, 2, 4)


def _from_chunks(t):
    nc, bsz, h, c, d = t.shape
    return t.transpose(1, 0, 3, 2, 4).reshape(bsz, nc * c, h, d)


def _causal_conv(u, w):
    k_w = w.shape[0]
    seq = u.shape[1]
    up = jnp.pad(u, ((0, 0), (k_w - 1, 0), (0, 0)))
    out = up[:, 0:seq] * w[0]
    for j in range(1, k_w):
        out = out + up[:, j:j + seq] * w[j]
    return out


def _hgrn_lower_bounds(lb_logits):
    p = jax.nn.softmax(lb_logits.astype(jnp.float32), axis=0)
    c = jnp.cumsum(p, axis=0)
    return c - c[0:1]


def _hgrn2_mixer(x, w_in, norm_w, w_out, lb):
    bsz, seq, _ = x.shape
    q, f, v, g = jnp.split(x @ w_in, 4, axis=-1)
    f = f.astype(jnp.float32)
    log_f = jnp.logaddexp(jnp.log(lb), jnp.log1p(-lb) + jax.nn.log_sigmoid(f))
    k = (1.0 - lb) * jax.nn.sigmoid(-f)
    q = jax.nn.silu(q.astype(jnp.float32))
    qc = _to_chunks(q, HGRN_HEADS, HGRN_CHUNK)
    kc = _to_chunks(k, HGRN_HEADS, HGRN_CHUNK)
    vc = _to_chunks(v.astype(jnp.float32), HGRN_HEADS, HGRN_CHUNK)
    lfc = _to_chunks(log_f, HGRN_HEADS, HGRN_CHUNK)
    mask = jnp.tril(jnp.ones((HGRN_CHUNK, HGRN_CHUNK), dtype=bool))

    def step(s_prev, inp):
        qb, kb, vb, lfb = inp
        b = jnp.cumsum(lfb, axis=2)
        diff = b[:, :, :, None, :] - b[:, :, None, :, :]
        decay = jnp.exp(jnp.where(mask[:, :, None], diff, -jnp.inf))
        att = jnp.einsum('bhtd,bhsd,bhtsd->bhts', qb, kb, decay)
        o = jnp.einsum('bhts,bhse->bhte', att, vb) + jnp.einsum('bhtd,bhde->bhte', qb * jnp.exp(b), s_prev)
        b_last = b[:, :, -1]
        k_dec = kb * jnp.exp(b_last[:, :, None, :] - b)
        s_new = jnp.exp(b_last)[..., None] * s_prev + jnp.einsum('bhsd,bhse->bhde', k_dec, vb)
        return s_new, o

    s0 = jnp.zeros((bsz, HGRN_HEADS, HGRN_DK, HGRN_DV), jnp.float32)
    _, oc = lax.scan(step, s0, (qc, kc, vc, lfc))
    o = _head_rmsnorm(_from_chunks(oc), norm_w.astype(jnp.float32))
    o = o.reshape(bsz, seq, D_MODEL) * jax.nn.silu(g.astype(jnp.float32))
    return o.astype(x.dtype) @ w_out


def _mlstm_mixer(x, w_in, conv_w, gate_b, norm_w, w_out):
    bsz, seq, _ = x.shape
    proj = x @ w_in
    qk = proj[..., :2 * MLSTM_QK_W]
    v = proj[..., 2 * MLSTM_QK_W:2 * MLSTM_QK_W + D_MODEL]
    o_pre = proj[..., 2 * MLSTM_QK_W + D_MODEL:2 * MLSTM_QK_W + 2 * D_MODEL]
    gates = proj[..., 2 * MLSTM_QK_W + 2 * D_MODEL:].astype(jnp.float32) + gate_b.astype(jnp.float32)
    qk = jax.nn.silu(_causal_conv(qk, conv_w).astype(jnp.float32))
    q = qk[..., :MLSTM_QK_W]
    k = qk[..., MLSTM_QK_W:] * (MLSTM_DQK ** -0.5)
    log_i = gates[..., :MLSTM_HEADS]
    log_f = jax.nn.log_sigmoid(gates[..., MLSTM_HEADS:])
    qc = _to_chunks(q, MLSTM_HEADS, MLSTM_CHUNK)
    kc = _to_chunks(k, MLSTM_HEADS, MLSTM_CHUNK)
    vc = _to_chunks(v.astype(jnp.float32), MLSTM_HEADS, MLSTM_CHUNK)
    lic = _to_chunks(log_i, MLSTM_HEADS, MLSTM_CHUNK)[..., 0]
    lfc = _to_chunks(log_f, MLSTM_HEADS, MLSTM_CHUNK)[..., 0]
    mask = jnp.tril(jnp.ones((MLSTM_CHUNK, MLSTM_CHUNK), dtype=bool))

    def step(carry, inp):
        c_prev, n_prev, m_prev = carry
        qb, kb, vb, lib, lfb = inp
        b = jnp.cumsum(lfb, axis=-1)
        log_d = jnp.where(mask, b[..., :, None] - b[..., None, :] + lib[..., None, :], -jnp.inf)
        log_inter = b + m_prev[..., None]
        m_t = jnp.maximum(jnp.max(log_d, axis=-1), log_inter)
        d_mat = jnp.exp(log_d - m_t[..., None])
        w_inter = jnp.exp(log_inter - m_t)
        s = jnp.einsum('bhtd,bhsd->bhts', qb, kb) * d_mat
        num = jnp.einsum('bhts,bhse->bhte', s, vb) + w_inter[..., None] * jnp.einsum('bhtd,bhde->bhte', qb, c_prev)
        den = jnp.sum(s, axis=-1) + w_inter * jnp.einsum('bhtd,bhd->bht', qb, n_prev)
        h = num / jnp.maximum(jnp.abs(den), jnp.exp(-m_t))[..., None]
        b_last = b[..., -1]
        log_w = b_last[..., None] - b + lib
        m_new = jnp.maximum(b_last + m_prev, jnp.max(log_w, axis=-1))
        w_s = jnp.exp(log_w - m_new[..., None])
        w_prev = jnp.exp(b_last + m_prev - m_new)
        c_new = w_prev[..., None, None] * c_prev + jnp.einsum('bhs,bhsd,bhse->bhde', w_s, kb, vb)
        n_new = w_prev[..., None] * n_prev + jnp.einsum('bhs,bhsd->bhd', w_s, kb)
        return (c_new, n_new, m_new), h

    init = (jnp.zeros((bsz, MLSTM_HEADS, MLSTM_DQK, MLSTM_DV), jnp.float32),
            jnp.zeros((bsz, MLSTM_HEADS, MLSTM_DQK), jnp.float32),
            jnp.zeros((bsz, MLSTM_HEADS), jnp.float32))
    _, hc = lax.scan(step, init, (qc, kc, vc, lic, lfc))
    h = _head_rmsnorm(_from_chunks(hc), norm_w.astype(jnp.float32).reshape(MLSTM_HEADS, MLSTM_DV))
    h = h.reshape(bsz, seq, D_MODEL) * jax.nn.sigmoid(o_pre.astype(jnp.float32))
    return h.astype(x.dtype) @ w_out


def _cmul(ar, ai, br, bi):
    return ar * br - ai * bi, ar * bi + ai * br


def _s5_combine(e1, e2):
    a1r, a1i, b1r, b1i = e1
    a2r, a2i, b2r, b2i = e2
    ar, ai = _cmul(a2r, a2i, a1r, a1i)
    tr, ti = _cmul(a2r, a2i, b1r, b1i)
    return ar, ai, tr + b2r, ti + b2i


def _s5_mixer(x, w_in, a_re, a_im, log_dt, b_re, b_im, c_re, c_im, d_skip, w_out):
    bsz, seq, _ = x.shape
    f32 = jnp.float32
    u = (x @ w_in).astype(f32)
    a_re = a_re.astype(f32)
    a_im = a_im.astype(f32)
    b_re = b_re.astype(f32)
    b_im = b_im.astype(f32)
    c_re = c_re.astype(f32)
    c_im = c_im.astype(f32)
    dt = jnp.exp(log_dt.astype(f32))[:, None]
    mag = jnp.exp(a_re * dt)
    abar_re = mag * jnp.cos(a_im * dt)
    abar_im = mag * jnp.sin(a_im * dt)
    nr = abar_re - 1.0
    ni = abar_im
    den = a_re * a_re + a_im * a_im
    coef_re = (nr * a_re + ni * a_im) / den
    coef_im = (ni * a_re - nr * a_im) / den
    bbar_re = coef_re[..., None] * b_re - coef_im[..., None] * b_im
    bbar_im = coef_re[..., None] * b_im + coef_im[..., None] * b_re
    chunk = math.gcd(seq, S5_CHUNK)
    nc = seq // chunk
    ug = u.reshape(bsz, nc, chunk, S5_GROUPS, S5_GROUP_CH).transpose(1, 0, 2, 3, 4)
    a_r = jnp.broadcast_to(abar_re, (bsz, chunk, S5_GROUPS, S5_STATE))
    a_i = jnp.broadcast_to(abar_im, (bsz, chunk, S5_GROUPS, S5_STATE))

    def step(carry, ub):
        hr, hi = carry
        bu_re = jnp.einsum('blgc,gpc->blgp', ub, bbar_re)
        bu_im = jnp.einsum('blgc,gpc->blgp', ub, bbar_im)
        pr, pi, sr, si = lax.associative_scan(_s5_combine, (a_r, a_i, bu_re, bu_im), axis=1)
        cr, ci = _cmul(pr, pi, hr[:, None], hi[:, None])
        sr = sr + cr
        si = si + ci
        y = jnp.einsum('blgp,gcp->blgc', sr, c_re) - jnp.einsum('blgp,gcp->blgc', si, c_im)
        return (sr[:, -1], si[:, -1]), y

    h0 = (jnp.zeros((bsz, S5_GROUPS, S5_STATE), f32), jnp.zeros((bsz, S5_GROUPS, S5_STATE), f32))
    _, yc = lax.scan(step, h0, ug)
    y = yc.transpose(1, 0, 2, 3, 4).reshape(bsz, seq, D_MODEL)
    y = y + d_skip.astype(f32) * u
    y = jax.nn.gelu(y).astype(x.dtype)
    za, zb = jnp.split(y @ w_out, 2, axis=-1)
    return za * jax.nn.sigmoid(zb)


def _swiglu(x, w_gate_up, w_down):
    gate, up = jnp.split(x @ w_gate_up, 2, axis=-1)
    return (jax.nn.silu(gate) * up) @ w_down


def setup_inputs(seed: int = 0) -> dict:
    key = jax.random.key(seed)
    ks = jax.random.split(key, 24)
    D = D_MODEL
    nrm = jax.random.normal
    x = nrm(ks[0], (BATCH, SEQ, D), jnp.float32)
    hgrn_w_in = nrm(ks[1], (N_HGRN, D, 4 * D), jnp.float32) * D ** -0.5
    hgrn_norm_w = 1.0 + 0.02 * nrm(ks[2], (N_HGRN, HGRN_DV), jnp.float32)
    hgrn_w_out = nrm(ks[3], (N_HGRN, D, D), jnp.float32) * (D ** -0.5 * BETA)
    hgrn_lb_logits = 0.5 * nrm(ks[4], (DEPTH, HGRN_HEADS * HGRN_DK), jnp.float32)
    mlstm_w_in = nrm(ks[5], (N_MLSTM, D, MLSTM_IN_W), jnp.float32) * D ** -0.5
    mlstm_conv_w = nrm(ks[6], (N_MLSTM, MLSTM_CONV, 2 * MLSTM_QK_W), jnp.float32) * MLSTM_CONV ** -0.5
    i_bias = 0.1 * nrm(ks[7], (N_MLSTM, MLSTM_HEADS), jnp.float32)
    f_bias = jnp.linspace(3.0, 6.0, MLSTM_HEADS, dtype=jnp.float32)[None] + 0.1 * nrm(ks[8], (N_MLSTM, MLSTM_HEADS), jnp.float32)
    mlstm_gate_b = jnp.concatenate([i_bias, f_bias], axis=-1)
    mlstm_norm_w = 1.0 + 0.02 * nrm(ks[9], (N_MLSTM, D), jnp.float32)
    mlstm_w_out = nrm(ks[10], (N_MLSTM, D, D), jnp.float32) * (D ** -0.5 * BETA)
    s5_w_in = nrm(ks[11], (N_S5, D, D), jnp.float32) * D ** -0.5
    s5_a_re = -0.5 + 0.01 * nrm(ks[12], (N_S5, S5_GROUPS, S5_STATE), jnp.float32)
    s5_a_im = (jnp.pi * jnp.arange(S5_STATE, dtype=jnp.float32))[None, None] + 0.01 * nrm(ks[13], (N_S5, S5_GROUPS, S5_STATE), jnp.float32)
    s5_log_dt = jax.random.uniform(ks[14], (N_S5, S5_GROUPS), jnp.float32, math.log(S5_DT_MIN), math.log(S5_DT_MAX))
    s5_b_re = nrm(ks[15], (N_S5, S5_GROUPS, S5_STATE, S5_GROUP_CH), jnp.float32) * (2 * S5_GROUP_CH) ** -0.5
    s5_b_im = nrm(ks[16], (N_S5, S5_GROUPS, S5_STATE, S5_GROUP_CH), jnp.float32) * (2 * S5_GROUP_CH) ** -0.5
    s5_c_re = nrm(ks[17], (N_S5, S5_GROUPS, S5_GROUP_CH, S5_STATE), jnp.float32) * S5_STATE ** -0.5
    s5_c_im = nrm(ks[18], (N_S5, S5_GROUPS, S5_GROUP_CH, S5_STATE), jnp.float32) * S5_STATE ** -0.5
    s5_d = nrm(ks[19], (N_S5, D), jnp.float32)
    s5_w_out = nrm(ks[20], (N_S5, D, 2 * D), jnp.float32) * (D ** -0.5 * BETA)
    ffn_w_gate_up = nrm(ks[21], (DEPTH, D, 2 * FFN_HIDDEN), jnp.float32) * D ** -0.5
    ffn_w_down = nrm(ks[22], (DEPTH, FFN_HIDDEN, D), jnp.float32) * (FFN_HIDDEN ** -0.5 * BETA)
    kg, kb = jax.random.split(ks[23])
    ln_g = 1.0 + 0.02 * nrm(kg, (DEPTH, 2, D), jnp.float32)
    ln_b = 0.02 * nrm(kb, (DEPTH, 2, D), jnp.float32)
    return {"x": x, "hgrn_w_in": hgrn_w_in, "hgrn_norm_w": hgrn_norm_w, "hgrn_w_out": hgrn_w_out,
            "hgrn_lb_logits": hgrn_lb_logits, "mlstm_w_in": mlstm_w_in, "mlstm_conv_w": mlstm_conv_w,
            "mlstm_gate_b": mlstm_gate_b, "mlstm_norm_w": mlstm_norm_w, "mlstm_w_out": mlstm_w_out,
            "s5_w_in": s5_w_in, "s5_a_re": s5_a_re, "s5_a_im": s5_a_im, "s5_log_dt": s5_log_dt,
            "s5_b_re": s5_b_re, "s5_b_im": s5_b_im, "s5_c_re": s5_c_re, "s5_c_im": s5_c_im,
            "s5_d": s5_d, "s5_w_out": s5_w_out, "ffn_w_gate_up": ffn_w_gate_up, "ffn_w_down": ffn_w_down,
            "ln_g": ln_g, "ln_b": ln_b}


def reference(x, hgrn_w_in, hgrn_norm_w, hgrn_w_out, hgrn_lb_logits, mlstm_w_in, mlstm_conv_w,
              mlstm_gate_b, mlstm_norm_w, mlstm_w_out, s5_w_in, s5_a_re, s5_a_im, s5_log_dt,
              s5_b_re, s5_b_im, s5_c_re, s5_c_im, s5_d, s5_w_out, ffn_w_gate_up, ffn_w_down,
              ln_g, ln_b):
    lb_all = _hgrn_lower_bounds(hgrn_lb_logits)
    for i in range(DEPTH):
        kind = i % N_MIXERS
        j = i // N_MIXERS
        if kind == 0:
            h = _hgrn2_mixer(x, hgrn_w_in[j], hgrn_norm_w[j], hgrn_w_out[j], lb_all[i])
        elif kind == 1:
            h = _mlstm_mixer(x, mlstm_w_in[j], mlstm_conv_w[j], mlstm_gate_b[j], mlstm_norm_w[j], mlstm_w_out[j])
        else:
            h = _s5_mixer(x, s5_w_in[j], s5_a_re[j], s5_a_im[j], s5_log_dt[j], s5_b_re[j], s5_b_im[j],
                          s5_c_re[j], s5_c_im[j], s5_d[j], s5_w_out[j])
        x = _layernorm(ALPHA * x + h, ln_g[i, 0], ln_b[i, 0])
        x = _layernorm(ALPHA * x + _swiglu(x, ffn_w_gate_up[i], ffn_w_down[i]), ln_g[i, 1], ln_b[i, 1])
    return x
```

```python
import contextlib
import math
import numpy as np
import concourse.bass as bass
import concourse.mybir as mybir
from concourse.bass_utils import run_bass_kernel_spmd

F32 = mybir.dt.float32
BF16 = mybir.dt.bfloat16
AF = mybir.ActivationFunctionType
ALU = mybir.AluOpType
AX = mybir.AxisListType

NCORES = 8
SEQ = 16384
TOK = SEQ // NCORES
D = 1024
KT = 8
NTILE = TOK // 128
DEPTH = 4
ALPHA = (2.0 * DEPTH) ** 0.25
LN_EPS = 1e-5
HN_EPS = 1e-6
FFN_H = 2816
NFT = FFN_H // 128

ENGS = ("pe", "act", "dve", "pool", "sp")


class Buf:
    __slots__ = ("name", "last_w", "readers", "grp")

    def __init__(self, name, grp=None):
        self.name = name
        self.last_w = None
        self.readers = []
        self.grp = grp


class DmaGroup:
    __slots__ = ("sem", "count", "final", "last", "unit", "frozen", "kind", "base")

    def __init__(self, sem, final=False, start=0, unit=16):
        self.sem = sem
        self.count = start
        self.final = final
        self.last = None
        self.unit = unit
        self.frozen = None
        self.kind = None
        self.base = 0


class Op:
    __slots__ = ("eng", "fn", "deps", "is_dma", "grp", "grp_count", "sig", "has_dep", "epoch")

    def __init__(self, eng, fn, is_dma=False):
        self.epoch = 0
        self.eng = eng
        self.fn = fn
        self.deps = []
        self.is_dma = is_dma
        self.grp = None
        self.grp_count = 0
        self.sig = None
        self.has_dep = False


class Prog:
    def __init__(self, nc):
        self.nc = nc
        self.ops = {e: [] for e in ENGS}
        self.stack = contextlib.ExitStack()
        self.nsem = 0
        self.eng_sem = {}
        self.groups = []
        self._uid = 0
        self.free_sems = {}
        self.live_groups = []
        self.epoch = 0
        self.pool_t = None
        self.pool_off = 0
        self.pool_mark = 0
        self.pool_size = 0

    def sem(self, name=None):
        self.nsem += 1
        return self.stack.enter_context(self.nc.semaphore(name or f"s{self.nsem}"))

    def uid(self, pfx):
        self._uid += 1
        return f"{pfx}{self._uid}"

    def sbuf(self, name, shape, dt):
        if self.pool_t is None:
            return self.stack.enter_context(self.nc.sbuf_tensor(name, list(shape), dt))
        n = 1
        for d_ in shape[1:]:
            n *= d_
        units = (n * mybir.dt.size(dt) + 1) // 2
        units = (units + 7) // 8 * 8
        assert self.pool_off + units <= self.pool_size, ("sbuf pool overflow", name, self.pool_off, units, self.pool_size)
        v = self.pool_t[0:shape[0], self.pool_off:self.pool_off + units]
        self.pool_off += units
        if dt != BF16:
            v = v.bitcast(dt)
        v = v[:, 0:n]
        if len(shape) > 2:
            names = [f"d{i}" for i in range(len(shape) - 1)]
            v = v.rearrange("p (" + " ".join(names) + ") -> p " + " ".join(names), **{nm: shape[i + 1] for i, nm in enumerate(names[:-1])})
        return v

    def psum(self, name, shape, dt):
        return self.stack.enter_context(self.nc.psum_tensor(name, list(shape), dt))

    def buf(self, name="b"):
        return Buf(name)

    def group(self, final=False, unit=16):
        g = DmaGroup(None, final, 0, unit)
        self.groups.append(g)
        self.live_groups.append(g)
        return g

    def _bind(self, g, kind):
        if g.sem is not None:
            assert g.kind == kind, ("DMA group mixes queues", g.kind, kind)
            return
        g.kind = kind
        fl = self.free_sems.setdefault(kind, [])
        if fl:
            g.sem, g.base = fl.pop()
        else:
            g.sem, g.base = self.sem(), 0

    def new_stage(self):
        self.barrier()
        for g in self.live_groups:
            if g.unit == 1:
                continue
            g.frozen = g.count
            if g.sem is not None:
                self.free_sems.setdefault(g.kind, []).append((g.sem, g.base + g.count))
        self.live_groups = []
        self.epoch += 1
        self.pool_off = self.pool_mark

    def coll(self, fn, reads, writes):
        op = Op("pool", fn, is_dma=True)
        g = None
        for b in writes:
            if b.grp is not None:
                g = b.grp
                break
        assert g is not None and g.unit == 1
        self._bind(g, "cc")
        g.count += 1
        g.last = op
        op.grp = g
        op.grp_count = g.count
        op.epoch = self.epoch
        self._track(op, reads, writes)
        self.ops["pool"].append(op)
        return op

    def dbuf(self, name="d", final=False, grp=None, unit=16):
        if grp is None:
            grp = self.group(final, unit)
        return Buf(name, grp)

    def _track(self, op, reads, writes):
        for b in reads:
            if b.last_w is not None:
                op.deps.append((b.last_w, True))
        for b in writes:
            if b.last_w is not None:
                op.deps.append((b.last_w, False))
            for r in b.readers:
                if r is not op:
                    op.deps.append((r, False))
        for b in reads:
            b.readers.append(op)
        for b in writes:
            b.last_w = op
            b.readers = []

    def add(self, eng, fn, reads=(), writes=()):
        op = Op(eng, fn)
        op.epoch = self.epoch
        self._track(op, reads, writes)
        self.ops[eng].append(op)
        return op

    def dma(self, eng, out, in_, reads=(), writes=(), grp=None, **kw):
        op = Op(eng, lambda e: e.dma_start(out=out, in_=in_, **kw), is_dma=True)
        g = grp
        if g is None:
            for b in writes:
                if b.grp is not None:
                    g = b.grp
                    break
        assert g is not None, "dma needs a group"
        self._bind(g, "sw" if eng == "pool" else "hw")
        g.count += 1
        g.last = op
        op.grp = g
        op.grp_count = g.count
        op.epoch = self.epoch
        self._track(op, reads, writes)
        self.ops[eng].append(op)
        return op

    def barrier(self):
        lasts = []
        for e in ENGS:
            for op in reversed(self.ops[e]):
                if not op.is_dma:
                    lasts.append(op)
                    break
        for g in self.live_groups:
            if g.last is not None and g.unit != 1:
                lasts.append(g.last)
        for e in ENGS:
            op = Op(e, lambda eng: eng.nop())
            op.epoch = self.epoch
            op.deps = [(o, True) for o in lasts]
            self.ops[e].append(op)

    def mm(self, out, lhsT, rhs, start, stop, reads, writes, tp=None):
        if tp is not None:
            return self.add("pe", lambda e: e.matmul(out, lhsT=lhsT, rhs=rhs, start=start, stop=stop, tile_position=tp), reads, writes)
        return self.add("pe", lambda e: e.matmul(out, lhsT=lhsT, rhs=rhs, start=start, stop=stop), reads, writes)

    def tr(self, out, in_, ident, reads, writes):
        return self.add("pe", lambda e: e.transpose(out, in_, ident), reads, writes)

    def act(self, out, in_, func, reads, writes, scale=1.0, bias=0.0, accum=None):
        if accum is None:
            return self.add("act", lambda e: e.activation(out=out, in_=in_, func=func, bias=bias, scale=scale), reads, writes)
        return self.add("act", lambda e: e.activation(out=out, in_=in_, func=func, bias=bias, scale=scale, accum_out=accum), reads, writes)

    def tt(self, eng, out, in0, in1, op, reads, writes):
        return self.add(eng, lambda e: e.tensor_tensor(out=out, in0=in0, in1=in1, op=op), reads, writes)

    def ts(self, eng, out, in0, s1, s2, op0, op1, reads, writes):
        if s2 is None:
            return self.add(eng, lambda e: e.tensor_scalar(out=out, in0=in0, scalar1=s1, scalar2=None, op0=op0), reads, writes)
        return self.add(eng, lambda e: e.tensor_scalar(out=out, in0=in0, scalar1=s1, scalar2=s2, op0=op0, op1=op1), reads, writes)

    def stt(self, out, in0, scalar, in1, op0, op1, reads, writes):
        return self.add("dve", lambda e: e.scalar_tensor_tensor(out=out, in0=in0, scalar=scalar, in1=in1, op0=op0, op1=op1), reads, writes)

    def copy(self, eng, out, in_, reads, writes):
        if eng == "act":
            return self.add("act", lambda e: e.copy(out, in_), reads, writes)
        return self.add(eng, lambda e: e.tensor_copy(out, in_), reads, writes)

    def memset(self, eng, ap, val, writes):
        return self.add(eng, lambda e: e.memset(ap, val), (), writes)

    def emit(self):
        nc = self.nc
        for e in ENGS:
            for op in self.ops[e]:
                for d, raw in op.deps:
                    if d.eng == e and not d.is_dma and not raw and e != "pool":
                        continue
                    d.has_dep = True
        for e in ENGS:
            cnts = {}
            for op in self.ops[e]:
                if op.is_dma:
                    continue
                if op.has_dep:
                    cnts[op.epoch] = cnts.get(op.epoch, 0) + 1
                    op.sig = cnts[op.epoch]
            for ep in cnts:
                self.eng_sem[(e, ep)] = self.sem(f"eng_{e}_{ep}")
        block = self.stack.enter_context(nc.Block())

        def emit_engine(e, eng_obj):
            waited = {}
            for op in self.ops[e]:
                need = {}
                for d, raw in op.deps:
                    if d.is_dma:
                        s = d.grp.sem
                        fin = d.grp.frozen if d.grp.frozen is not None else d.grp.count
                        v = d.grp.unit * (d.grp.base + (fin if d.grp.final else d.grp_count))
                    else:
                        if d.eng == e and (e == "pe" or (not raw and e != "pool")):
                            continue
                        s = self.eng_sem[(d.eng, d.epoch)]
                        v = d.sig
                    k = id(s)
                    if k not in need or need[k][1] < v:
                        need[k] = (s, v)
                for k, (s, v) in need.items():
                    if waited.get(k, 0) >= v:
                        continue
                    waited[k] = v
                    eng_obj.wait_ge(s, v)
                ins = op.fn(eng_obj)
                if op.is_dma:
                    ins.then_inc(op.grp.sem, op.grp.unit)
                elif op.has_dep:
                    ins.then_inc(self.eng_sem[(e, op.epoch)], 1)

        @block.sync
        def _(eng):
            emit_engine("sp", eng)

        @block.scalar
        def _(eng):
            emit_engine("act", eng)

        @block.vector
        def _(eng):
            emit_engine("dve", eng)

        @block.gpsimd
        def _(eng):
            emit_engine("pool", eng)

        @block.tensor
        def _(eng):
            emit_engine("pe", eng)

        self.stack.close()


class Rot:
    def __init__(self, slots):
        self.slots = slots
        self.i = 0

    def next(self):
        s = self.slots[self.i % len(self.slots)]
        self.i += 1
        return s


class Ctx:
    def __init__(self, nc, pool_units=None):
        self.nc = nc
        self.p = Prog(nc)
        p = self.p
        self.fused = pool_units is not None
        if pool_units is not None:
            t = p.stack.enter_context(nc.sbuf_tensor("pool", [128, pool_units], BF16))
            p.pool_t = t[:]
            p.pool_size = pool_units
        self.pending_out = []
        self.ln_ctr = 0
        self.banks = [p.psum(f"bank{i}", [128, 512], F32) for i in range(8)]
        self.bank_bufs = [p.buf(f"bankbuf{i}") for i in range(8)]
        self.cgrp = p.group(final=True)
        self.eps_ln = p.sbuf("eps_ln", [128, 1], F32)
        self.eps_hn = p.sbuf("eps_hn", [128, 1], F32)
        self.b_eps = p.buf("eps")
        p.memset("pool", self.eps_ln[:], LN_EPS, [self.b_eps])
        p.memset("pool", self.eps_hn[:], 128.0 * HN_EPS, [self.b_eps])
        self.ident_bf = p.sbuf("ident_bf", [128, 128], BF16)
        self.ident_f = p.sbuf("ident_f", [128, 128], F32)
        self.b_ident = p.buf("ident")
        for t in (self.ident_bf, self.ident_f):
            p.memset("pool", t[:], 0.0, [self.b_ident])
            p.add("pool", lambda e, t=t: e.affine_select(out=t[:], in_=t[:], pattern=[[-1, 128]], compare_op=ALU.not_equal,
                                                         fill=1.0, base=0, channel_multiplier=1), [self.b_ident], [self.b_ident])

    def mark_persistent(self):
        self.p.pool_mark = self.p.pool_off

    def new_stage(self):
        self.p.new_stage()
        self.cgrp = self.p.group(final=True)

    def done(self, own, out_bufs):
        if own:
            self.finish(out_bufs)
            return self.nc
        self.pending_out += list(out_bufs)
        return None

    def const_load(self, name, shape, dt, src, eng="sp"):
        p = self.p
        t = p.sbuf(name, shape, dt)
        b = Buf(name, self.cgrp)
        p.dma(eng, t[:], src, writes=[b])
        return t, b

    def finish(self, out_bufs):
        self.p.add("sp", lambda e: e.nop(), reads=out_bufs)
        self.p.emit()


def load_xT(cx, x_dram, xT, b_xT, xb, ntile=NTILE, banks=(6, 7)):
    p = cx.p
    for t in range(ntile):
        bank = banks[t % len(banks)]
        ps = cx.banks[bank][:].bitcast(BF16)
        b_ps = cx.bank_bufs[bank]
        xt, bx = xb[t % 2]
        p.dma("pool", xt, x_dram[t * 128:(t + 1) * 128, :], writes=[bx])
        for kt in range(KT):
            p.tr(ps[:, kt * 128:(kt + 1) * 128], xt[:, kt * 128:(kt + 1) * 128], cx.ident_bf[:], [bx, cx.b_ident], [b_ps])
        eng = "act" if t % 2 == 0 else "dve"
        p.copy(eng, xT[:, :, t * 128:(t + 1) * 128], ps.rearrange("p (k t) -> p k t", k=KT), [], [b_ps, b_xT[t]])


def layer_norm_tile(cx, y, b_y, gam, bet, b_gb, out_f, b_out, tmp, b_tmp, st, b_st):
    p = cx.p
    if isinstance(tmp, list):
        i = cx.ln_ctr % len(tmp)
        cx.ln_ctr += 1
        tmp, b_tmp, st, b_st = tmp[i], b_tmp[i], st[i], b_st[i]
    p.add("dve", lambda e: e.bn_stats(st[:, 0:6], y[:, 0:512]), [b_y], [b_st])
    p.add("dve", lambda e: e.bn_stats(st[:, 6:12], y[:, 512:1024]), [b_y], [b_st])
    p.add("dve", lambda e: e.bn_aggr(st[:, 12:14], st[:, 0:12].rearrange("p (a b) -> p a b", a=2)), [b_st], [b_st])
    p.act(st[:, 15:16], st[:, 13:14], AF.Sqrt, [b_st, cx.b_eps], [b_st], bias=cx.eps_ln[:, 0:1])
    p.add("dve", lambda e: e.reciprocal(st[:, 14:15], st[:, 15:16]), [b_st], [b_st])
    p.stt(st[:, 15:16], st[:, 12:13], -1.0, st[:, 14:15], ALU.mult, ALU.mult, [b_st], [b_st])
    p.act(tmp[:], y[:], AF.Identity, [b_y, b_st], [b_tmp], scale=st[:, 14:15], bias=st[:, 15:16])
    p.tt("dve", tmp[:], tmp[:], gam, ALU.mult, [b_tmp, b_gb], [b_tmp])
    p.tt("dve", out_f, tmp[:], bet, ALU.add, [b_tmp, b_gb], [b_out])


def transpose_tile_to_xT(cx, src_f, b_src, xT, b_xT_t, t, xbf, b_xbf, bank=7, b_ps=None):
    p = cx.p
    p.copy("act", xbf[:], src_f, [b_src], [b_xbf])
    ps = cx.banks[bank][:].bitcast(BF16)
    b_ps = cx.bank_bufs[bank]
    for kt in range(KT):
        p.tr(ps[:, kt * 128:(kt + 1) * 128], xbf[:, kt * 128:(kt + 1) * 128], cx.ident_bf[:], [b_xbf, cx.b_ident], [b_ps])
    p.copy("dve", xT[:, :, t * 128:(t + 1) * 128], ps.rearrange("p (k t) -> p k t", k=KT), [], [b_ps, b_xT_t])


def ffn_phase(cx, x1T, b_x1T, x1s_dram, b_x1s, wgu, wd, g2, be2, b_gb, out_dram, b_outd, arena_bf, arena_off, row_fn=None):
    p = cx.p
    G = 1024
    off = arena_off

    def carve(n_bf16):
        nonlocal off
        v = arena_bf[:, off:off + n_bf16]
        off += n_bf16
        return v

    hT = carve(NFT * G).rearrange("p (f t) -> p f t", f=NFT)
    b_hT = [p.buf(f"hT{f}") for f in range(NFT)]
    wg_bufs = [(carve(KT * 512).rearrange("p (k c) -> p k c", k=KT), p.dbuf("wg")) for _ in range(2)]
    wu_bufs = [(carve(KT * 512).rearrange("p (k c) -> p k c", k=KT), p.dbuf("wu")) for _ in range(2)]
    wd_bufs = [(carve(1024), p.dbuf("wd")) for _ in range(3)]
    sg_bufs = [(carve(1024).bitcast(F32), p.buf("sgt")) for _ in range(2)]
    xr_bufs = [(carve(2048).bitcast(F32), p.dbuf("xr")) for _ in range(4)]
    y_bufs = [(carve(2048).bitcast(F32), p.buf("y")) for _ in range(4)]
    tmp = [carve(2048).bitcast(F32) for _ in range(2)]
    b_tmp = [p.buf("lntmp") for _ in range(2)]
    o_bufs = [(carve(2048).bitcast(F32), p.buf("o")) for _ in range(2)]
    st = [p.sbuf(p.uid("stf"), [128, 16], F32) for _ in range(2)]
    b_st = [p.buf("st") for _ in range(2)]
    wgu_v = wgu.rearrange("(kt p) c -> p kt c", p=128)
    bank_bufs = cx.bank_bufs
    nquad = (NFT + 3) // 4
    for g in range(TOK // G):
        ib = 0
        for fq in range(nquad):
            nf = min(4, NFT - fq * 4)
            wgt, b_wg = wg_bufs[fq % 2]
            wut, b_wu = wu_bufs[fq % 2]
            c0 = fq * 512
            p.dma("pool", wgt[:, :, 0:nf * 128], wgu_v[:, :, c0:c0 + nf * 128], writes=[b_wg])
            p.dma("pool", wut[:, :, 0:nf * 128], wgu_v[:, :, FFN_H + c0:FFN_H + c0 + nf * 128], writes=[b_wu])
            for fi in range(nf):
                f = fq * 4 + fi
                for tb in range(G // 512):
                    tok0 = g * G + tb * 512
                    rd = [b_x1T[(tok0 // 128) + j] for j in range(4)]
                    bg = ib % 4
                    ib += 1
                    gps, b_gps = cx.banks[2 * bg], bank_bufs[2 * bg]
                    ups, b_ups = cx.banks[2 * bg + 1], bank_bufs[2 * bg + 1]
                    for kt in range(KT):
                        p.mm(gps[:], wgt[:, kt, fi * 128:(fi + 1) * 128], x1T[:, kt, tok0:tok0 + 512], kt == 0, kt == KT - 1, [b_wg] + rd, [b_gps])
                    for kt in range(KT):
                        p.mm(ups[:], wut[:, kt, fi * 128:(fi + 1) * 128], x1T[:, kt, tok0:tok0 + 512], kt == 0, kt == KT - 1, [b_wu] + rd, [b_ups])
                    sgt, b_sgt = sg_bufs[ib % 2]
                    p.act(sgt, gps[:], AF.Silu, [], [b_gps, b_sgt])
                    p.tt("dve", hT[:, f, tb * 512:(tb + 1) * 512], sgt, ups[:], ALU.mult, [b_sgt], [b_ups, b_hT[f]])
        for tq in range(G // 512):
            for f in range(NFT):
                wdt, b_wd = wd_bufs[f % 3]
                p.dma("pool", wdt, wd[f * 128:(f + 1) * 128, :], writes=[b_wd])
                for ti in range(4):
                    tl = tq * 4 + ti
                    for half in range(2):
                        bk = ti * 2 + half
                        p.mm(cx.banks[bk][:], hT[:, f, tl * 128:(tl + 1) * 128], wdt[:, half * 512:(half + 1) * 512], f == 0, f == NFT - 1,
                             [b_hT[f], b_wd], [bank_bufs[bk]])
            for ti in range(4):
                t = g * (G // 128) + tq * 4 + ti
                xr, b_xr = xr_bufs[t % 4]
                p.dma("sp", xr, x1s_dram[t * 128:(t + 1) * 128, :], reads=[b_x1s], writes=[b_xr])
                y, b_y = y_bufs[t % 4]
                for half in range(2):
                    bk = ti * 2 + half
                    p.stt(y[:, half * 512:(half + 1) * 512], xr[:, half * 512:(half + 1) * 512], ALPHA, cx.banks[bk][:], ALU.mult, ALU.add,
                          [b_xr], [bank_bufs[bk], b_y])
            for ti in range(4):
                t = g * (G // 128) + tq * 4 + ti
                y, b_y = y_bufs[t % 4]
                o, b_o = o_bufs[t % 2]
                layer_norm_tile(cx, y, b_y, g2, be2, b_gb, o, b_o, tmp, b_tmp, st, b_st)
                p.dma("sp", (out_dram[t * 128:(t + 1) * 128, :] if row_fn is None else row_fn(out_dram, t)), o, reads=[b_o], writes=[b_outd])
    return off


def residual_ln1_tile(cx, t, h_aps, h_bufs, x_dram, xr_bufs, y_bufs, o_bufs, tmp, b_tmp, st, b_st, g1, be1, b_gb,
                      x1s_dram, b_x1s, x1T, b_x1T, xbf, b_xbf, b_pstr, row_fn=None):
    p = cx.p
    xr, b_xr = xr_bufs[t % 2]
    p.dma("sp", xr, (x_dram[t * 128:(t + 1) * 128, :] if row_fn is None else row_fn(x_dram, t)), writes=[b_xr])
    y, b_y = y_bufs[t % 2]
    for half in range(2):
        p.stt(y[:, half * 512:(half + 1) * 512], xr[:, half * 512:(half + 1) * 512], ALPHA, h_aps[half], ALU.mult, ALU.add,
              [b_xr], [h_bufs[half], b_y])
    o, b_o = o_bufs[t % 2]
    layer_norm_tile(cx, y, b_y, g1, be1, b_gb, o, b_o, tmp, b_tmp, st, b_st)
    p.dma("sp", x1s_dram[t * 128:(t + 1) * 128, :], o, reads=[b_o], writes=[b_x1s])
    transpose_tile_to_xT(cx, o, b_o, x1T, b_x1T[t], t, xbf, b_xbf, bank=7, b_ps=b_pstr)


def load_ln_consts(cx, ln_g, ln_b):
    g, bg = cx.const_load("ln_g_bc", [128, 2, D], F32, ln_g.partition_broadcast(128))
    b, bb = cx.const_load("ln_b_bc", [128, 2, D], F32, ln_b.partition_broadcast(128))
    return g, b, bg


def emit_summary(cx, io, pieces, osum, b_outd):
    p = cx.p
    if io is None or "bin" not in io:
        for src, bufs, col0, w in pieces:
            p.dma("sp", osum[:, col0:col0 + w], src, reads=list(bufs), writes=[b_outd])
        return
    oh, b_oh = io["ohot_sb"]
    bin_ = io["bin"]
    b_bin = p.dbuf("bin")
    for src, bufs, col0, w in pieces:
        tmps = [(p.sbuf(p.uid("pub"), [128, w], F32), p.buf("pub")) for _ in range(2)]
        for r in range(NCORES):
            t, b_t = tmps[r % 2]
            p.ts("pool" if (r % 2 and w > 64) else "dve", t[:], src, oh[:, r:r + 1], None, ALU.mult, None, list(bufs) + [b_oh], [b_t])
            p.dma("sp", bin_[r, :, col0:col0 + w], t[:], reads=[b_t], writes=[b_bin])
    in2d, out2d = io["bin2d"], io["bout2d"]
    p.coll(lambda e: e.collective_compute("AllReduce", ALU.add, replica_groups=[list(range(NCORES))], ins=[in2d.opt()], outs=[out2d.opt()]),
           [b_bin], [io["b_bout"]])


def load_ohot(cx, io, full):
    if not full and io is not None and "ohot" in io:
        io["ohot_sb"] = cx.const_load("ohot_sb", [128, NCORES], F32, io["ohot"])

def hgrn_lb_consts(cx, lb_logits, layer):
    p = cx.p
    lg, b_lg = cx.const_load("lb_lg", [32, 128], F32, lb_logits.rearrange("l (h d) -> (l h) d", d=128))
    ps = cx.banks[7]
    b_ps = cx.bank_bufs[7]
    p.tr(ps[:, 0:32], lg[:], cx.ident_f[0:32, 0:32], [b_lg, cx.b_ident], [b_ps])
    lt = p.sbuf("lb_lt", [128, 4, 8], F32)
    b_lt = p.buf("lb_lt")
    p.copy("dve", lt[:], ps[:, 0:32].rearrange("p (l h) -> p l h", l=4), [], [b_ps, b_lt])
    mx = p.sbuf("lb_mx", [128, 8], F32)
    p.tt("dve", mx[:], lt[:, 0, :], lt[:, 1, :], ALU.max, [b_lt], [b_lt])
    p.tt("dve", mx[:], mx[:], lt[:, 2, :], ALU.max, [b_lt], [b_lt])
    p.tt("dve", mx[:], mx[:], lt[:, 3, :], ALU.max, [b_lt], [b_lt])
    ex = p.sbuf("lb_ex", [128, 4, 8], F32)
    p.tt("dve", ex[:], lt[:], mx[:].unsqueeze(1).to_broadcast([128, 4, 8]), ALU.subtract, [b_lt], [b_lt])
    p.act(ex[:], ex[:], AF.Exp, [b_lt], [b_lt])
    sm = p.sbuf("lb_sm", [128, 8], F32)
    p.tt("dve", sm[:], ex[:, 0, :], ex[:, 1, :], ALU.add, [b_lt], [b_lt])
    p.tt("dve", sm[:], sm[:], ex[:, 2, :], ALU.add, [b_lt], [b_lt])
    p.tt("dve", sm[:], sm[:], ex[:, 3, :], ALU.add, [b_lt], [b_lt])
    num = p.sbuf("lb_num", [128, 8], F32)
    p.memset("dve", num[:], 0.0, [b_lt])
    for j in range(1, layer + 1):
        p.tt("dve", num[:], num[:], ex[:, j, :], ALU.add, [b_lt], [b_lt])
    lbc = p.sbuf("lbc", [128, 3, 8], F32)
    b_lbc = p.buf("lbc")
    p.add("dve", lambda e: e.reciprocal(sm[:], sm[:]), [b_lt], [b_lt])
    p.tt("dve", lbc[:, 0, :], num[:], sm[:], ALU.mult, [b_lt], [b_lbc])
    p.ts("dve", lbc[:, 1, :], lbc[:, 0, :], -1.0, 1.0, ALU.mult, ALU.add, [b_lbc], [b_lbc])
    p.ts("dve", lbc[:, 2, :], lbc[:, 0, :], -1.0, None, ALU.add, None, [b_lbc], [b_lbc])
    return lbc, b_lbc


def build_hgrn(layer, j, stage, cx=None, io=None):
    own = cx is None
    if own:
        cx = Ctx(bass.Bass("TRN2", target_bir_lowering=False))
    nc = cx.nc
    p = cx.p

    def dt_(name, shape, kind="Internal"):
        if io is not None and name in io:
            return io[name]
        return nc.dram_tensor(name, shape, F32, kind=kind).ap()
    full = stage == "B"
    x = dt_("x", [TOK, D], "ExternalInput")
    w_in = dt_("w_in", [D, 4 * D], "ExternalInput")
    lb_logits = dt_("lb_logits", [4, D], "ExternalInput")
    if full:
        norm_w = dt_("norm_w", [1, 128], "ExternalInput")
        w_out = dt_("w_out", [D, D], "ExternalInput")
        ln_g = dt_("ln_g", [2, D], "ExternalInput")
        ln_b = dt_("ln_b", [2, D], "ExternalInput")
        wgu = dt_("wgu", [D, 2 * FFN_H], "ExternalInput")
        wd = dt_("wd", [FFN_H, D], "ExternalInput")
        ssum = dt_("ssum", [NCORES, 128, 1032], "ExternalInput")
        cmask = dt_("cmask", [128, NCORES], "ExternalInput")
        y_out = dt_("y", [TOK, D], "ExternalOutput")
        x1s = dt_("x1s", [TOK, D])
    else:
        osum = dt_("osum", [128, 1032], "ExternalOutput") if (io is None or "bin" not in io) else None
    b_outd = p.dbuf("outd")

    load_ohot(cx, io, full)
    lbc, b_lbc = hgrn_lb_consts(cx, lb_logits, layer)
    cmk = p.sbuf("chunkmask", [128, 512], F32)
    b_cmk = p.buf("cmk")
    p.memset("pool", cmk[:], 1.0, [b_cmk])
    p.memset("pool", cmk[:, 0:512:64], 0.0, [b_cmk])
    if full:
        ones64 = p.sbuf("ones64", [64, 64], F32)
        maskT = p.sbuf("maskT", [64, 64], F32)
        b_mask = p.buf("mask")
        p.memset("pool", ones64[:], 1.0, [b_mask])
        p.add("pool", lambda e: e.affine_select(out=maskT[:], in_=ones64[:], pattern=[[1, 64]], compare_op=ALU.is_ge, fill=0.0,
                                                base=0, channel_multiplier=-1), [b_mask], [b_mask])
        nwb, b_nwb = cx.const_load("nw_bc", [64, 128], F32, norm_w.broadcast_to([64, 128]))
        nws = p.sbuf("nws", [64, 128], F32)
        b_nws = p.buf("nws")
        p.ts("dve", nws[:], nwb[:], math.sqrt(128.0), None, ALU.mult, None, [b_nwb], [b_nws])
        lng, lnb, b_gb = load_ln_consts(cx, ln_g, ln_b)
        cm, b_cm = cx.const_load("cmask_sb", [128, NCORES], F32, cmask)

    ARENA = 88 * 1024
    arena = p.sbuf("arena", [128, ARENA], BF16)
    off = 0

    def carve(n):
        nonlocal off
        v = arena[:, off:off + n]
        off += n
        assert off <= ARENA, off
        return v

    b_x1T = [p.buf(f"x1T{t}") for t in range(NTILE)]
    xT = carve(KT * TOK).rearrange("p (k t) -> p k t", k=KT)
    x1T = xT
    b_xT = [p.buf(f"xT{t}") for t in range(NTILE)]
    xb = [(carve(D), p.dbuf("xbf")) for _ in range(2)]
    load_xT(cx, x, xT, b_xT, xb)

    S = p.sbuf("S", [128, 8, 128], F32)
    b_S = [p.buf(f"S{h}") for h in range(8)]
    for h in range(8):
        p.memset("dve", S[:, h, :], 0.0, [b_S[h]])
    rdb = [io["b_bout"]] if (io is not None and "b_bout" in io) else []

    def combine():
        sBt = p.sbuf("sB_sb", [128, NCORES, 8], F32)
        b_sBt = p.dbuf("sBt")
        p.dma("sp", sBt[:], ssum[:, :, 1024:1032].rearrange("c d h -> d c h"), reads=rdb, writes=[b_sBt])
        ea = p.sbuf("ea", [128, NCORES, 8], F32)
        b_ea = p.buf("ea")
        p.act(ea[:], sBt[:], AF.Exp, [b_sBt], [b_ea])
        p.ts("dve", ea[:], ea[:], -1.0, None, ALU.add, None, [b_ea], [b_ea])
        p.tt("dve", ea[:], ea[:], cm[:].unsqueeze(2).to_broadcast([128, NCORES, 8]), ALU.mult, [b_ea, b_cm], [b_ea])
        p.ts("dve", ea[:], ea[:], 1.0, None, ALU.add, None, [b_ea], [b_ea])
        for c in range(NCORES):
            sc, b_scc = sc_bufs[c % 2]
            p.dma("sp", sc, ssum[c, :, 0:1024].rearrange("d (h e) -> d h e", h=8), reads=rdb, writes=[b_scc])
            p.ts("dve", sc, sc, cm[:, c:c + 1], None, ALU.mult, None, [b_scc, b_cm], [b_scc])
            for hh in range(8):
                p.stt(S[:, hh, :], S[:, hh, :], ea[:, c, hh:hh + 1], sc[:, hh, :], ALU.mult, ALU.add, [b_S[hh], b_ea, b_scc], [b_S[hh]])

    if full:
        sc_bufs = [(carve(2048).bitcast(F32).rearrange("p (h e) -> p h e", h=8), p.dbuf("sSc")) for _ in range(2)]

    wh_bufs = [(carve(KT * 512).rearrange("p (k c) -> p k c", k=KT), p.dbuf("wh")) for _ in range(2)]
    qT = carve(TOK)
    kT = carve(TOK)
    b_qT = p.buf("qT")
    b_kT = p.buf("kT")
    bcum = carve(2 * TOK).bitcast(F32)
    b_bc = p.buf("bcum")
    t1s = [(carve(1024).bitcast(F32), p.buf("t1")) for _ in range(1)]
    t2s = [(carve(1024).bitcast(F32), p.buf("t2")) for _ in range(1)]
    t3s = [(carve(1024).bitcast(F32), p.buf("t3")) for _ in range(1)]
    t4s = [(carve(1024).bitcast(F32), p.buf("t4")) for _ in range(1)]
    t5s = [(carve(1024).bitcast(F32), p.buf("t5")) for _ in range(1)]
    t6s = [(carve(1024).bitcast(F32), p.buf("t6")) for _ in range(1)]
    NCH = TOK // 64
    sc1 = p.sbuf("sc1", [128, NCH], F32)
    sc2 = p.sbuf("sc2", [128, NCH], F32)
    sc3 = p.sbuf("sc3", [128, NCH], F32)
    b_sc = p.buf("sc")
    v_ch = carve(NCH * 128).rearrange("p (c e) -> p c e", c=NCH)
    b_vch = p.buf("vch")
    if full:
        sgw = carve(NCH * 128).rearrange("p (c e) -> p c e", c=NCH)
        b_sgw = p.buf("sgw")
        ONT = carve(KT * TOK).rearrange("p (h t) -> p h t", h=8)
        b_ONT = [p.buf(f"ONT{t}") for t in range(NTILE)]
        wo_sb = carve(KT * D).rearrange("p (k c) -> p k c", k=KT)
        b_wo = p.dbuf("wo")
        p.dma("pool", wo_sb, w_out.rearrange("(kt p) c -> p kt c", p=128), writes=[b_wo])
    sgt_bufs = [(carve(256).bitcast(F32), p.buf("sgt")) for _ in range(2)]
    ktok_bufs = [(carve(128), p.buf("ktok")) for _ in range(2)]
    attm_bufs = [(carve(64), p.buf("attm")) for _ in range(2)]
    Sbf_bufs = [(carve(128), p.buf("Sbf")) for _ in range(2)]
    kvt_bufs = [(carve(256).bitcast(F32), p.buf("kvt")) for _ in range(2)]
    on_bufs = [(carve(128), p.buf("on")) for _ in range(4)]
    ss_l = [p.sbuf(p.uid("ss"), [64, 4], F32) for _ in range(2)]
    b_ss_l = [p.buf("ss") for _ in range(2)]
    junk_l = [carve(256).bitcast(F32) for _ in range(2)]
    b_junk_l = [p.buf("junk") for _ in range(2)]
    btot = p.sbuf("btot", [128, 8], F32)
    b_btot = p.buf("btot")

    BB = cx.bank_bufs
    bq = [(cx.banks[0], BB[0]), (cx.banks[2], BB[2])]
    bf_ = [(cx.banks[1], BB[1]), (cx.banks[3], BB[3])]
    vg_slots = [(cx.banks[4][0:64, 0:256], BB[4]), (cx.banks[5][0:64, 0:256], BB[5])]
    ktr_slots = [(cx.banks[0][:].bitcast(BF16)[0:64, 0:128], BB[0]), (cx.banks[1][:].bitcast(BF16)[0:64, 0:128], BB[1])]
    kv_slots = [(cx.banks[0][:, 0:128], BB[0]), (cx.banks[1][:, 0:128], BB[1])]
    att_slots = [(cx.banks[2][0:64, 0:64], BB[2]), (cx.banks[3][0:64, 0:64], BB[3])]
    o_slots = [(cx.banks[4][0:64, 0:128], BB[4]), (cx.banks[5][0:64, 0:128], BB[5])]
    ontr_slots = [(cx.banks[6][:].bitcast(BF16)[:, 0:64], BB[6]), (cx.banks[7][:].bitcast(BF16)[:, 0:64], BB[7])]

    w_in_v = w_in.rearrange("(kt p) (s h j) -> p kt s h j", p=128, s=4, h=8)
    import os
    if not full and os.environ.get("HG_SLOWA") is None:
        kkf = carve(2 * TOK).bitcast(F32)
        b_kk = p.buf("kkf")
        kdT = carve(TOK)
        b_kd = p.buf("kdT")
        vtok = carve(NTILE * 128).rearrange("p (t e) -> p t e", t=NTILE)
        b_vt = p.buf("vtok")
        ktk = carve(NTILE * 128).rearrange("p (t e) -> p t e", t=NTILE)
        b_ktk = p.buf("ktk")
        lf_bufs = [(carve(1024).bitcast(F32), p.buf("lf")) for _ in range(2)]
        sg_bufs2 = [(carve(1024).bitcast(F32), p.buf("sg2")) for _ in range(2)]
        ones512 = p.sbuf("ones512", [128, 512], F32)
        b_on5 = p.buf("ones512")
        p.memset("pool", ones512[:], 1.0, [b_on5])
        for h in range(8):
            wh, b_wh = wh_bufs[h % 2]
            for s_ in (1, 2):
                p.dma("pool", wh[:, :, s_ * 128:(s_ + 1) * 128], w_in_v[:, :, s_, h, :], writes=[b_wh])
            lb_h, oml_h, noml_h = lbc[:, 0, h:h + 1], lbc[:, 1, h:h + 1], lbc[:, 2, h:h + 1]
            for tb in range(TOK // 512):
                tok0 = tb * 512
                rd = [b_xT[tok0 // 128 + jj] for jj in range(4)]
                fps, b_fps = bf_[tb % 2]
                for kt in range(KT):
                    p.mm(fps[:], wh[:, kt, 128:256], xT[:, kt, tok0:tok0 + 512], kt == 0, kt == KT - 1, [b_wh] + rd, [b_fps])
                sg_, b_sg_ = sg_bufs2[tb % 2]
                lf_, b_lf_ = lf_bufs[tb % 2]
                p.act(sg_, fps[:], AF.Sigmoid, [], [b_fps, b_sg_])
                p.act(lf_, sg_, AF.Ln, [b_sg_, b_lbc], [b_lf_], scale=oml_h, bias=lb_h)
                p.ts("dve", kkf[:, tok0:tok0 + 512], sg_, 1.0, noml_h, ALU.subtract, ALU.mult, [b_sg_, b_lbc], [b_kk])
                init = 0.0 if tb == 0 else bcum[:, tok0 - 1:tok0]
                p.add("dve", lambda e, tok0=tok0, lf_=lf_, init=init: e.tensor_tensor_scan(out=bcum[:, tok0:tok0 + 512], data0=ones512[:], data1=lf_, initial=init,
                                                                                        op0=ALU.mult, op1=ALU.add), [b_on5, b_lf_, b_bc], [b_bc])
            p.copy("dve", btot[:, h:h + 1], bcum[:, TOK - 1:TOK], [b_bc], [b_btot])
            for tb in range(TOK // 512):
                tok0 = tb * 512
                lf_, b_lf_ = lf_bufs[tb % 2]
                p.act(lf_, bcum[:, tok0:tok0 + 512], AF.Exp, [b_bc, b_btot], [b_lf_], scale=-1.0, bias=btot[:, h:h + 1])
                p.tt("dve", kdT[:, tok0:tok0 + 512], kkf[:, tok0:tok0 + 512], lf_, ALU.mult, [b_kk, b_lf_], [b_kd])
            for t in range(NTILE):
                bk = 4 + (t % 2)
                vps = cx.banks[bk][:, 0:128]
                for kt in range(KT):
                    p.mm(vps, xT[:, kt, t * 128:(t + 1) * 128], wh[:, kt, 256:384], kt == 0, kt == KT - 1, [b_wh, b_xT[t]], [BB[bk]])
                p.copy("act", vtok[:, t, :], vps, [], [BB[bk], b_vt])
                bk2 = 6 + (t % 2)
                kps = cx.banks[bk2][:].bitcast(BF16)[:, 0:128]
                p.tr(kps, kdT[:, t * 128:(t + 1) * 128], cx.ident_bf[:], [b_kd, cx.b_ident], [BB[bk2]])
                p.copy("dve", ktk[:, t, :], kps, [], [BB[bk2], b_ktk])
            sps = cx.banks[h % 2][:, 0:128]
            for t in range(NTILE):
                p.mm(sps, ktk[:, t, :], vtok[:, t, :], t == 0, t == NTILE - 1, [b_ktk, b_vt], [BB[h % 2]])
            p.copy("act", S[:, h, :], sps, [], [BB[h % 2], b_S[h]])
        emit_summary(cx, io, [(S[:].rearrange("p h e -> p (h e)"), b_S, 0, 1024), (btot[:], [b_btot], 1024, 8)], osum, b_outd)
        return cx.done(own, [b_outd])
    STOP = int(os.environ.get("HG_STOP", "99"))
    NH = int(os.environ.get("HG_NH", "8"))
    for h in range(NH if STOP > 0 else 0):
        wh, b_wh = wh_bufs[h % 2]
        for s_ in range(4):
            if not full and s_ in (0, 3):
                continue
            p.dma("pool", wh[:, :, s_ * 128:(s_ + 1) * 128], w_in_v[:, :, s_, h, :], writes=[b_wh])
        lb_h, oml_h, noml_h = lbc[:, 0, h:h + 1], lbc[:, 1, h:h + 1], lbc[:, 2, h:h + 1]
        for tb in range(TOK // 512):
            tok0 = tb * 512
            rd = [b_xT[tok0 // 128 + jj] for jj in range(4)]
            (qps, b_qps), (fps, b_fps) = bq[tb % 2], bf_[tb % 2]
            if full:
                for kt in range(KT):
                    p.mm(qps[:], wh[:, kt, 0:128], xT[:, kt, tok0:tok0 + 512], kt == 0, kt == KT - 1, [b_wh] + rd, [b_qps])
            for kt in range(KT):
                p.mm(fps[:], wh[:, kt, 128:256], xT[:, kt, tok0:tok0 + 512], kt == 0, kt == KT - 1, [b_wh] + rd, [b_fps])
            (t1, b_t1), (t2, b_t2), (t3, b_t3) = t1s[0], t2s[0], t3s[0]
            (t4, b_t4), (t5, b_t5), (t6, b_t6) = t4s[0], t5s[0], t6s[0]
            p.act(t1, fps[:], AF.Sigmoid, [], [b_fps, b_t1])
            p.act(t4, t1, AF.Ln, [b_t1, b_lbc], [b_t4], scale=oml_h, bias=lb_h)
            p.ts("dve", t2, t1, 1.0, noml_h, ALU.subtract, ALU.mult, [b_t1, b_lbc], [b_t2])
            if full:
                p.act(t3, qps[:], AF.Silu, [], [b_qps, b_t3])
            blk = bcum[:, tok0:tok0 + 512]
            p.add("dve", lambda e, blk=blk, t4=t4: e.tensor_tensor_scan(out=blk, data0=cmk[:], data1=t4, initial=0.0, op0=ALU.mult, op1=ALU.add),
                  [b_cmk, b_t4], [b_bc])
            b3 = blk.rearrange("p (c t) -> p c t", t=64)
            t4v = t4.rearrange("p (c t) -> p c t", t=64)
            p.tt("dve", t4v, b3, b3[:, :, 32:33].to_broadcast([128, 8, 64]), ALU.subtract, [b_bc], [b_t4])
            p.act(t6, t4, AF.Exp, [b_t4], [b_t6], scale=-1.0)
            p.tt("dve", kT[:, tok0:tok0 + 512], t2, t6, ALU.mult, [b_t2, b_t6], [b_kT])
            if full:
                p.act(t5, t4, AF.Exp, [b_t4], [b_t5])
                p.tt("dve", qT[:, tok0:tok0 + 512], t3, t5, ALU.mult, [b_t3, b_t5], [b_qT])
            c0 = tb * 8
            p.act(sc1[:, c0:c0 + 8], bcum[:, tok0 + 32:tok0 + 512:64], AF.Exp, [b_bc], [b_sc])
            p.act(sc3[:, c0:c0 + 8], bcum[:, tok0 + 63:tok0 + 512:64], AF.Exp, [b_bc], [b_sc])
            p.tt("dve", sc2[:, c0:c0 + 8], bcum[:, tok0 + 63:tok0 + 512:64], bcum[:, tok0 + 32:tok0 + 512:64], ALU.subtract, [b_bc], [b_sc])
            p.act(sc2[:, c0:c0 + 8], sc2[:, c0:c0 + 8], AF.Exp, [b_sc], [b_sc])
        if STOP <= 1:
            continue
        if not full and os.environ.get("HG_NORED") is None:
            p.add("dve", lambda e, h=h: e.tensor_reduce(out=btot[:, h:h + 1], in_=bcum[:, 63:TOK:64], axis=AX.X, op=ALU.add), [b_bc], [b_btot])
        for c in range(NCH):
            vg, b_vg = vg_slots[c % 2]
            rd = [b_xT[c // 2]]
            ncol = 256 if full else 128
            for kt in range(KT):
                p.mm(vg[:, 0:ncol], xT[:, kt, c * 64:(c + 1) * 64], wh[:, kt, 256:256 + ncol], kt == 0, kt == KT - 1, [b_wh] + rd, [b_vg])
            p.copy("act", v_ch[0:64, c, :], vg[:, 0:128], [], [b_vg, b_vch])
            if full:
                sgt, b_sgt = sgt_bufs[c % 2]
                p.act(sgt[0:64, :], vg[:, 128:256], AF.Silu, [], [b_vg, b_sgt])
                p.tt("dve", sgw[0:64, c, :], sgt[0:64, :], nws[:], ALU.mult, [b_sgt, b_nws], [b_sgw])
        if full and h == 0:
            combine()
        pend_norm, pend_tail = [], []
        for c0 in range(0, NCH if STOP > 2 else 0, 2):
            pair = (c0, c0 + 1)
            for c in pair:
                ktr, b_ktr = ktr_slots[c % 2]
                p.tr(ktr, kT[:, c * 64:(c + 1) * 64], cx.ident_bf[:], [b_kT, cx.b_ident], [b_ktr])
            for c in pair:
                ktr, b_ktr = ktr_slots[c % 2]
                ktok, b_ktok = ktok_bufs[c % 2]
                p.copy("act", ktok[0:64, :], ktr, [], [b_ktr, b_ktok])
            if full:
                for c in pair:
                    att, b_att = att_slots[c % 2]
                    cs = slice(c * 64, (c + 1) * 64)
                    p.mm(att, kT[:, cs], qT[:, cs], True, True, [b_kT, b_qT], [b_att])
                for c in pair:
                    att, b_att = att_slots[c % 2]
                    attm, b_attm = attm_bufs[c % 2]
                    p.tt("dve", attm[0:64, :], att, maskT[:], ALU.mult, [b_mask], [b_att, b_attm])
            for c in pair:
                kv, b_kv = kv_slots[c % 2]
                ktok, b_ktok = ktok_bufs[c % 2]
                p.mm(kv, ktok[0:64, :], v_ch[0:64, c, :], True, True, [b_ktok, b_vch], [b_kv])
            for c in pair:
                kv, b_kv = kv_slots[c % 2]
                kvt, b_kvt = kvt_bufs[c % 2]
                p.act(kvt, kv, AF.Identity, [b_sc], [b_kv, b_kvt], scale=sc2[:, c:c + 1])
            for c in pair:
                i2 = c % 2
                cs = slice(c * 64, (c + 1) * 64)
                kvt, b_kvt = kvt_bufs[i2]
                if full:
                    Sbf, b_Sbf = Sbf_bufs[i2]
                    p.act(Sbf, S[:, h, :], AF.Identity, [b_S[h], b_sc], [b_Sbf], scale=sc1[:, c:c + 1])
                    attm, b_attm = attm_bufs[i2]
                    o, b_o = o_slots[i2]
                    p.mm(o, attm[0:64, :], v_ch[0:64, c, :], True, False, [b_attm, b_vch], [b_o])
                    p.mm(o, qT[:, cs], Sbf, False, True, [b_qT, b_Sbf], [b_o])
                p.stt(S[:, h, :], S[:, h, :], sc3[:, c:c + 1], kvt, ALU.mult, ALU.add, [b_S[h], b_sc, b_kvt], [b_S[h]])
            if full:
                def norm_fn(pair=pair, h=h):
                    for c in pair:
                        i2 = c % 2
                        o, b_o = o_slots[i2]
                        ss, b_ss, junk, b_junk = ss_l[i2], b_ss_l[i2], junk_l[i2], b_junk_l[i2]
                        p.act(junk[0:64, :], o, AF.Square, [], [b_o, b_junk, b_ss], accum=ss[:, 0:1])
                    for c in pair:
                        ss, b_ss = ss_l[c % 2], b_ss_l[c % 2]
                        p.act(ss[:, 2:3], ss[:, 0:1], AF.Sqrt, [b_ss, cx.b_eps], [b_ss], bias=cx.eps_hn[0:64, 0:1])
                    for c in pair:
                        ss, b_ss = ss_l[c % 2], b_ss_l[c % 2]
                        p.add("dve", lambda e, ss=ss: e.reciprocal(ss[:, 1:2], ss[:, 2:3]), [b_ss], [b_ss])
                    for c in pair:
                        i2 = c % 2
                        o, b_o = o_slots[i2]
                        ss, b_ss = ss_l[i2], b_ss_l[i2]
                        on, b_on = on_bufs[c % 4]
                        p.stt(on[0:64, :], o, ss[:, 1:2], sgw[0:64, c, :], ALU.mult, ALU.mult, [b_ss, b_sgw], [b_o, b_on])

                def tail_fn(pair=pair, h=h):
                    for c in pair:
                        on, b_on = on_bufs[c % 4]
                        ontr, b_ontr = ontr_slots[c % 2]
                        p.tr(ontr, on[0:64, :], cx.ident_bf[0:64, 0:64], [b_on, cx.b_ident], [b_ontr])
                    for c in pair:
                        ontr, b_ontr = ontr_slots[c % 2]
                        p.copy("act", ONT[:, h, c * 64:(c + 1) * 64], ontr, [], [b_ontr, b_ONT[c // 2]])

                pend_norm.append(norm_fn)
                pend_tail.append(tail_fn)
                pend_norm.pop(0)()
                if len(pend_tail) > 1:
                    pend_tail.pop(0)()
        while full and pend_norm:
            pend_norm.pop(0)()
        while full and pend_tail:
            pend_tail.pop(0)()
    if not full:
        emit_summary(cx, io, [(S[:].rearrange("p h e -> p (h e)"), b_S, 0, 1024), (btot[:], [b_btot], 1024, 8)], osum, b_outd)
        return cx.done(own, [b_outd])

    p.barrier()
    off = KT * TOK
    xr_bufs = [(carve(2048).bitcast(F32), p.dbuf("xr")) for _ in range(2)]
    y_bufs = [(carve(2048).bitcast(F32), p.buf("y")) for _ in range(2)]
    o_bufs = [(carve(2048).bitcast(F32), p.buf("o")) for _ in range(2)]
    tmp = [carve(2048).bitcast(F32) for _ in range(2)]
    b_tmp = [p.buf("tmp") for _ in range(2)]
    xbf = carve(1024)
    b_xbf = p.buf("xbf")
    st = [p.sbuf(p.uid("st1_"), [128, 16], F32) for _ in range(2)]
    b_st = [p.buf("st1") for _ in range(2)]
    b_x1s = p.dbuf("x1s")
    assert off <= KT * TOK + 30720, off
    b_pstr = p.buf("pstr")
    hb = [BB[0], BB[1], BB[2], BB[3]]
    for t in range(NTILE):
        ts_ = slice(t * 128, (t + 1) * 128)
        hps = [cx.banks[(t % 2) * 2], cx.banks[(t % 2) * 2 + 1]]
        hbb = [hb[(t % 2) * 2], hb[(t % 2) * 2 + 1]]
        for half in range(2):
            for hh in range(8):
                p.mm(hps[half][:], ONT[:, hh, ts_], wo_sb[:, hh, half * 512:(half + 1) * 512], hh == 0, hh == 7, [b_ONT[t], b_wo], [hbb[half]])
        residual_ln1_tile(cx, t, [hps[0][:], hps[1][:]], hbb, x, xr_bufs, y_bufs, o_bufs, tmp, b_tmp, st, b_st,
                          lng[:, 0, :], lnb[:, 0, :], b_gb, x1s, b_x1s, x1T, b_x1T, xbf, b_xbf, b_pstr)
    p.barrier()
    ffn_phase(cx, x1T, b_x1T, x1s, b_x1s, wgu, wd, lng[:, 1, :], lnb[:, 1, :], b_gb, y_out, b_outd, arena, KT * TOK)
    return cx.done(own, [b_outd])


ML_W = 3088


def build_mlstm(layer, j, stage, cx=None, io=None):
    import os
    own = cx is None
    if own:
        cx = Ctx(bass.Bass("TRN2", target_bir_lowering=False))
    nc = cx.nc
    p = cx.p

    def dt_(name, shape, kind="Internal"):
        if io is not None and name in io:
            return io[name]
        return nc.dram_tensor(name, shape, F32, kind=kind).ap()
    BB = cx.bank_bufs
    full = stage == "B"
    x = dt_("x", [TOK, D], "ExternalInput")
    xh = dt_("xh", [3, D], "ExternalInput")
    w_in = dt_("w_in", [D, ML_W], "ExternalInput")
    conv_w = dt_("conv_w", [4, D], "ExternalInput")
    gate_b = dt_("gate_b", [16, 1], "ExternalInput")
    if full:
        norm_w = dt_("norm_w", [1, D], "ExternalInput")
        w_out = dt_("w_out", [D, D], "ExternalInput")
        ln_g = dt_("ln_g", [2, D], "ExternalInput")
        ln_b = dt_("ln_b", [2, D], "ExternalInput")
        wgu = dt_("wgu", [D, 2 * FFN_H], "ExternalInput")
        wd = dt_("wd", [FFN_H, D], "ExternalInput")
        ssum = dt_("ssum", [NCORES, 128, 520], "ExternalInput")
        cmask = dt_("cmask", [128, NCORES], "ExternalInput")
        y_out = dt_("y", [TOK, D], "ExternalOutput")
        x1s = dt_("x1s", [TOK, D])
    else:
        osum = dt_("osum", [128, 520], "ExternalOutput") if (io is None or "bin" not in io) else None
    b_outd = p.dbuf("outd")
    NCH = TOK // 64

    load_ohot(cx, io, full)
    cmk = p.sbuf("chunkmask", [128, 512], F32)
    b_cmk = p.buf("cmk")
    p.memset("pool", cmk[:], 1.0, [b_cmk])
    p.memset("pool", cmk[:, 0:512:64], 0.0, [b_cmk])
    cwl, b_cwl = cx.const_load("cw_l", [32, 128], F32, conv_w.rearrange("j (b d) -> (j b) d", d=128))
    cw = p.sbuf("cw", [128, 4, 8], F32)
    b_cw = p.buf("cw")
    p.tr(cx.banks[7][:, 0:32], cwl[:], cx.ident_f[0:32, 0:32], [b_cwl, cx.b_ident], [BB[7]])
    p.copy("dve", cw[:], cx.banks[7][:, 0:32].rearrange("p (j b) -> p j b", j=4), [], [BB[7], b_cw])
    gbi, b_gbi = cx.const_load("gbi", [8, 1], F32, gate_b[0:8, :])
    gbf, b_gbf = cx.const_load("gbf", [8, 1], F32, gate_b[8:16, :])
    ngbf = p.sbuf("ngbf", [8, 1], F32)
    b_ng = p.buf("ngbf")
    p.ts("dve", ngbf[:], gbf[:], -1.0, None, ALU.mult, None, [b_gbf], [b_ng])
    sel = p.sbuf("sel", [8, 4, 128], F32)
    b_sel = p.buf("sel")
    p.memset("pool", sel[:], 1.0, [b_sel])
    for blk in range(4):
        p.add("pool", lambda e, blk=blk: e.affine_select(out=sel[:, blk, :], in_=sel[:, blk, :], pattern=[[1, 128]], compare_op=ALU.is_ge,
                                                         fill=0.0, base=128 * blk, channel_multiplier=-64), [b_sel], [b_sel])
        p.add("pool", lambda e, blk=blk: e.affine_select(out=sel[:, blk, :], in_=sel[:, blk, :], pattern=[[-1, 128]], compare_op=ALU.is_ge,
                                                         fill=0.0, base=63 - 128 * blk, channel_multiplier=64), [b_sel], [b_sel])
    if full:
        ones64 = p.sbuf("ones64", [64, 64], F32)
        maskT = p.sbuf("maskT", [64, 64], F32)
        b_mask = p.buf("mask")
        p.memset("pool", ones64[:], 1.0, [b_mask])
        p.add("pool", lambda e: e.affine_select(out=maskT[:], in_=ones64[:], pattern=[[1, 64]], compare_op=ALU.is_ge, fill=0.0,
                                                base=0, channel_multiplier=-1), [b_mask], [b_mask])
        nwb, b_nwb = cx.const_load("nw_bc", [64, D], F32, norm_w.broadcast_to([64, D]))
        b_nws = p.buf("nws")
        p.ts("dve", nwb[:], nwb[:], math.sqrt(128.0), None, ALU.mult, None, [b_nwb], [b_nws])
        nws = nwb
        lng, lnb, b_gb = load_ln_consts(cx, ln_g, ln_b)
        cm, b_cm = cx.const_load("cmask_sb", [128, NCORES], F32, cmask)

    ARENA = 88 * 1024 + 512
    arena = p.sbuf("arena", [128, ARENA], BF16)
    off = 0

    def carve(n):
        nonlocal off
        v = arena[:, off:off + n]
        off += n
        assert off <= ARENA, off
        return v

    b_x1T = [p.buf(f"x1T{t}") for t in range(NTILE)]
    xT = carve(KT * TOK).rearrange("p (k t) -> p k t", k=KT)
    x1T = xT
    b_xT = [p.buf(f"xT{t}") for t in range(NTILE)]
    xb = [(carve(D), p.dbuf("xbf")) for _ in range(2)]
    load_xT(cx, x, xT, b_xT, xb)
    xhb = carve(D)
    b_xhb = p.dbuf("xhb")
    p.memset("dve", xhb, 0.0, [b_xhb])
    p.dma("pool", xhb[0:3, :], xh, writes=[b_xhb])
    xhT = carve(KT * 128).rearrange("p (k t) -> p k t", k=KT)
    b_xhT = p.buf("xhT")
    ps7 = cx.banks[7][:].bitcast(BF16)
    for kt in range(KT):
        p.tr(ps7[:, kt * 128:(kt + 1) * 128], xhb[:, kt * 128:(kt + 1) * 128], cx.ident_bf[:], [b_xhb, cx.b_ident], [BB[7]])
    p.copy("dve", xhT, ps7.rearrange("p (k t) -> p k t", k=KT), [], [BB[7], b_xhT])

    C = p.sbuf("Cst", [128, 4, 129], F32)
    b_C = [p.buf(f"C{b}") for b in range(4)]
    for b in range(4):
        p.memset("dve", C[:, b, :], 0.0, [b_C[b]])
    rdb = [io["b_bout"]] if (io is not None and "b_bout" in io) else []

    def combine():
        sBt = p.sbuf("sB_sb", [128, NCORES, 4], F32)
        b_sBt = p.dbuf("sBt")
        p.dma("sp", sBt[:], ssum[:, :, 516:520].rearrange("c p b -> p c b"), reads=rdb, writes=[b_sBt])
        ea = p.sbuf("ea", [128, NCORES, 4], F32)
        b_ea = p.buf("ea")
        p.act(ea[:], sBt[:], AF.Exp, [b_sBt], [b_ea])
        p.ts("dve", ea[:], ea[:], -1.0, None, ALU.add, None, [b_ea], [b_ea])
        p.tt("dve", ea[:], ea[:], cm[:].unsqueeze(2).to_broadcast([128, NCORES, 4]), ALU.mult, [b_ea, b_cm], [b_ea])
        p.ts("dve", ea[:], ea[:], 1.0, None, ALU.add, None, [b_ea], [b_ea])
        for c in range(NCORES):
            sc, b_scc = sc_bufs[c % 2]
            p.dma("sp", sc, ssum[c, :, 0:516].rearrange("p (b e) -> p b e", b=4), reads=rdb, writes=[b_scc])
            p.ts("dve", sc, sc, cm[:, c:c + 1], None, ALU.mult, None, [b_scc, b_cm], [b_scc])
            for b in range(4):
                p.stt(C[:, b, :], C[:, b, :], ea[:, c, b:b + 1], sc[:, b, :], ALU.mult, ALU.add, [b_C[b], b_ea, b_scc], [b_C[b]])

    if full:
        sc_bufs = [(carve(2 * 4 * 129 + 8)[:, 0:2 * 4 * 129].bitcast(F32).rearrange("p (b e) -> p b e", b=4), p.dbuf("sCc")) for _ in range(2)]

    wg_sb = carve(KT * 16).rearrange("p (k c) -> p k c", k=KT)
    b_wg = p.dbuf("wgate")
    p.dma("pool", wg_sb, w_in.rearrange("(kt p) c -> p kt c", p=128)[:, :, 3072:3088], writes=[b_wg])
    aT = carve(2 * TOK).bitcast(F32)
    cT = carve(2 * TOK).bitcast(F32)
    b_aT = p.buf("aT")
    b_cT = p.buf("cT")
    gt = [carve(1024).bitcast(F32) for _ in range(3)]
    b_gt = p.buf("gt")
    fastA = (not full) and os.environ.get("ML_SLOWA") is None
    ones8 = cmk
    if fastA:
        ones8 = p.sbuf("ones8", [8, 512], F32)
        p.memset("pool", ones8[:], 1.0, [b_cmk])
    for tb in range(TOK // 512):
        tok0 = tb * 512
        rd = [b_xT[tok0 // 128 + jj] for jj in range(4)]
        gi_ps, gf_ps = cx.banks[4], cx.banks[5]
        for kt in range(KT):
            p.mm(gi_ps[0:8, :], wg_sb[:, kt, 0:8], xT[:, kt, tok0:tok0 + 512], kt == 0, kt == KT - 1, [b_wg] + rd, [BB[4]])
        for kt in range(KT):
            p.mm(gf_ps[0:8, :], wg_sb[:, kt, 8:16], xT[:, kt, tok0:tok0 + 512], kt == 0, kt == KT - 1, [b_wg] + rd, [BB[5]])
        p.act(gt[0][0:8, :], gf_ps[0:8, :], AF.Exp, [b_ng], [BB[5], b_gt], scale=-1.0, bias=ngbf[:, 0:1])
        p.act(gt[0][0:8, :], gt[0][0:8, :], AF.Ln, [b_gt], [b_gt], bias=1.0)
        if fastA:
            init = 0.0 if tb == 0 else aT[0:8, tok0 - 1:tok0]
            p.add("dve", lambda e, tok0=tok0, init=init: e.tensor_tensor_scan(out=aT[0:8, tok0:tok0 + 512], data0=ones8[0:8, :], data1=gt[0][0:8, :], initial=init,
                                                                             op0=ALU.mult, op1=ALU.add), [b_cmk, b_gt, b_aT], [b_aT])
            p.stt(cT[0:8, tok0:tok0 + 512], gi_ps[0:8, :], gbi[:, 0:1], aT[0:8, tok0:tok0 + 512], ALU.add, ALU.add, [b_gbi, b_aT], [BB[4], b_cT])
            continue
        p.add("dve", lambda e, tb=tb: e.tensor_tensor_scan(out=gt[1][0:8, :], data0=cmk[0:8, :], data1=gt[0][0:8, :], initial=0.0, op0=ALU.mult, op1=ALU.add),
              [b_cmk, b_gt], [b_gt])
        p.act(aT[0:8, tok0:tok0 + 512], gt[1][0:8, :], AF.Exp, [b_gt], [b_aT], scale=-1.0)
        p.stt(gt[2][0:8, :], gi_ps[0:8, :], gbi[:, 0:1], gt[1][0:8, :], ALU.add, ALU.add, [b_gbi, b_gt], [BB[4], b_gt])
        p.act(cT[0:8, tok0:tok0 + 512], gt[2][0:8, :], AF.Exp, [b_gt], [b_cT])
    if not full:
        lt8 = p.sbuf("lt8", [8, NCH + 1], F32)
        b_lt8 = p.buf("lt8")
        if fastA:
            p.ts("dve", lt8[:, NCH:NCH + 1], aT[0:8, TOK - 1:TOK], -1.0, None, ALU.mult, None, [b_aT], [b_lt8])
            p.act(cT[0:8, :], cT[0:8, :], AF.Exp, [b_cT, b_lt8], [b_cT], bias=lt8[:, NCH:NCH + 1])
        else:
            p.act(lt8[:, 0:NCH], aT[0:8, 63:TOK:64], AF.Ln, [b_aT], [b_lt8])
            p.add("dve", lambda e: e.tensor_reduce(out=lt8[:, NCH:NCH + 1], in_=lt8[:, 0:NCH], axis=AX.X, op=ALU.add), [b_lt8], [b_lt8])

    wqk_bufs = [(carve(KT * 256).rearrange("p (k c) -> p k c", k=KT), p.dbuf("wqk")) for _ in range(1)]
    pre_q = carve(2 * (TOK + 4)).bitcast(F32)
    pre_k = carve(2 * (TOK + 4)).bitcast(F32)
    b_pq = p.buf("pre_q")
    b_pk = p.buf("pre_k")
    cvt = [(carve(1024).bitcast(F32), p.buf("cvt")) for _ in range(2)]
    qT = carve(TOK)
    kT = carve(TOK)
    b_qT = p.buf("qT")
    b_kT = p.buf("kT")
    wvo_bufs = [(carve(KT * 256).rearrange("p (k c) -> p k c", k=KT), p.dbuf("wvo")) for _ in range(2)]
    vaug = carve(NCH * 130).rearrange("p (c e) -> p c e", c=NCH)
    b_va = p.buf("vaug")
    p.memset("pool", vaug[0:64, :, 128:129], 1.0, [b_va])
    if full:
        sgw = carve(NCH * 128).rearrange("p (c e) -> p c e", c=NCH)
        b_sgw = p.buf("sgw")
        ONT = carve(KT * TOK).rearrange("p (h t) -> p h t", h=8)
        b_ONT = [p.buf(f"ONT{t}") for t in range(NTILE)]
        wo_sb = carve(KT * D).rearrange("p (k c) -> p k c", k=KT)
        b_wo = p.dbuf("wo")
        p.dma("pool", wo_sb, w_out.rearrange("(kt p) c -> p kt c", p=128), writes=[b_wo])
    sgt_bufs = [(carve(256).bitcast(F32), p.buf("sgt")) for _ in range(2)]
    ktok_bufs = [(carve(128), p.buf("ktok")) for _ in range(2)]
    sm_bufs = [(carve(64), p.buf("smk")) for _ in range(2)]
    akv_bufs = [(carve(264).bitcast(F32)[:, 0:129], p.buf("akv")) for _ in range(2)]
    Cbf_bufs = [(carve(136), p.buf("Cbf")) for _ in range(4)]
    for cb_, b_cb_ in Cbf_bufs:
        p.memset("dve", cb_, 0.0, [b_cb_])
    on_bufs = [(carve(128), p.buf("on")) for _ in range(4)]
    ss_l = [p.sbuf(p.uid("ss"), [64, 8], F32) for _ in range(2)]
    b_ss_l = [p.buf("ss") for _ in range(2)]
    junk_l = [carve(256).bitcast(F32) for _ in range(2)]
    b_junk_l = [p.buf("junk") for _ in range(2)]
    abc_sb = carve(2 * NCH).bitcast(F32)
    b_abc = p.buf("abc")

    w_v = w_in.rearrange("(kt p) c -> p kt c", p=128)
    NBLK = int(os.environ.get("ML_NBLK", "4"))
    if fastA:
        ktk = carve(NTILE * 128).rearrange("p (t e) -> p t e", t=NTILE)
        b_ktk = p.buf("ktk")
        va128 = carve(NTILE * 130).rearrange("p (t e) -> p t e", t=NTILE)
        p.memset("pool", va128[:, :, 128:129], 1.0, [b_va])
    for blk in range(NBLK):
        wqk, b_wqk = wqk_bufs[0]
        if full:
            p.dma("pool", wqk[:, :, 0:128], w_v[:, :, blk * 128:(blk + 1) * 128], writes=[b_wqk])
        p.dma("pool", wqk[:, :, 128:256], w_v[:, :, 512 + blk * 128:512 + (blk + 1) * 128], writes=[b_wqk])
        hq, hk = cx.banks[6], cx.banks[7]
        if full:
            for kt in range(KT):
                p.mm(hq[:, 0:4], wqk[:, kt, 0:128], xhT[:, kt, 0:4], kt == 0, kt == KT - 1, [b_wqk, b_xhT], [BB[6]])
            p.copy("dve", pre_q[:, 1:4], hq[:, 0:3], [], [BB[6], b_pq])
        for kt in range(KT):
            p.mm(hk[:, 0:4], wqk[:, kt, 128:256], xhT[:, kt, 0:4], kt == 0, kt == KT - 1, [b_wqk, b_xhT], [BB[7]])
        p.copy("dve", pre_k[:, 1:4], hk[:, 0:3], [], [BB[7], b_pk])
        for tb in range(TOK // 512):
            tok0 = tb * 512
            rd = [b_xT[tok0 // 128 + jj] for jj in range(4)]
            qps, kps = cx.banks[(tb % 2) * 2], cx.banks[(tb % 2) * 2 + 1]
            b_qps, b_kps = BB[(tb % 2) * 2], BB[(tb % 2) * 2 + 1]
            if full:
                for kt in range(KT):
                    p.mm(qps[:], wqk[:, kt, 0:128], xT[:, kt, tok0:tok0 + 512], kt == 0, kt == KT - 1, [b_wqk] + rd, [b_qps])
                p.copy("act", pre_q[:, 4 + tok0:4 + tok0 + 512], qps[:], [], [b_qps, b_pq])
            for kt in range(KT):
                p.mm(kps[:], wqk[:, kt, 128:256], xT[:, kt, tok0:tok0 + 512], kt == 0, kt == KT - 1, [b_wqk] + rd, [b_kps])
            p.copy("act", pre_k[:, 4 + tok0:4 + tok0 + 512], kps[:], [], [b_kps, b_pk])
            abc_ps, cbc_ps = cx.banks[4], cx.banks[5]
            p.mm(abc_ps[:], sel[:, blk, :], aT[0:8, tok0:tok0 + 512], True, True, [b_sel, b_aT], [BB[4]])
            p.mm(cbc_ps[:], sel[:, blk, :], cT[0:8, tok0:tok0 + 512], True, True, [b_sel, b_cT], [BB[5]])
            if not fastA:
                p.copy("dve", abc_sb[:, tb * 8:(tb + 1) * 8], abc_ps[:, 63:512:64], [], [BB[4], b_abc])
            for which in ((0, 1) if full else (1,)):
                pre, b_pre = (pre_q, b_pq) if which == 0 else (pre_k, b_pk)
                cb = blk if which == 0 else 4 + blk
                t0_, b_t0 = cvt[0]
                t1_, b_t1 = cvt[1]
                base = 4 + tok0
                p.ts("dve", t0_, pre[:, base - 3:base - 3 + 512], cw[:, 0, cb:cb + 1], None, ALU.mult, None, [b_pre, b_cw], [b_t0])
                for jj in (1, 2, 3):
                    p.stt(t0_, pre[:, base - 3 + jj:base - 3 + jj + 512], cw[:, jj, cb:cb + 1], t0_, ALU.mult, ALU.add, [b_pre, b_cw, b_t0], [b_t0])
                p.act(t1_, t0_, AF.Silu, [b_t0], [b_t1])
                if which == 0:
                    p.tt("dve", qT[:, tok0:tok0 + 512], t1_, abc_ps[:], ALU.mult, [b_t1], [BB[4], b_qT])
                else:
                    p.stt(kT[:, tok0:tok0 + 512], t1_, 0.125, cbc_ps[:], ALU.mult, ALU.mult, [b_t1], [BB[5], b_kT])
        if fastA:
            for t in range(NTILE):
                bk2 = 6 + (t % 2)
                kps = cx.banks[bk2][:].bitcast(BF16)[:, 0:128]
                p.tr(kps, kT[:, t * 128:(t + 1) * 128], cx.ident_bf[:], [b_kT, cx.b_ident], [BB[bk2]])
                p.copy("dve", ktk[:, t, :], kps, [], [BB[bk2], b_ktk])
            for hl in range(2):
                h = 2 * blk + hl
                rows = slice(64 * hl, 64 * hl + 64)
                wvo, b_wvo = wvo_bufs[h % 2]
                p.dma("pool", wvo[:, :, 0:128], w_v[:, :, 1024 + h * 128:1024 + (h + 1) * 128], writes=[b_wvo])
                for t in range(NTILE):
                    bk = 4 + (t % 2)
                    vps = cx.banks[bk][:, 0:128]
                    for kt in range(KT):
                        p.mm(vps, xT[:, kt, t * 128:(t + 1) * 128], wvo[:, kt, 0:128], kt == 0, kt == KT - 1, [b_wvo, b_xT[t]], [BB[bk]])
                    p.copy("act", va128[:, t, 0:128], vps, [], [BB[bk], b_va])
                cps = cx.banks[hl][:, 0:129]
                for t in range(NTILE):
                    p.mm(cps, ktk[:, t, :], va128[:, t, 0:129], t == 0, t == NTILE - 1, [b_ktk, b_va], [BB[hl]])
                p.copy("act", C[rows, blk, :], cps[rows, :], [], [BB[hl], b_C[blk]])
            continue
        for hl in range(2):
            h = 2 * blk + hl
            rows = slice(64 * hl, 64 * hl + 64)
            wvo, b_wvo = wvo_bufs[h % 2]
            p.dma("pool", wvo[:, :, 0:128], w_v[:, :, 1024 + h * 128:1024 + (h + 1) * 128], writes=[b_wvo])
            if full:
                p.dma("pool", wvo[:, :, 128:256], w_v[:, :, 2048 + h * 128:2048 + (h + 1) * 128], writes=[b_wvo])
            ncol = 256 if full else 128
            for c in range(NCH):
                bk = 4 + (c % 2)
                vg = cx.banks[bk][0:64, 0:256]
                for kt in range(KT):
                    p.mm(vg[:, 0:ncol], xT[:, kt, c * 64:(c + 1) * 64], wvo[:, kt, 0:ncol], kt == 0, kt == KT - 1, [b_wvo, b_xT[c // 2]], [BB[bk]])
                p.copy("act", vaug[0:64, c, 0:128], vg[:, 0:128], [], [BB[bk], b_va])
                if full:
                    sgt, b_sgt = sgt_bufs[c % 2]
                    p.act(sgt[0:64, :], vg[:, 128:256], AF.Sigmoid, [], [BB[bk], b_sgt])
                    p.tt("dve", sgw[0:64, c, :], sgt[0:64, :], nws[:, h * 128:(h + 1) * 128], ALU.mult, [b_sgt, b_nws], [b_sgw])
            if full and blk == 0 and hl == 0:
                combine()
            pend_tail = []
            for c0 in range(0, NCH, 2):
                pair = (c0, c0 + 1)
                for c in pair:
                    ktr = cx.banks[c % 2][:].bitcast(BF16)[0:64, 0:128]
                    p.tr(ktr, kT[:, c * 64:(c + 1) * 64], cx.ident_bf[:], [b_kT, cx.b_ident], [BB[c % 2]])
                for c in pair:
                    ktr = cx.banks[c % 2][:].bitcast(BF16)[0:64, 0:128]
                    ktok, b_ktok = ktok_bufs[c % 2]
                    p.copy("act", ktok[0:64, :], ktr, [], [BB[c % 2], b_ktok])
                if full:
                    for c in pair:
                        cs = slice(c * 64, (c + 1) * 64)
                        sT = cx.banks[2 + c % 2][0:64, 0:64]
                        p.mm(sT, kT[rows, cs], qT[rows, cs], True, True, [b_kT, b_qT], [BB[2 + c % 2]])
                    for c in pair:
                        sT = cx.banks[2 + c % 2][0:64, 0:64]
                        smk, b_smk = sm_bufs[c % 2]
                        p.tt("dve", smk[0:64, :], sT, maskT[:], ALU.mult, [b_mask], [BB[2 + c % 2], b_smk])
                for c in pair:
                    kv = cx.banks[c % 2][:, 256:256 + 129]
                    ktok, b_ktok = ktok_bufs[c % 2]
                    p.mm(kv, ktok[0:64, :], vaug[0:64, c, 0:129], True, True, [b_ktok, b_va], [BB[c % 2]])
                for c in pair:
                    i2 = c % 2
                    cs = slice(c * 64, (c + 1) * 64)
                    kv = cx.banks[i2][:, 256:256 + 129]
                    if full:
                        Cbf, b_Cbf = Cbf_bufs[hl * 2 + i2]
                        p.copy("act", Cbf[rows, 0:129], C[rows, blk, :], [b_C[blk]], [b_Cbf])
                        smk, b_smk = sm_bufs[i2]
                        num = cx.banks[4 + i2][0:64, 256:256 + 129]
                        p.mm(num, smk[0:64, :], vaug[0:64, c, 0:129], True, False, [b_smk, b_va], [BB[4 + i2]])
                        p.mm(num, qT[:, cs], Cbf[:, 0:129], False, True, [b_qT, b_Cbf], [BB[4 + i2]])
                    akv, b_akv = akv_bufs[i2]
                    p.act(akv[rows, :], kv[rows, :], AF.Identity, [b_abc], [BB[i2], b_akv], scale=abc_sb[rows, c:c + 1])
                    p.stt(C[rows, blk, :], C[rows, blk, :], abc_sb[rows, c:c + 1], akv[rows, :], ALU.mult, ALU.add, [b_C[blk], b_abc, b_akv], [b_C[blk]])
                if full:
                    for c in pair:
                        i2 = c % 2
                        num = cx.banks[4 + i2][0:64, 256:256 + 129]
                        ss, b_ss, junk, b_junk = ss_l[i2], b_ss_l[i2], junk_l[i2], b_junk_l[i2]
                        p.act(ss[:, 0:1], num[:, 128:129], AF.Square, [], [BB[4 + i2], b_ss])
                        p.act(junk[0:64, :], num[:, 0:128], AF.Square, [], [BB[4 + i2], b_junk, b_ss], accum=ss[:, 2:3])
                    for c in pair:
                        ss, b_ss = ss_l[c % 2], b_ss_l[c % 2]
                        p.ts("dve", ss[:, 1:2], ss[:, 0:1], 1.0, 128.0 * HN_EPS, ALU.max, ALU.mult, [b_ss], [b_ss])
                        p.tt("dve", ss[:, 3:4], ss[:, 1:2], ss[:, 2:3], ALU.add, [b_ss], [b_ss])
                    for c in pair:
                        ss, b_ss = ss_l[c % 2], b_ss_l[c % 2]
                        p.act(ss[:, 4:5], ss[:, 3:4], AF.Sqrt, [b_ss], [b_ss])
                    for c in pair:
                        ss, b_ss = ss_l[c % 2], b_ss_l[c % 2]
                        p.add("dve", lambda e, ss=ss: e.reciprocal(ss[:, 5:6], ss[:, 4:5]), [b_ss], [b_ss])
                    for c in pair:
                        i2 = c % 2
                        num = cx.banks[4 + i2][0:64, 256:256 + 129]
                        ss, b_ss = ss_l[i2], b_ss_l[i2]
                        on, b_on = on_bufs[c % 4]
                        p.stt(on[0:64, :], num[:, 0:128], ss[:, 5:6], sgw[0:64, c, :], ALU.mult, ALU.mult, [b_ss, b_sgw], [BB[4 + i2], b_on])

                    def tail_fn(pair=pair, h=h):
                        for c in pair:
                            on, b_on = on_bufs[c % 4]
                            ontr = cx.banks[6 + c % 2][:].bitcast(BF16)[:, 0:64]
                            p.tr(ontr, on[0:64, :], cx.ident_bf[0:64, 0:64], [b_on, cx.b_ident], [BB[6 + c % 2]])
                        for c in pair:
                            ontr = cx.banks[6 + c % 2][:].bitcast(BF16)[:, 0:64]
                            p.copy("act", ONT[:, h, c * 64:(c + 1) * 64], ontr, [], [BB[6 + c % 2], b_ONT[c // 2]])

                    pend_tail.append(tail_fn)
                    if len(pend_tail) > 1:
                        pend_tail.pop(0)()
            while full and pend_tail:
                pend_tail.pop(0)()
    if not full:
        ob = p.sbuf("ob", [128, 4], F32)
        b_ob = p.buf("ob")
        for blk in range(4):
            p.mm(cx.banks[4][:, 0:1], sel[:, blk, :], lt8[:, NCH:NCH + 1], True, True, [b_sel, b_lt8], [BB[4]])
            p.copy("dve", ob[:, blk:blk + 1], cx.banks[4][:, 0:1], [], [BB[4], b_ob])
        emit_summary(cx, io, [(C[:].rearrange("p b e -> p (b e)"), b_C, 0, 516), (ob[:], [b_ob], 516, 4)], osum, b_outd)
        return cx.done(own, [b_outd])

    p.barrier()
    off = KT * TOK
    xr_bufs = [(carve(2048).bitcast(F32), p.dbuf("xr")) for _ in range(2)]
    y_bufs = [(carve(2048).bitcast(F32), p.buf("y")) for _ in range(2)]
    o_bufs = [(carve(2048).bitcast(F32), p.buf("o")) for _ in range(2)]
    tmp = [carve(2048).bitcast(F32) for _ in range(2)]
    b_tmp = [p.buf("tmp") for _ in range(2)]
    xbf = carve(1024)
    b_xbf = p.buf("xbf")
    st = [p.sbuf(p.uid("st1_"), [128, 16], F32) for _ in range(2)]
    b_st = [p.buf("st1") for _ in range(2)]
    b_x1s = p.dbuf("x1s")
    hb = [BB[0], BB[1], BB[2], BB[3]]
    for t in range(NTILE):
        ts_ = slice(t * 128, (t + 1) * 128)
        hps = [cx.banks[(t % 2) * 2], cx.banks[(t % 2) * 2 + 1]]
        hbb = [hb[(t % 2) * 2], hb[(t % 2) * 2 + 1]]
        for half in range(2):
            for hh in range(8):
                p.mm(hps[half][:], ONT[:, hh, ts_], wo_sb[:, hh, half * 512:(half + 1) * 512], hh == 0, hh == 7, [b_ONT[t], b_wo], [hbb[half]])
        residual_ln1_tile(cx, t, [hps[0][:], hps[1][:]], hbb, x, xr_bufs, y_bufs, o_bufs, tmp, b_tmp, st, b_st,
                          lng[:, 0, :], lnb[:, 0, :], b_gb, x1s, b_x1s, x1T, b_x1T, xbf, b_xbf, None)
    p.barrier()
    ffn_phase(cx, x1T, b_x1T, x1s, b_x1s, wgu, wd, lng[:, 1, :], lnb[:, 1, :], b_gb, y_out, b_outd, arena, KT * TOK)
    return cx.done(own, [b_outd])


NCH8 = TOK // 8
TWO_PI = 2.0 * math.pi
I32 = mybir.dt.int32


def tile_rows_s5(ap, ti):
    half, t = ti // 8, ti % 8
    return ap[half * 1024 + t:half * 1024 + 1024:8, :]


def build_s5(layer, j, stage, cx=None, io=None):
    import os
    own = cx is None
    if own:
        cx = Ctx(bass.Bass("TRN2", target_bir_lowering=False))
    nc = cx.nc
    p = cx.p

    def dt_(name, shape, kind="Internal"):
        if io is not None and name in io:
            return io[name]
        return nc.dram_tensor(name, shape, F32, kind=kind).ap()
    BB = cx.bank_bufs
    full = stage == "B"
    x = dt_("x", [TOK, D], "ExternalInput")
    w_in = dt_("w_in", [D, D], "ExternalInput")
    a_re = dt_("a_re", [32, 128], "ExternalInput")
    a_im = dt_("a_im", [32, 128], "ExternalInput")
    ldt = dt_("ldt", [32, 128], "ExternalInput")
    b_re = dt_("b_re", [64, 64, 16], "ExternalInput")
    b_im = dt_("b_im", [64, 64, 16], "ExternalInput")
    if full:
        c_re = dt_("c_re", [64, 16, 64], "ExternalInput")
        c_im = dt_("c_im", [64, 16, 64], "ExternalInput")
        d_skip = dt_("d_skip", [8, 128], "ExternalInput")
        w_out = dt_("w_out", [D, 2 * D], "ExternalInput")
        ln_g = dt_("ln_g", [2, D], "ExternalInput")
        ln_b = dt_("ln_b", [2, D], "ExternalInput")
        wgu = dt_("wgu", [D, 2 * FFN_H], "ExternalInput")
        wd = dt_("wd", [FFN_H, D], "ExternalInput")
        ssum = dt_("ssum", [NCORES, 128, 64], "ExternalInput")
        cmask = dt_("cmask", [128, NCORES], "ExternalInput")
        y_out = dt_("y", [TOK, D], "ExternalOutput")
        x1s = dt_("x1s", [TOK, D])
    else:
        osum = dt_("osum", [128, 64], "ExternalOutput") if (io is None or "bin" not in io) else None
    b_outd = p.dbuf("outd")
    load_ohot(cx, io, full)

    ARENA = 80 * 1024 + 4608
    arena = p.sbuf("arena", [128, ARENA], BF16)
    R_XT, R_UT, R_HZ, R_HB = 0, 16384, 32768, 65536

    def region(off, n):
        assert off + n <= ARENA
        return arena[:, off:off + n]

    def sl_load(name, src):
        t, b_t = cx.const_load(name + "_l", [32, 128], F32, src)
        o = p.sbuf(name + "_sl", [128, 32], F32)
        b_o = p.buf(name)
        p.tr(cx.banks[7][:, 0:32], t[:], cx.ident_f[0:32, 0:32], [b_t, cx.b_ident], [BB[7]])
        p.copy("dve", o[:], cx.banks[7][:, 0:32], [], [BB[7], b_o])
        return o, b_o

    are, b_are = sl_load("are", a_re)
    aim, b_aim = sl_load("aim", a_im)
    ldt_sl, b_ldt = sl_load("ldt", ldt)
    if full:
        lng, lnb, b_gb = load_ln_consts(cx, ln_g, ln_b)
        cm, b_cm = cx.const_load("cmask_sb", [128, NCORES], F32, cmask)
        dl, b_dl = cx.const_load("d_l", [8, 128], F32, d_skip)
    prm = p.sbuf("prm", [128, 9, 32], F32)
    b_prm = p.buf("prm")
    p.act(prm[:, 0, :], ldt_sl[:], AF.Exp, [b_ldt], [b_prm])
    p.tt("dve", prm[:, 1, :], are[:], prm[:, 0, :], ALU.mult, [b_are, b_prm], [b_prm])
    p.tt("dve", prm[:, 2, :], aim[:], prm[:, 0, :], ALU.mult, [b_aim, b_prm], [b_prm])
    nvec = p.sbuf("nvec", [128, 9], F32)
    b_nv = p.buf("nvec")
    for n in range(9):
        p.memset("pool", nvec[:, n:n + 1], float(n), [b_nv])

    xT = region(R_XT, KT * TOK).rearrange("p (k t) -> p k t", k=KT)
    b_xT = [p.buf(f"xT{t}") for t in range(NTILE)]
    xb = [(region(R_UT + i * 1024, 1024), p.dbuf("xbf")) for i in range(2)]
    load_xT(cx, x, xT, b_xT, xb)
    p.barrier()
    uT = region(R_UT, KT * TOK).rearrange("p (k t) -> p k t", k=KT)
    b_uT = [p.buf(f"uT{b}") for b in range(8)]
    wi_sb = region(R_HZ, 4096).rearrange("p (k c) -> p k c", k=KT)
    b_wi = p.dbuf("wi")
    w_v = w_in.rearrange("(kt p) c -> p kt c", p=128)
    ib = 0
    for hf in range(2):
        p.dma("pool", wi_sb, w_v[:, :, hf * 512:(hf + 1) * 512], writes=[b_wi])
        for bl in range(4):
            blk = hf * 4 + bl
            for tb in range(TOK // 512):
                bk = ib % 4
                ib += 1
                for kt in range(KT):
                    p.mm(cx.banks[bk][:], wi_sb[:, kt, bl * 128:(bl + 1) * 128], xT[:, kt, tb * 512:(tb + 1) * 512], kt == 0, kt == KT - 1,
                         [b_wi] + [b_xT[tb * 4 + jj] for jj in range(4)], [BB[bk]])
                p.copy("act" if ib % 2 == 0 else "dve", uT[:, blk, tb * 512:(tb + 1) * 512], cx.banks[bk][:], [], [BB[bk], b_uT[blk]])
    p.barrier()

    toff = R_HZ

    def tcarve(n):
        nonlocal toff
        v = arena[:, toff:toff + n]
        toff += n
        assert toff <= R_HB
        return v

    b_T = p.buf("preptmp")
    sh9 = [128, 32, 9]
    nth = tcarve(576).bitcast(F32).rearrange("p (j n) -> p j n", n=9)
    nar = tcarve(576).bitcast(F32).rearrange("p (j n) -> p j n", n=9)
    kint = tcarve(576).bitcast(I32).rearrange("p (j n) -> p j n", n=9)
    kf = tcarve(576).bitcast(F32).rearrange("p (j n) -> p j n", n=9)
    sinv = tcarve(576).bitcast(F32).rearrange("p (j n) -> p j n", n=9)
    cosv = tcarve(576).bitcast(F32).rearrange("p (j n) -> p j n", n=9)
    Pre = p.sbuf("Pre", sh9, F32)
    Pim = p.sbuf("Pim", sh9, F32)
    b_P = p.buf("P")
    nv_bc = nvec[:].unsqueeze(1).to_broadcast(sh9)
    p.tt("dve", nth, prm[:, 2, :].unsqueeze(2).to_broadcast(sh9), nv_bc, ALU.mult, [b_prm, b_nv], [b_T])
    p.tt("dve", nar, prm[:, 1, :].unsqueeze(2).to_broadcast(sh9), nv_bc, ALU.mult, [b_prm, b_nv], [b_T])
    p.act(nar, nar, AF.Exp, [b_T], [b_T])
    for which, outv, shift in ((0, sinv, 0.0), (1, cosv, 0.5 * math.pi)):
        src = nth
        if shift != 0.0:
            p.ts("dve", kf, nth, shift, None, ALU.add, None, [b_T], [b_T])
            p.copy("dve", outv, kf, [b_T], [b_T])
            src = outv
        p.ts("dve", kint, src, 1.0 / TWO_PI, None, ALU.mult, None, [b_T], [b_T])
        p.copy("dve", kf, kint, [b_T], [b_T])
        p.stt(outv, kf, -TWO_PI, src, ALU.mult, ALU.add, [b_T], [b_T])
        p.ts("dve", outv, outv, math.pi, -math.pi, ALU.min, ALU.max, [b_T], [b_T])
        p.act(outv, outv, AF.Sin, [b_T], [b_T])
    p.tt("dve", Pre[:], nar, cosv, ALU.mult, [b_T], [b_P])
    p.tt("dve", Pim[:], nar, sinv, ALU.mult, [b_T], [b_P])

    p.ts("dve", prm[:, 6, :], Pre[:, :, 1], -1.0, None, ALU.add, None, [b_P], [b_prm])
    p.tt("dve", prm[:, 7, :], are[:], are[:], ALU.mult, [b_are], [b_prm])
    p.tt("dve", prm[:, 8, :], aim[:], aim[:], ALU.mult, [b_aim], [b_prm])
    p.tt("dve", prm[:, 7, :], prm[:, 7, :], prm[:, 8, :], ALU.add, [b_prm], [b_prm])
    p.add("dve", lambda e: e.reciprocal(prm[:, 3, :], prm[:, 7, :]), [b_prm], [b_prm])
    p.tt("dve", prm[:, 7, :], prm[:, 6, :], are[:], ALU.mult, [b_prm, b_are], [b_prm])
    p.tt("dve", prm[:, 8, :], Pim[:, :, 1], aim[:], ALU.mult, [b_P, b_aim], [b_prm])
    p.tt("dve", prm[:, 7, :], prm[:, 7, :], prm[:, 8, :], ALU.add, [b_prm], [b_prm])
    p.tt("dve", prm[:, 4, :], prm[:, 7, :], prm[:, 3, :], ALU.mult, [b_prm], [b_prm])
    p.tt("dve", prm[:, 7, :], Pim[:, :, 1], are[:], ALU.mult, [b_P, b_are], [b_prm])
    p.tt("dve", prm[:, 8, :], prm[:, 6, :], aim[:], ALU.mult, [b_prm, b_aim], [b_prm])
    p.tt("dve", prm[:, 7, :], prm[:, 7, :], prm[:, 8, :], ALU.subtract, [b_prm], [b_prm])
    p.tt("dve", prm[:, 5, :], prm[:, 7, :], prm[:, 3, :], ALU.mult, [b_prm], [b_prm])
    sh16 = [128, 32, 16]

    def f16():
        return tcarve(1024).bitcast(F32).rearrange("p (j c) -> p j c", c=16)

    bbr, bbi = f16(), f16()
    b_bb = p.dbuf("bb")
    for g2 in range(2):
        p.dma("sp", bbr[64 * g2:64 * g2 + 64], b_re.rearrange("(j g2) p c -> g2 p j c", g2=2)[g2], writes=[b_bb])
        p.dma("sp", bbi[64 * g2:64 * g2 + 64], b_im.rearrange("(j g2) p c -> g2 p j c", g2=2)[g2], writes=[b_bb])
    bre, bim = f16(), f16()
    t1, t2 = f16(), f16()
    b_bar = p.buf("bbar")
    cre_bc = prm[:, 4, :].unsqueeze(2).to_broadcast(sh16)
    cim_bc = prm[:, 5, :].unsqueeze(2).to_broadcast(sh16)
    p.tt("dve", t1, bbr, cre_bc, ALU.mult, [b_bb, b_prm], [b_T])
    p.tt("dve", t2, bbi, cim_bc, ALU.mult, [b_bb, b_prm], [b_T])
    p.tt("dve", bre, t1, t2, ALU.subtract, [b_T], [b_bar])
    p.tt("dve", t1, bbi, cre_bc, ALU.mult, [b_bb, b_prm], [b_T])
    p.tt("dve", t2, bbr, cim_bc, ALU.mult, [b_bb, b_prm], [b_T])
    p.tt("dve", bim, t1, t2, ALU.add, [b_T], [b_bar])

    WZ = region(R_HB, 16384).rearrange("p (b s r m) -> p b s r m", b=8, s=8, r=2)
    b_WZ = p.buf("WZ")
    epads = []
    for i in range(2):
        ep = tcarve(2048)
        b_ep = p.buf(f"epad{i}")
        p.memset("dve", ep, 0.0, [b_ep])
        epads.append((ep, b_ep))
    t3, t4 = f16(), f16()
    for s in range(8):
        n = 7 - s
        pr_bc = Pre[:, :, n].unsqueeze(2).to_broadcast(sh16)
        pi_bc = Pim[:, :, n].unsqueeze(2).to_broadcast(sh16)
        ep, b_ep = epads[s % 2]
        p.tt("dve", t1, bre, pr_bc, ALU.mult, [b_bar, b_P], [b_T])
        p.tt("dve", t2, bim, pi_bc, ALU.mult, [b_bar, b_P], [b_T])
        p.tt("dve", t3, bim, pr_bc, ALU.mult, [b_bar, b_P], [b_T])
        p.tt("dve", t4, bre, pi_bc, ALU.mult, [b_bar, b_P], [b_T])
        for half in range(2):
            rows = slice(64 * half, 64 * half + 64)
            for ri, (ta, tb_, op) in enumerate(((t1, t2, ALU.subtract), (t3, t4, ALU.add))):
                dst = ep.rearrange("p (b r q h c) -> p b r q h c", b=8, r=2, q=4, h=2)[rows, :, ri, :, half, :]
                p.tt("dve", dst, ta[rows].rearrange("p (b q) c -> p b q c", q=4), tb_[rows].rearrange("p (b q) c -> p b q c", q=4), op, [b_T], [b_ep])
        epv = ep.rearrange("p (b r m) -> p b r m", b=8, r=2)
        for b in range(8):
            bk = b % 4
            psb = cx.banks[bk][:].bitcast(BF16)
            for ri in range(2):
                p.tr(psb[:, ri * 128:(ri + 1) * 128], epv[:, b, ri, :], cx.ident_bf[:], [b_ep, cx.b_ident], [BB[bk]])
            p.copy("act" if b % 2 == 0 else "dve", WZ[:, b, s, :, :], psb[:, 0:256].rearrange("p (r m) -> p r m", r=2), [], [BB[bk], b_WZ])

    ArAr = p.sbuf("ArAr", [128, 2, 32], F32)
    AiN = p.sbuf("AiN", [128, 2, 32], F32)
    b_A8 = p.buf("A8")
    for r in range(2):
        p.copy("dve", ArAr[:, r, :], Pre[:, :, 8], [b_P], [b_A8])
    p.ts("dve", AiN[:, 0, :], Pim[:, :, 8], -1.0, None, ALU.mult, None, [b_P], [b_A8])
    p.copy("dve", AiN[:, 1, :], Pim[:, :, 8], [b_P], [b_A8])
    hin = p.sbuf("hin", [128, 2, 32], F32)
    b_hin = p.buf("hin")
    p.memset("pool", hin[:], 0.0, [b_hin])

    if full:
        Ct = [f16(), f16()]
        b_Ct = p.buf("Ct")
        xc_bufs = [(tcarve(256).bitcast(F32), p.dbuf("xc")) for _ in range(2)]
        ii = 0
        for ri, csrc in enumerate((c_re, c_im)):
            for tb4 in range(4):
                xc, b_xc = xc_bufs[ii % 2]
                ii += 1
                for jl in range(8):
                    jj = tb4 * 8 + jl
                    p.dma("sp", xc[jl * 16:(jl + 1) * 16, :].rearrange("c (g p) -> c g p", g=2),
                          csrc[2 * jj:2 * jj + 2].rearrange("g c p -> c g p"), writes=[b_xc])
                bk = 4 + (ii % 2)
                p.tr(cx.banks[bk][:, 0:128], xc, cx.ident_f[:], [b_xc, cx.b_ident], [BB[bk]])
                p.copy("dve", Ct[ri][:, tb4 * 8:(tb4 + 1) * 8, :], cx.banks[bk][:, 0:128].rearrange("p (j c) -> p j c", c=16), [], [BB[bk], b_Ct])
        if int(os.environ.get("S5_PSTOP", "99")) <= 1:
            cx.finish([b_outd])
            return nc
        G = region(R_XT, 9216).rearrange("p (j r n c) -> p j r n c", j=32, r=2, n=9)
        b_G = p.buf("G")
        for n in range(9):
            pr_bc = Pre[:, :, n].unsqueeze(2).to_broadcast(sh16)
            pi_bc = Pim[:, :, n].unsqueeze(2).to_broadcast(sh16)
            p.tt("dve", t1, Ct[0], pr_bc, ALU.mult, [b_Ct, b_P], [b_T])
            p.tt("dve", t2, Ct[1], pi_bc, ALU.mult, [b_Ct, b_P], [b_T])
            p.tt("dve", G[:, :, 0, n, :], t1, t2, ALU.subtract, [b_T], [b_G])
            p.tt("dve", t3, Ct[0], pi_bc, ALU.mult, [b_Ct, b_P], [b_T])
            p.tt("dve", t4, Ct[1], pr_bc, ALU.mult, [b_Ct, b_P], [b_T])
            p.stt(G[:, :, 1, n, :], t3, -1.0, t4, ALU.mult, ALU.subtract, [b_T], [b_G])
        if int(os.environ.get("S5_PSTOP", "99")) <= 2:
            cx.finish([b_outd])
            return nc
        WT = region(R_XT + 9216, 3840).rearrange("p (b g w) -> p b g w", b=8, g=2)
        b_WT = p.buf("WT")
        p.memset("dve", WT, 0.0, [b_WT])
        bbpad = tcarve(8192)
        b_bbp = p.buf("bbpad")
        p.memset("dve", bbpad, 0.0, [b_bbp])
        for ri, bsrc in enumerate((bre, bim)):
            for half in range(2):
                rows = slice(64 * half, 64 * half + 64)
                base = bbpad[rows, :]
                dst = bass.AP(base.tensor, base.offset + ri * 128 + 16 * half, [list(base.ap[0]), [1024, 8], [256 + 32, 4], [1, 16]])
                p.copy("dve", dst, bsrc[rows].rearrange("p (b q) c -> p b q c", q=4), [b_bar], [b_bbp])
        bbp = bbpad.rearrange("p (b q r m) -> p b q r m", b=8, q=4, r=2)
        mask16 = p.sbuf("mask16", [128, 16], F32)
        rowm = p.sbuf("rowm", [128, 2], F32)
        b_mk = p.buf("mk")
        p.add("dve", lambda e: e.tensor_reduce(out=mask16[:], in_=cx.ident_f[:].rearrange("p (k c) -> p c k", c=16), axis=AX.X, op=ALU.add), [cx.b_ident], [b_mk])
        for g2 in range(2):
            p.add("dve", lambda e, g2=g2: e.tensor_reduce(out=rowm[:, g2:g2 + 1], in_=cx.ident_f[:].rearrange("p (k h c) -> p h k c", h=2, c=16)[:, g2],
                                                          axis=AX.XY, op=ALU.add), [cx.b_ident], [b_mk])
        p.tr(cx.banks[7][:, 0:8], dl[:], cx.ident_f[0:8, 0:8], [b_dl, cx.b_ident], [BB[7]])
        dmk = p.sbuf("dmk", [128, 2, 8], F32)
        for g2 in range(2):
            p.ts("dve", dmk[:, g2, :], cx.banks[7][:, 0:8], rowm[:, g2:g2 + 1], None, ALU.mult, None, [b_mk], [BB[7], b_mk])
        if int(os.environ.get("S5_PSTOP", "99")) <= 3:
            cx.finish([b_outd])
            return nc
        for b in range(8):
            for g2 in range(2):
                bk = 4 + ((b * 2 + g2) % 2)
                rows = slice(64 * g2, 64 * g2 + 64)
                kps = cx.banks[bk][:, 0:128]
                i = 0
                for q in range(4):
                    for ri in range(2):
                        p.mm(kps, bbp[rows, b, q, ri, :], G[rows, 4 * b + q, ri, 0:8, :], i == 0, i == 7, [b_bbp, b_G], [BB[bk]])
                        i += 1
                p.stt(WT[:, b, g2, 112:128], mask16[:], dmk[:, g2, b:b + 1], kps[:, 0:16], ALU.mult, ALU.add, [b_mk], [BB[bk], b_WT])
                p.copy("act", WT[:, b, g2, 128:240], kps[:, 16:128], [], [BB[bk], b_WT])
        if int(os.environ.get("S5_PSTOP", "99")) <= 4:
            cx.finish([b_outd])
            return nc
        At = p.sbuf("Atot", [128, 2, 32], F32)
        tq = p.sbuf("tq", [128, 4, 32], F32)
        b_At = p.buf("At")
        p.copy("dve", At[:, 0, :], Pre[:, :, 8], [b_P], [b_At])
        p.copy("dve", At[:, 1, :], Pim[:, :, 8], [b_P], [b_At])
        for _ in range(8):
            p.tt("dve", tq[:, 0, :], At[:, 0, :], At[:, 0, :], ALU.mult, [b_At], [b_At])
            p.tt("dve", tq[:, 1, :], At[:, 1, :], At[:, 1, :], ALU.mult, [b_At], [b_At])
            p.tt("dve", tq[:, 2, :], At[:, 0, :], At[:, 1, :], ALU.mult, [b_At], [b_At])
            p.tt("dve", At[:, 0, :], tq[:, 0, :], tq[:, 1, :], ALU.subtract, [b_At], [b_At])
            p.ts("dve", At[:, 1, :], tq[:, 2, :], 2.0, None, ALU.mult, None, [b_At], [b_At])
        hn = p.sbuf("hnew", [128, 2, 32], F32)
        rdb = [io["b_bout"]] if (io is not None and "b_bout" in io) else []
        sHt = p.sbuf("sH_sb", [128, NCORES, 2, 32], F32)
        b_sHt = p.dbuf("sHt")
        p.dma("sp", sHt[:], ssum.rearrange("c p (r j) -> p c r j", r=2), reads=rdb, writes=[b_sHt])
        for c in range(NCORES):
            p.tt("dve", tq[:, 0, :], At[:, 0, :], hin[:, 0, :], ALU.mult, [b_At, b_hin], [b_At])
            p.tt("dve", tq[:, 1, :], At[:, 1, :], hin[:, 1, :], ALU.mult, [b_At, b_hin], [b_At])
            p.tt("dve", tq[:, 2, :], At[:, 0, :], hin[:, 1, :], ALU.mult, [b_At, b_hin], [b_At])
            p.tt("dve", tq[:, 3, :], At[:, 1, :], hin[:, 0, :], ALU.mult, [b_At, b_hin], [b_At])
            p.tt("dve", hn[:, 0, :], tq[:, 0, :], tq[:, 1, :], ALU.subtract, [b_At], [b_At])
            p.tt("dve", hn[:, 1, :], tq[:, 2, :], tq[:, 3, :], ALU.add, [b_At], [b_At])
            p.tt("dve", hn[:], hn[:], sHt[:, c], ALU.add, [b_At, b_sHt], [b_At])
            p.tt("dve", hn[:], hn[:], hin[:], ALU.subtract, [b_At, b_hin], [b_At])
            p.stt(hin[:], hn[:], cm[:, c:c + 1], hin[:], ALU.mult, ALU.add, [b_At, b_cm, b_hin], [b_hin])

    if full and int(os.environ.get("S5_STOP", "99")) <= 1:
        return cx.done(own, [b_outd])
    HZ = region(R_HZ, 32768).bitcast(F32).rearrange("p (r j k) -> p r j k", r=2, j=32)
    b_HZ = p.buf("HZ")
    ib = 0
    for jt in range(32):
        b, q = jt // 4, jt % 4
        rows = slice(32 * q, 32 * q + 32)
        for ri in range(2):
            bk = ib % 8
            ib += 1
            zps = cx.banks[bk][:, 0:NCH8]
            for s in range(8):
                p.mm(zps, WZ[rows, b, s, ri, :], uT[rows, b, s:TOK:8], s == 0, s == 7, [b_WZ, b_uT[b]], [BB[bk]], tp=(32 * q, 0))
            p.copy("act" if ib % 2 == 0 else "dve", HZ[:, ri, jt, :], zps, [], [BB[bk], b_HZ])
    p.barrier()

    AA4 = p.sbuf("AA4", [128, 2, 2, 32], F32)
    b_AA4 = p.buf("AA4")
    for r in range(2):
        p.copy("dve", AA4[:, 0, r, :], ArAr[:, r, :], [b_A8], [b_AA4])
    p.ts("dve", AA4[:, 1, 0, :], AiN[:, 0, :], -1.0, None, ALU.mult, None, [b_A8], [b_AA4])
    p.copy("dve", AA4[:, 1, 1, :], AiN[:, 0, :], [b_A8], [b_AA4])
    sx_ = p.sbuf("scanX", [128, 2, 2, 32], F32)
    st_ = p.sbuf("scanT", [128, 2, 32], F32)
    b_sc = p.buf("scan")
    xb_ = sx_[:, 1, 1, :]
    x1rev = bass.AP(xb_.tensor, xb_.offset, [list(xb_.ap[0]), [-32, 2], [1, 32]])
    NSTEP = int(os.environ.get("S5_NSTEP", str(NCH8)))
    for k in range(NSTEP):
        if k == 0:
            base = hin[:, 0, :]
            prev4 = bass.AP(base.tensor, base.offset, [list(base.ap[0]), [0, 2], [32, 2], [1, 32]])
        else:
            base = HZ[:, 0, :, k - 1]
            prev4 = bass.AP(base.tensor, base.offset, [list(base.ap[0]), [0, 2], [32 * NCH8, 2], [NCH8, 32]])
        p.tt("dve", sx_[:], prev4, AA4[:], ALU.mult, [b_HZ, b_hin, b_AA4], [b_sc])
        p.tt("dve", st_[:], sx_[:, 0, :, :], x1rev, ALU.add, [b_sc], [b_sc])
        p.tt("dve", HZ[:, :, :, k], HZ[:, :, :, k], st_[:], ALU.add, [b_sc], [b_HZ])
    if not full:
        fin = p.sbuf("fin", [128, 2, 32], F32)
        b_fin = p.buf("fin")
        p.copy("dve", fin[:], HZ[:, :, :, NCH8 - 1], [b_HZ], [b_fin])
        emit_summary(cx, io, [(fin[:].rearrange("p r j -> p (r j)"), [b_fin], 0, 64)], osum, b_outd)
        return cx.done(own, [b_outd])
    if full and int(os.environ.get("S5_STOP", "99")) <= 3:
        return cx.done(own, [b_outd])
    p.barrier()
    Hbf = region(R_HB, 16384).rearrange("p (r j k) -> p r j k", r=2, j=32)
    b_Hbf = p.buf("Hbf")
    p.copy("dve", Hbf[:, :, :, 0], hin[:], [b_hin], [b_Hbf])
    for r in range(2):
        p.copy("act" if r == 0 else "dve", Hbf[:, r, :, 1:NCH8], HZ[:, r, :, 0:NCH8 - 1], [b_HZ], [b_Hbf])
    p.barrier()

    if full and int(os.environ.get("S5_STOP", "99")) <= 4:
        return cx.done(own, [b_outd])
    y8b = region(R_HZ, 16384).rearrange("p (h t c) -> p h t c", h=2, t=8)
    b_y8 = [p.buf(f"y8_{h}") for h in range(2)]
    W4 = R_XT + 13056
    it_bufs = [(region(W4 + i * 512, 512).bitcast(F32), p.buf("itmp")) for i in range(2)]
    ys_bufs = [(region(W4 + 1024 + i * 512, 512).bitcast(F32), p.buf("ysum")) for i in range(2)]
    g1_bufs = [(region(W4 + 2048 + i * 512, 512).bitcast(F32), p.buf("g1")) for i in range(2)]
    GC = 2.0 * math.sqrt(2.0 / math.pi)
    ib = 0
    for half in range(2):
        for jt in range(32):
            b, q = jt // 4, jt % 4
            rows = slice(32 * q, 32 * q + 32)
            tbk = ib % 2
            i0, i1 = 2 + (ib % 2) * 2, 3 + (ib % 2) * 2
            tps = cx.banks[tbk][:, 0:256]
            for g2 in range(2):
                for s in range(8):
                    c0 = half * 1024 + s
                    p.mm(tps[:, g2 * 128:(g2 + 1) * 128], uT[rows, b, c0:(half + 1) * 1024:8], WT[rows, b, g2, (7 - s) * 16:(7 - s) * 16 + 128], s == 0, s == 7,
                         [b_uT[b], b_WT], [BB[tbk]], tp=(32 * q, 0))
            for g2 in range(2):
                ibk = i0 if g2 == 0 else i1
                hr = slice(64 * g2, 64 * g2 + 64)
                for ri in range(2):
                    p.mm(cx.banks[ibk][:, 0:128], Hbf[hr, ri, jt, half * 128:(half + 1) * 128], G[hr, jt, ri, 1:9, :], ri == 0, ri == 1, [b_Hbf, b_G], [BB[ibk]])
            it, b_it = it_bufs[ib % 2]
            ys, b_ys = ys_bufs[ib % 2]
            g1, b_g1 = g1_bufs[ib % 2]
            p.copy("act", it[:, 0:128], cx.banks[i0][:, 0:128], [], [BB[i0], b_it])
            p.copy("act", it[:, 128:256], cx.banks[i1][:, 0:128], [], [BB[i1], b_it])
            p.tt("dve", ys, tps, it, ALU.add, [b_it], [BB[tbk], b_ys])
            p.act(g1, ys, AF.Square, [b_ys], [b_g1])
            p.ts("dve", g1, g1, 0.044715, 1.0, ALU.mult, ALU.add, [b_g1], [b_g1])
            p.tt("dve", g1, g1, ys, ALU.mult, [b_g1, b_ys], [b_g1])
            p.act(g1, g1, AF.Sigmoid, [b_g1], [b_g1], scale=GC)
            dst = y8b[:, half, :, jt * 32:(jt + 1) * 32].rearrange("p t (g c) -> p g t c", g=2)
            p.tt("dve", dst, g1.rearrange("p (g t c) -> p g t c", g=2, t=8), ys.rearrange("p (g t c) -> p g t c", g=2, t=8), ALU.mult, [b_g1, b_ys], [b_y8[half]])
            ib += 1
    p.barrier()

    if full and int(os.environ.get("S5_STOP", "99")) <= 5:
        return cx.done(own, [b_outd])
    yT = region(R_HZ + 16384, KT * TOK).rearrange("p (k t) -> p k t", k=KT)
    b_yT = [p.buf(f"yT{t}") for t in range(NTILE)]
    wo_sb = region(R_UT, 16384).rearrange("p (k c) -> p k c", k=KT)
    b_wo = p.dbuf("wo")
    wo_v = w_out.rearrange("(kt p) c -> p kt c", p=128)
    for qd in range(4):
        p.dma("pool", wo_sb[:, :, qd * 512:(qd + 1) * 512], wo_v[:, :, qd * 512:(qd + 1) * 512], writes=[b_wo])
    for ti in range(NTILE):
        half, t = ti // 8, ti % 8
        bk = 6 + (ti % 2)
        psb = cx.banks[bk][:].bitcast(BF16)
        for blk in range(8):
            p.tr(psb[:, blk * 128:(blk + 1) * 128], y8b[:, half, t, blk * 128:(blk + 1) * 128], cx.ident_bf[:], [b_y8[half], cx.b_ident], [BB[bk]])
        p.copy("act" if ti % 2 == 0 else "dve", yT[:, :, ti * 128:(ti + 1) * 128], psb.rearrange("p (k t) -> p k t", k=KT), [], [BB[bk], b_yT[ti]])
    off = R_HB
    def carve(n):
        nonlocal off
        v = arena[:, off:off + n]
        off += n
        assert off <= ARENA
        return v
    xr_bufs = [(carve(2048).bitcast(F32), p.dbuf("xr")) for _ in range(2)]
    y_bufs = [(carve(2048).bitcast(F32), p.buf("y")) for _ in range(2)]
    o_bufs = [(carve(2048).bitcast(F32), p.buf("o")) for _ in range(2)]
    tmp = carve(2048).bitcast(F32)
    b_tmp = p.buf("tmp")
    xbf = carve(1024)
    b_xbf = p.buf("xbf")
    hs = carve(2048).bitcast(F32)
    b_hs = p.buf("hs")
    sg = carve(2048).bitcast(F32)
    b_sg = p.buf("sg")
    st = p.sbuf("st1", [128, 16], F32)
    b_st = p.buf("st1")
    b_x1s = p.dbuf("x1s")
    x1T = region(R_XT, KT * TOK).rearrange("p (k t) -> p k t", k=KT)
    b_x1T = [p.buf(f"x1T{t}") for t in range(NTILE)]
    for ti in range(NTILE):
        ts_ = slice(ti * 128, (ti + 1) * 128)
        for qd in range(4):
            for blk in range(8):
                p.mm(cx.banks[qd][:], yT[:, blk, ts_], wo_sb[:, blk, qd * 512:(qd + 1) * 512], blk == 0, blk == 7, [b_yT[ti], b_wo], [BB[qd]])
        for hh in range(2):
            p.act(sg[:, hh * 512:(hh + 1) * 512], cx.banks[2 + hh][:], AF.Sigmoid, [], [BB[2 + hh], b_sg])
            p.tt("dve", hs[:, hh * 512:(hh + 1) * 512], sg[:, hh * 512:(hh + 1) * 512], cx.banks[hh][:], ALU.mult, [b_sg], [BB[hh], b_hs])
        residual_ln1_tile(cx, ti, [hs[:, 0:512], hs[:, 512:1024]], [b_hs, b_hs], x, xr_bufs, y_bufs, o_bufs, tmp, b_tmp, st, b_st,
                          lng[:, 0, :], lnb[:, 0, :], b_gb, x1s, b_x1s, x1T, b_x1T, xbf, b_xbf, None, row_fn=tile_rows_s5)
    p.barrier()
    ffn_phase(cx, x1T, b_x1T, x1s, b_x1s, wgu, wd, lng[:, 1, :], lnb[:, 1, :], b_gb, y_out, b_outd, arena, KT * TOK, row_fn=tile_rows_s5)
    return cx.done(own, [b_outd])


POOL_UNITS = 106400
SUMW = {0: 1032, 1: 520, 2: 64}


def halo_exchange(cx, x_in, io):
    p = cx.p
    oh, b_oh = cx.const_load("ohot_h", [128, NCORES], F32, io["ohot"])
    ph, b_ph = cx.const_load("phot_h", [128, NCORES], F32, io["phot"])
    last = p.sbuf("hl_last", [3, D], F32)
    b_last = p.dbuf("hl_last")
    p.dma("sp", last[:], x_in[TOK - 3:TOK, :], writes=[b_last])
    b_hin = p.dbuf("hl_in")
    tmps = [(p.sbuf(p.uid("hlt"), [3, D], F32), p.buf("hlt")) for _ in range(2)]
    hin3 = io["hin2d"].rearrange("(c r) d -> c r d", r=3)
    for r in range(NCORES):
        t, b_t = tmps[r % 2]
        p.ts("dve", t[:], last[:], oh[0:3, r:r + 1], None, ALU.mult, None, [b_last, b_oh], [b_t])
        p.dma("sp", hin3[r], t[:], reads=[b_t], writes=[b_hin])
    b_hout = p.dbuf("hl_out", unit=1)
    in2d, out2d = io["hin2d"], io["hout2d"]
    p.coll(lambda e: e.collective_compute("AllReduce", ALU.add, replica_groups=[list(range(NCORES))], ins=[in2d.opt()], outs=[out2d.opt()]),
           [b_hin], [b_hout])
    g = p.sbuf("hl_g", [3, NCORES, D], F32)
    b_g = p.dbuf("hl_g")
    p.dma("sp", g[:], io["hout2d"].rearrange("(c r) d -> r c d", r=3), reads=[b_hout], writes=[b_g])
    acc = p.sbuf("hl_acc", [3, D], F32)
    b_acc = p.buf("hl_acc")
    p.memset("dve", acc[:], 0.0, [b_acc])
    for r in range(NCORES):
        p.stt(acc[:], g[:, r, :], ph[0:3, r:r + 1], acc[:], ALU.mult, ALU.add, [b_g, b_ph, b_acc], [b_acc])
    b_xh = p.dbuf("xh_d")
    p.dma("sp", io["xh"], acc[:], reads=[b_acc], writes=[b_xh])


def build_fused():
    nc = bass.Bass("TRN2", target_bir_lowering=False)
    cx = Ctx(nc, pool_units=POOL_UNITS)
    cx.mark_persistent()
    p = cx.p

    def ein(name, shape):
        return nc.dram_tensor(name, shape, F32, kind="ExternalInput").ap()

    x = ein("x", [TOK, D])
    hg_w_in = ein("hgrn_w_in", [2, D, 4 * D])
    hg_nw = ein("hgrn_norm_w", [2, 128])
    hg_wo = ein("hgrn_w_out", [2, D, D])
    hg_lb = ein("hgrn_lb_logits", [4, D])
    ml_w_in = ein("mlstm_w_in", [1, D, ML_W])
    ml_cw = ein("mlstm_conv_w", [1, 4, D])
    ml_gb = ein("mlstm_gate_b", [16, 1])
    ml_nw = ein("mlstm_norm_w", [1, D])
    ml_wo = ein("mlstm_w_out", [1, D, D])
    s5_wi = ein("s5_w_in", [1, D, D])
    s5_are = ein("s5_a_re", [32, 128])
    s5_aim = ein("s5_a_im", [32, 128])
    s5_ldt = ein("s5_ldt", [32, 128])
    s5_bre = ein("s5_b_re", [64, 64, 16])
    s5_bim = ein("s5_b_im", [64, 64, 16])
    s5_cre = ein("s5_c_re", [64, 16, 64])
    s5_cim = ein("s5_c_im", [64, 16, 64])
    s5_d = ein("s5_d", [8, 128])
    s5_wo = ein("s5_w_out", [1, D, 2 * D])
    wgu = ein("ffn_w_gate_up", [DEPTH, D, 2 * FFN_H])
    wd = ein("ffn_w_down", [DEPTH, FFN_H, D])
    ln_g = ein("ln_g", [DEPTH, 2, D])
    ln_b = ein("ln_b", [DEPTH, 2, D])
    cmask = ein("cmask", [128, NCORES])
    ohot = ein("ohot", [128, NCORES])
    phot = ein("phot", [128, NCORES])
    y = nc.dram_tensor("y", [TOK, D], F32, kind="ExternalOutput").ap()
    xbuf = [nc.dram_tensor(f"xbuf{i}", [TOK, D], F32).ap() for i in range(2)]
    x1s = nc.dram_tensor("x1s", [TOK, D], F32).ap()
    xh = nc.dram_tensor("xh_scr", [3, D], F32).ap()
    hin2d = nc.dram_tensor("halo_in", [NCORES * 3, D], F32).ap()
    hout2d = nc.dram_tensor("halo_out", [NCORES * 3, D], F32).ap()

    first = True
    x_in = x
    for i in range(DEPTH):
        kind, j = i % 3, i // 3
        W = SUMW[kind]
        bin2d = nc.dram_tensor(f"bin{i}", [NCORES * 128, W], F32).ap()
        bout2d = nc.dram_tensor(f"bout{i}", [NCORES * 128, W], F32).ap()
        bin3 = bin2d.rearrange("(c p) w -> c p w", p=128)
        bout3 = bout2d.rearrange("(c p) w -> c p w", p=128)
        y_dst = y if i == DEPTH - 1 else xbuf[i % 2]
        ffn = {"ln_g": ln_g[i], "ln_b": ln_b[i], "wgu": wgu[i], "wd": wd[i], "x1s": x1s, "cmask": cmask, "ssum": bout3, "y": y_dst}
        if kind == 0:
            base = {"x": x_in, "w_in": hg_w_in[j], "lb_logits": hg_lb}
            extra = {"norm_w": hg_nw[j:j + 1, :], "w_out": hg_wo[j]}
            fn = build_hgrn
        elif kind == 1:
            if not first:
                cx.new_stage()
            first = False
            halo_exchange(cx, x_in, {"ohot": ohot, "phot": phot, "hin2d": hin2d, "hout2d": hout2d, "xh": xh})
            base = {"x": x_in, "xh": xh, "w_in": ml_w_in[j], "conv_w": ml_cw[j], "gate_b": ml_gb}
            extra = {"norm_w": ml_nw[j:j + 1, :], "w_out": ml_wo[j]}
            fn = build_mlstm
        else:
            base = {"x": x_in, "w_in": s5_wi[j], "a_re": s5_are, "a_im": s5_aim, "ldt": s5_ldt, "b_re": s5_bre, "b_im": s5_bim}
            extra = {"c_re": s5_cre, "c_im": s5_cim, "d_skip": s5_d, "w_out": s5_wo[j]}
            fn = build_s5
        if not first:
            cx.new_stage()
        first = False
        b_bout = p.dbuf("bout", unit=1)
        fn(i, j, "A", cx=cx, io=dict(base, ohot=ohot, bin=bin3, bin2d=bin2d, bout2d=bout2d, b_bout=b_bout))
        cx.new_stage()
        fn(i, j, "B", cx=cx, io=dict(base, **extra, **ffn, b_bout=b_bout))
        x_in = y_dst
    p.barrier()
    cx.finish(cx.pending_out)
    return nc


_NC_CACHE = {}


def _masks(c):
    cm = np.zeros((128, NCORES), np.float32)
    cm[:, :c] = 1.0
    oh = np.zeros((128, NCORES), np.float32)
    oh[:, c] = 1.0
    ph = np.zeros((128, NCORES), np.float32)
    if c > 0:
        ph[:, c - 1] = 1.0
    return cm, oh, ph


def kernel(x, hgrn_w_in, hgrn_norm_w, hgrn_w_out, hgrn_lb_logits, mlstm_w_in, mlstm_conv_w, mlstm_gate_b, mlstm_norm_w,
           mlstm_w_out, s5_w_in, s5_a_re, s5_a_im, s5_log_dt, s5_b_re, s5_b_im, s5_c_re, s5_c_im, s5_d, s5_w_out,
           ffn_w_gate_up, ffn_w_down, ln_g, ln_b):
    f = lambda a: np.ascontiguousarray(np.asarray(a, dtype=np.float32))
    if "fused" not in _NC_CACHE:
        _NC_CACHE["fused"] = build_fused()
    nc = _NC_CACHE["fused"]
    ldt = np.ascontiguousarray(np.broadcast_to(f(s5_log_dt)[0].reshape(32, 2, 1), (32, 2, 64)).reshape(32, 128))
    shared = {
        "hgrn_w_in": f(hgrn_w_in), "hgrn_norm_w": f(hgrn_norm_w), "hgrn_w_out": f(hgrn_w_out), "hgrn_lb_logits": f(hgrn_lb_logits),
        "mlstm_w_in": f(mlstm_w_in), "mlstm_conv_w": f(mlstm_conv_w), "mlstm_gate_b": f(mlstm_gate_b).reshape(16, 1),
        "mlstm_norm_w": f(mlstm_norm_w), "mlstm_w_out": f(mlstm_w_out),
        "s5_w_in": f(s5_w_in), "s5_a_re": f(s5_a_re).reshape(32, 128), "s5_a_im": f(s5_a_im).reshape(32, 128), "s5_ldt": ldt,
        "s5_b_re": f(s5_b_re)[0], "s5_b_im": f(s5_b_im)[0], "s5_c_re": f(s5_c_re)[0], "s5_c_im": f(s5_c_im)[0],
        "s5_d": f(s5_d).reshape(8, 128), "s5_w_out": f(s5_w_out),
        "ffn_w_gate_up": f(ffn_w_gate_up), "ffn_w_down": f(ffn_w_down), "ln_g": f(ln_g), "ln_b": f(ln_b),
    }
    xf = f(x)[0]
    in_maps = []
    for c in range(NCORES):
        cm, oh, ph = _masks(c)
        in_maps.append(dict(shared, x=np.ascontiguousarray(xf[c * TOK:(c + 1) * TOK]), cmask=cm, ohot=oh, phot=ph))
    res = run_bass_kernel_spmd(nc, in_maps, core_ids=list(range(NCORES)))
    return np.concatenate([res.results[c]["y"] for c in range(NCORES)], axis=0)[None].astype(np.float32)
```

```python
import contextlib
import math
import numpy as np
import concourse.bass as bass
import concourse.mybir as mybir
from concourse.bass_utils import run_bass_kernel_spmd

F32 = mybir.dt.float32
BF16 = mybir.dt.bfloat16
AF = mybir.ActivationFunctionType
ALU = mybir.AluOpType
AX = mybir.AxisListType

NCORES = 8
SEQ = 16384
TOK = SEQ // NCORES
D = 1024
KT = 8
NTILE = TOK // 128
DEPTH = 4
ALPHA = (2.0 * DEPTH) ** 0.25
LN_EPS = 1e-5
HN_EPS = 1e-6
FFN_H = 2816
NFT = FFN_H // 128

ENGS = ("pe", "act", "dve", "pool", "sp")


class Buf:
    __slots__ = ("name", "last_w", "readers", "grp")

    def __init__(self, name, grp=None):
        self.name = name
        self.last_w = None
        self.readers = []
        self.grp = grp


class DmaGroup:
    __slots__ = ("sem", "count", "final", "last", "unit", "frozen", "kind", "base")

    def __init__(self, sem, final=False, start=0, unit=16):
        self.sem = sem
        self.count = start
        self.final = final
        self.last = None
        self.unit = unit
        self.frozen = None
        self.kind = None
        self.base = 0


class Op:
    __slots__ = ("eng", "fn", "deps", "is_dma", "grp", "grp_count", "sig", "has_dep", "epoch")

    def __init__(self, eng, fn, is_dma=False):
        self.epoch = 0
        self.eng = eng
        self.fn = fn
        self.deps = []
        self.is_dma = is_dma
        self.grp = None
        self.grp_count = 0
        self.sig = None
        self.has_dep = False


class Prog:
    def __init__(self, nc):
        self.nc = nc
        self.ops = {e: [] for e in ENGS}
        self.stack = contextlib.ExitStack()
        self.nsem = 0
        self.eng_sem = {}
        self.groups = []
        self._uid = 0
        self.free_sems = {}
        self.live_groups = []
        self.epoch = 0
        self.pool_t = None
        self.pool_off = 0
        self.pool_mark = 0
        self.pool_size = 0

    def sem(self, name=None):
        self.nsem += 1
        return self.stack.enter_context(self.nc.semaphore(name or f"s{self.nsem}"))

    def uid(self, pfx):
        self._uid += 1
        return f"{pfx}{self._uid}"

    def sbuf(self, name, shape, dt):
        if self.pool_t is None:
            return self.stack.enter_context(self.nc.sbuf_tensor(name, list(shape), dt))
        n = 1
        for d_ in shape[1:]:
            n *= d_
        units = (n * mybir.dt.size(dt) + 1) // 2
        units = (units + 7) // 8 * 8
        assert self.pool_off + units <= self.pool_size, ("sbuf pool overflow", name, self.pool_off, units, self.pool_size)
        v = self.pool_t[0:shape[0], self.pool_off:self.pool_off + units]
        self.pool_off += units
        if dt != BF16:
            v = v.bitcast(dt)
        v = v[:, 0:n]
        if len(shape) > 2:
            names = [f"d{i}" for i in range(len(shape) - 1)]
            v = v.rearrange("p (" + " ".join(names) + ") -> p " + " ".join(names), **{nm: shape[i + 1] for i, nm in enumerate(names[:-1])})
        return v

    def psum(self, name, shape, dt):
        return self.stack.enter_context(self.nc.psum_tensor(name, list(shape), dt))

    def buf(self, name="b"):
        return Buf(name)

    def group(self, final=False, unit=16):
        g = DmaGroup(None, final, 0, unit)
        self.groups.append(g)
        self.live_groups.append(g)
        return g

    def _bind(self, g, kind):
        if g.sem is not None:
            assert g.kind == kind, ("DMA group mixes queues", g.kind, kind)
            return
        g.kind = kind
        fl = self.free_sems.setdefault(kind, [])
        if fl:
            g.sem, g.base = fl.pop()
        else:
            g.sem, g.base = self.sem(), 0

    def new_stage(self):
        self.barrier()
        for g in self.live_groups:
            if g.unit == 1:
                continue
            g.frozen = g.count
            if g.sem is not None:
                self.free_sems.setdefault(g.kind, []).append((g.sem, g.base + g.count))
        self.live_groups = []
        self.epoch += 1
        self.pool_off = self.pool_mark

    def coll(self, fn, reads, writes):
        op = Op("pool", fn, is_dma=True)
        g = None
        for b in writes:
            if b.grp is not None:
                g = b.grp
                break
        assert g is not None and g.unit == 1
        self._bind(g, "cc")
        g.count += 1
        g.last = op
        op.grp = g
        op.grp_count = g.count
        op.epoch = self.epoch
        self._track(op, reads, writes)
        self.ops["pool"].append(op)
        return op

    def dbuf(self, name="d", final=False, grp=None, unit=16):
        if grp is None:
            grp = self.group(final, unit)
        return Buf(name, grp)

    def _track(self, op, reads, writes):
        for b in reads:
            if b.last_w is not None:
                op.deps.append((b.last_w, True))
        for b in writes:
            if b.last_w is not None:
                op.deps.append((b.last_w, False))
            for r in b.readers:
                if r is not op:
                    op.deps.append((r, False))
        for b in reads:
            b.readers.append(op)
        for b in writes:
            b.last_w = op
            b.readers = []

    def add(self, eng, fn, reads=(), writes=()):
        op = Op(eng, fn)
        op.epoch = self.epoch
        self._track(op, reads, writes)
        self.ops[eng].append(op)
        return op

    def dma(self, eng, out, in_, reads=(), writes=(), grp=None, **kw):
        op = Op(eng, lambda e: e.dma_start(out=out, in_=in_, **kw), is_dma=True)
        g = grp
        if g is None:
            for b in writes:
                if b.grp is not None:
                    g = b.grp
                    break
        assert g is not None, "dma needs a group"
        self._bind(g, "sw" if eng == "pool" else "hw")
        g.count += 1
        g.last = op
        op.grp = g
        op.grp_count = g.count
        op.epoch = self.epoch
        self._track(op, reads, writes)
        self.ops[eng].append(op)
        return op

    def barrier(self):
        lasts = []
        for e in ENGS:
            for op in reversed(self.ops[e]):
                if not op.is_dma:
                    lasts.append(op)
                    break
        for g in self.live_groups:
            if g.last is not None and g.unit != 1:
                lasts.append(g.last)
        for e in ENGS:
            op = Op(e, lambda eng: eng.nop())
            op.epoch = self.epoch
            op.deps = [(o, True) for o in lasts]
            self.ops[e].append(op)

    def mm(self, out, lhsT, rhs, start, stop, reads, writes, tp=None):
        if tp is not None:
            return self.add("pe", lambda e: e.matmul(out, lhsT=lhsT, rhs=rhs, start=start, stop=stop, tile_position=tp), reads, writes)
        return self.add("pe", lambda e: e.matmul(out, lhsT=lhsT, rhs=rhs, start=start, stop=stop), reads, writes)

    def tr(self, out, in_, ident, reads, writes):
        return self.add("pe", lambda e: e.transpose(out, in_, ident), reads, writes)

    def act(self, out, in_, func, reads, writes, scale=1.0, bias=0.0, accum=None):
        if accum is None:
            return self.add("act", lambda e: e.activation(out=out, in_=in_, func=func, bias=bias, scale=scale), reads, writes)
        return self.add("act", lambda e: e.activation(out=out, in_=in_, func=func, bias=bias, scale=scale, accum_out=accum), reads, writes)

    def tt(self, eng, out, in0, in1, op, reads, writes):
        return self.add(eng, lambda e: e.tensor_tensor(out=out, in0=in0, in1=in1, op=op), reads, writes)

    def ts(self, eng, out, in0, s1, s2, op0, op1, reads, writes):
        if s2 is None:
            return self.add(eng, lambda e: e.tensor_scalar(out=out, in0=in0, scalar1=s1, scalar2=None, op0=op0), reads, writes)
        return self.add(eng, lambda e: e.tensor_scalar(out=out, in0=in0, scalar1=s1, scalar2=s2, op0=op0, op1=op1), reads, writes)

    def stt(self, out, in0, scalar, in1, op0, op1, reads, writes):
        return self.add("dve", lambda e: e.scalar_tensor_tensor(out=out, in0=in0, scalar=scalar, in1=in1, op0=op0, op1=op1), reads, writes)

    def copy(self, eng, out, in_, reads, writes):
        if eng == "act":
            return self.add("act", lambda e: e.copy(out, in_), reads, writes)
        return self.add(eng, lambda e: e.tensor_copy(out, in_), reads, writes)

    def memset(self, eng, ap, val, writes):
        return self.add(eng, lambda e: e.memset(ap, val), (), writes)

    def emit(self):
        nc = self.nc
        for e in ENGS:
            for op in self.ops[e]:
                for d, raw in op.deps:
                    if d.eng == e and not d.is_dma and not raw and e != "pool":
                        continue
                    d.has_dep = True
        for e in ENGS:
            cnts = {}
            for op in self.ops[e]:
                if op.is_dma:
                    continue
                if op.has_dep:
                    cnts[op.epoch] = cnts.get(op.epoch, 0) + 1
                    op.sig = cnts[op.epoch]
            for ep in cnts:
                self.eng_sem[(e, ep)] = self.sem(f"eng_{e}_{ep}")
        block = self.stack.enter_context(nc.Block())

        def emit_engine(e, eng_obj):
            waited = {}
            for op in self.ops[e]:
                need = {}
                for d, raw in op.deps:
                    if d.is_dma:
                        s = d.grp.sem
                        fin = d.grp.frozen if d.grp.frozen is not None else d.grp.count
                        v = d.grp.unit * (d.grp.base + (fin if d.grp.final else d.grp_count))
                    else:
                        if d.eng == e and (e == "pe" or (not raw and e != "pool")):
                            continue
                        s = self.eng_sem[(d.eng, d.epoch)]
                        v = d.sig
                    k = id(s)
                    if k not in need or need[k][1] < v:
                        need[k] = (s, v)
                for k, (s, v) in need.items():
                    if waited.get(k, 0) >= v:
                        continue
                    waited[k] = v
                    eng_obj.wait_ge(s, v)
                ins = op.fn(eng_obj)
                if op.is_dma:
                    ins.then_inc(op.grp.sem, op.grp.unit)
                elif op.has_dep:
                    ins.then_inc(self.eng_sem[(e, op.epoch)], 1)

        @block.sync
        def _(eng):
            emit_engine("sp", eng)

        @block.scalar
        def _(eng):
            emit_engine("act", eng)

        @block.vector
        def _(eng):
            emit_engine("dve", eng)

        @block.gpsimd
        def _(eng):
            emit_engine("pool", eng)

        @block.tensor
        def _(eng):
            emit_engine("pe", eng)

        self.stack.close()


class Rot:
    def __init__(self, slots):
        self.slots = slots
        self.i = 0

    def next(self):
        s = self.slots[self.i % len(self.slots)]
        self.i += 1
        return s


class Ctx:
    def __init__(self, nc, pool_units=None):
        self.nc = nc
        self.p = Prog(nc)
        p = self.p
        self.fused = pool_units is not None
        if pool_units is not None:
            t = p.stack.enter_context(nc.sbuf_tensor("pool", [128, pool_units], BF16))
            p.pool_t = t[:]
            p.pool_size = pool_units
        self.pending_out = []
        self.ln_ctr = 0
        self.banks = [p.psum(f"bank{i}", [128, 512], F32) for i in range(8)]
        self.bank_bufs = [p.buf(f"bankbuf{i}") for i in range(8)]
        self.cgrp = p.group(final=True)
        self.eps_ln = p.sbuf("eps_ln", [128, 1], F32)
        self.eps_hn = p.sbuf("eps_hn", [128, 1], F32)
        self.b_eps = p.buf("eps")
        p.memset("pool", self.eps_ln[:], LN_EPS, [self.b_eps])
        p.memset("pool", self.eps_hn[:], 128.0 * HN_EPS, [self.b_eps])
        self.ident_bf = p.sbuf("ident_bf", [128, 128], BF16)
        self.ident_f = p.sbuf("ident_f", [128, 128], F32)
        self.b_ident = p.buf("ident")
        for t in (self.ident_bf, self.ident_f):
            p.memset("pool", t[:], 0.0, [self.b_ident])
            p.add("pool", lambda e, t=t: e.affine_select(out=t[:], in_=t[:], pattern=[[-1, 128]], compare_op=ALU.not_equal,
                                                         fill=1.0, base=0, channel_multiplier=1), [self.b_ident], [self.b_ident])

    def mark_persistent(self):
        self.p.pool_mark = self.p.pool_off

    def new_stage(self):
        self.p.new_stage()
        self.cgrp = self.p.group(final=True)

    def done(self, own, out_bufs):
        if own:
            self.finish(out_bufs)
            return self.nc
        self.pending_out += list(out_bufs)
        return None

    def const_load(self, name, shape, dt, src, eng="sp"):
        p = self.p
        t = p.sbuf(name, shape, dt)
        b = Buf(name, self.cgrp)
        p.dma(eng, t[:], src, writes=[b])
        return t, b

    def finish(self, out_bufs):
        self.p.add("sp", lambda e: e.nop(), reads=out_bufs)
        self.p.emit()


def load_xT(cx, x_dram, xT, b_xT, xb, ntile=NTILE, banks=(6, 7)):
    p = cx.p
    for t in range(ntile):
        bank = banks[t % len(banks)]
        ps = cx.banks[bank][:].bitcast(BF16)
        b_ps = cx.bank_bufs[bank]
        xt, bx = xb[t % 2]
        p.dma("pool", xt, x_dram[t * 128:(t + 1) * 128, :], writes=[bx])
        for kt in range(KT):
            p.tr(ps[:, kt * 128:(kt + 1) * 128], xt[:, kt * 128:(kt + 1) * 128], cx.ident_bf[:], [bx, cx.b_ident], [b_ps])
        eng = "act" if t % 2 == 0 else "dve"
        p.copy(eng, xT[:, :, t * 128:(t + 1) * 128], ps.rearrange("p (k t) -> p k t", k=KT), [], [b_ps, b_xT[t]])


def layer_norm_tile(cx, y, b_y, gam, bet, b_gb, out_f, b_out, tmp, b_tmp, st, b_st):
    p = cx.p
    if isinstance(tmp, list):
        i = cx.ln_ctr % len(tmp)
        cx.ln_ctr += 1
        tmp, b_tmp, st, b_st = tmp[i], b_tmp[i], st[i], b_st[i]
    p.add("dve", lambda e: e.bn_stats(st[:, 0:6], y[:, 0:512]), [b_y], [b_st])
    p.add("dve", lambda e: e.bn_stats(st[:, 6:12], y[:, 512:1024]), [b_y], [b_st])
    p.add("dve", lambda e: e.bn_aggr(st[:, 12:14], st[:, 0:12].rearrange("p (a b) -> p a b", a=2)), [b_st], [b_st])
    p.act(st[:, 15:16], st[:, 13:14], AF.Sqrt, [b_st, cx.b_eps], [b_st], bias=cx.eps_ln[:, 0:1])
    p.add("dve", lambda e: e.reciprocal(st[:, 14:15], st[:, 15:16]), [b_st], [b_st])
    p.stt(st[:, 15:16], st[:, 12:13], -1.0, st[:, 14:15], ALU.mult, ALU.mult, [b_st], [b_st])
    p.act(tmp[:], y[:], AF.Identity, [b_y, b_st], [b_tmp], scale=st[:, 14:15], bias=st[:, 15:16])
    p.tt("dve", tmp[:], tmp[:], gam, ALU.mult, [b_tmp, b_gb], [b_tmp])
    p.tt("dve", out_f, tmp[:], bet, ALU.add, [b_tmp, b_gb], [b_out])


def transpose_tile_to_xT(cx, src_f, b_src, xT, b_xT_t, t, xbf, b_xbf, bank=7, b_ps=None):
    p = cx.p
    p.copy("act", xbf[:], src_f, [b_src], [b_xbf])
    ps = cx.banks[bank][:].bitcast(BF16)
    b_ps = cx.bank_bufs[bank]
    for kt in range(KT):
        p.tr(ps[:, kt * 128:(kt + 1) * 128], xbf[:, kt * 128:(kt + 1) * 128], cx.ident_bf[:], [b_xbf, cx.b_ident], [b_ps])
    p.copy("dve", xT[:, :, t * 128:(t + 1) * 128], ps.rearrange("p (k t) -> p k t", k=KT), [], [b_ps, b_xT_t])


def ffn_phase(cx, x1T, b_x1T, x1s_dram, b_x1s, wgu, wd, g2, be2, b_gb, out_dram, b_outd, arena_bf, arena_off, row_fn=None):
    p = cx.p
    G = 1024
    off = arena_off

    def carve(n_bf16):
        nonlocal off
        v = arena_bf[:, off:off + n_bf16]
        off += n_bf16
        return v

    hT = carve(NFT * G).rearrange("p (f t) -> p f t", f=NFT)
    b_hT = [p.buf(f"hT{f}") for f in range(NFT)]
    wg_bufs = [(carve(KT * 512).rearrange("p (k c) -> p k c", k=KT), p.dbuf("wg")) for _ in range(2)]
    wu_bufs = [(carve(KT * 512).rearrange("p (k c) -> p k c", k=KT), p.dbuf("wu")) for _ in range(2)]
    wd_bufs = [(carve(2048).rearrange("p (f c) -> p f c", f=2), p.dbuf("wd")) for _ in range(2)]
    sg_bufs = [(carve(1024).bitcast(F32), p.buf("sgt")) for _ in range(2)]
    xr_bufs = [(carve(2048).bitcast(F32), p.dbuf("xr")) for _ in range(4)]
    y_bufs = [(carve(2048).bitcast(F32), p.buf("y")) for _ in range(4)]
    tmp = [carve(2048).bitcast(F32) for _ in range(2)]
    b_tmp = [p.buf("lntmp") for _ in range(2)]
    o_bufs = [(carve(2048).bitcast(F32), p.buf("o")) for _ in range(2)]
    st = [p.sbuf(p.uid("stf"), [128, 16], F32) for _ in range(2)]
    b_st = [p.buf("st") for _ in range(2)]
    wgu_v = wgu.rearrange("(kt p) c -> p kt c", p=128)
    bank_bufs = cx.bank_bufs
    nquad = (NFT + 3) // 4
    for g in range(TOK // G):
        ib = 0
        for fq in range(nquad):
            nf = min(4, NFT - fq * 4)
            wgt, b_wg = wg_bufs[fq % 2]
            wut, b_wu = wu_bufs[fq % 2]
            c0 = fq * 512
            p.dma("pool", wgt[:, :, 0:nf * 128], wgu_v[:, :, c0:c0 + nf * 128], writes=[b_wg])
            p.dma("pool", wut[:, :, 0:nf * 128], wgu_v[:, :, FFN_H + c0:FFN_H + c0 + nf * 128], writes=[b_wu])
            for fi in range(nf):
                f = fq * 4 + fi
                for tb in range(G // 512):
                    tok0 = g * G + tb * 512
                    rd = [b_x1T[(tok0 // 128) + j] for j in range(4)]
                    bg = ib % 4
                    ib += 1
                    gps, b_gps = cx.banks[2 * bg], bank_bufs[2 * bg]
                    ups, b_ups = cx.banks[2 * bg + 1], bank_bufs[2 * bg + 1]
                    for kt in range(KT):
                        p.mm(gps[:], wgt[:, kt, fi * 128:(fi + 1) * 128], x1T[:, kt, tok0:tok0 + 512], kt == 0, kt == KT - 1, [b_wg] + rd, [b_gps])
                    for kt in range(KT):
                        p.mm(ups[:], wut[:, kt, fi * 128:(fi + 1) * 128], x1T[:, kt, tok0:tok0 + 512], kt == 0, kt == KT - 1, [b_wu] + rd, [b_ups])
                    sgt, b_sgt = sg_bufs[ib % 2]
                    p.act(sgt, gps[:], AF.Silu, [], [b_gps, b_sgt])
                    p.tt("dve", hT[:, f, tb * 512:(tb + 1) * 512], sgt, ups[:], ALU.mult, [b_sgt], [b_ups, b_hT[f]])
        for tq in range(G // 512):
            for f in range(NFT):
                wd2, b_wd = wd_bufs[(f // 2) % 2]
                if f % 2 == 0:
                    p.dma("pool", wd2, wd[f * 128:(f + 2) * 128, :].rearrange("(f p) c -> p f c", p=128), writes=[b_wd])
                wdt = wd2[:, f % 2, :]
                for ti in range(4):
                    tl = tq * 4 + ti
                    for half in range(2):
                        bk = ti * 2 + half
                        p.mm(cx.banks[bk][:], hT[:, f, tl * 128:(tl + 1) * 128], wdt[:, half * 512:(half + 1) * 512], f == 0, f == NFT - 1,
                             [b_hT[f], b_wd], [bank_bufs[bk]])
            for ti in range(4):
                t = g * (G // 128) + tq * 4 + ti
                xr, b_xr = xr_bufs[t % 4]
                p.dma("sp", xr, x1s_dram[t * 128:(t + 1) * 128, :], reads=[b_x1s], writes=[b_xr])
                y, b_y = y_bufs[t % 4]
                for half in range(2):
                    bk = ti * 2 + half
                    p.stt(y[:, half * 512:(half + 1) * 512], xr[:, half * 512:(half + 1) * 512], ALPHA, cx.banks[bk][:], ALU.mult, ALU.add,
                          [b_xr], [bank_bufs[bk], b_y])
            for ti in range(4):
                t = g * (G // 128) + tq * 4 + ti
                y, b_y = y_bufs[t % 4]
                o, b_o = o_bufs[t % 2]
                layer_norm_tile(cx, y, b_y, g2, be2, b_gb, o, b_o, tmp, b_tmp, st, b_st)
                p.dma("sp", (out_dram[t * 128:(t + 1) * 128, :] if row_fn is None else row_fn(out_dram, t)), o, reads=[b_o], writes=[b_outd])
    return off


def residual_ln1_tile(cx, t, h_aps, h_bufs, x_dram, xr_bufs, y_bufs, o_bufs, tmp, b_tmp, st, b_st, g1, be1, b_gb,
                      x1s_dram, b_x1s, x1T, b_x1T, xbf, b_xbf, b_pstr, row_fn=None):
    p = cx.p
    xr, b_xr = xr_bufs[t % 2]
    p.dma("sp", xr, (x_dram[t * 128:(t + 1) * 128, :] if row_fn is None else row_fn(x_dram, t)), writes=[b_xr])
    y, b_y = y_bufs[t % 2]
    for half in range(2):
        p.stt(y[:, half * 512:(half + 1) * 512], xr[:, half * 512:(half + 1) * 512], ALPHA, h_aps[half], ALU.mult, ALU.add,
              [b_xr], [h_bufs[half], b_y])
    o, b_o = o_bufs[t % 2]
    layer_norm_tile(cx, y, b_y, g1, be1, b_gb, o, b_o, tmp, b_tmp, st, b_st)
    p.dma("sp", x1s_dram[t * 128:(t + 1) * 128, :], o, reads=[b_o], writes=[b_x1s])
    transpose_tile_to_xT(cx, o, b_o, x1T, b_x1T[t], t, xbf, b_xbf, bank=7, b_ps=b_pstr)


def load_ln_consts(cx, ln_g, ln_b):
    g, bg = cx.const_load("ln_g_bc", [128, 2, D], F32, ln_g.partition_broadcast(128))
    b, bb = cx.const_load("ln_b_bc", [128, 2, D], F32, ln_b.partition_broadcast(128))
    return g, b, bg


def emit_summary(cx, io, pieces, osum, b_outd):
    p = cx.p
    if io is None or "bin" not in io:
        for src, bufs, col0, w in pieces:
            p.dma("sp", osum[:, col0:col0 + w], src, reads=list(bufs), writes=[b_outd])
        return
    oh, b_oh = io["ohot_sb"]
    bin_ = io["bin"]
    b_bin = p.dbuf("bin")
    for src, bufs, col0, w in pieces:
        tmps = [(p.sbuf(p.uid("pub"), [128, w], F32), p.buf("pub")) for _ in range(2)]
        for r in range(NCORES):
            t, b_t = tmps[r % 2]
            if r % 2 and w > 64:
                p.act(t[:], src, AF.Identity, list(bufs) + [b_oh], [b_t], scale=oh[:, r:r + 1])
            else:
                p.ts("dve", t[:], src, oh[:, r:r + 1], None, ALU.mult, None, list(bufs) + [b_oh], [b_t])
            p.dma("sp", bin_[r, :, col0:col0 + w], t[:], reads=[b_t], writes=[b_bin])
    in2d, out2d = io["bin2d"], io["bout2d"]
    p.coll(lambda e: e.collective_compute("AllReduce", ALU.add, replica_groups=[list(range(NCORES))], ins=[in2d.opt()], outs=[out2d.opt()]),
           [b_bin], [io["b_bout"]])


def load_ohot(cx, io, full):
    if not full and io is not None and "ohot" in io:
        io["ohot_sb"] = cx.const_load("ohot_sb", [128, NCORES], F32, io["ohot"])

def hgrn_lb_consts(cx, lb_logits, layer):
    p = cx.p
    lg, b_lg = cx.const_load("lb_lg", [32, 128], F32, lb_logits.rearrange("l (h d) -> (l h) d", d=128))
    ps = cx.banks[7]
    b_ps = cx.bank_bufs[7]
    p.tr(ps[:, 0:32], lg[:], cx.ident_f[0:32, 0:32], [b_lg, cx.b_ident], [b_ps])
    lt = p.sbuf("lb_lt", [128, 4, 8], F32)
    b_lt = p.buf("lb_lt")
    p.copy("dve", lt[:], ps[:, 0:32].rearrange("p (l h) -> p l h", l=4), [], [b_ps, b_lt])
    mx = p.sbuf("lb_mx", [128, 8], F32)
    p.tt("dve", mx[:], lt[:, 0, :], lt[:, 1, :], ALU.max, [b_lt], [b_lt])
    p.tt("dve", mx[:], mx[:], lt[:, 2, :], ALU.max, [b_lt], [b_lt])
    p.tt("dve", mx[:], mx[:], lt[:, 3, :], ALU.max, [b_lt], [b_lt])
    ex = p.sbuf("lb_ex", [128, 4, 8], F32)
    p.tt("dve", ex[:], lt[:], mx[:].unsqueeze(1).to_broadcast([128, 4, 8]), ALU.subtract, [b_lt], [b_lt])
    p.act(ex[:], ex[:], AF.Exp, [b_lt], [b_lt])
    sm = p.sbuf("lb_sm", [128, 8], F32)
    p.tt("dve", sm[:], ex[:, 0, :], ex[:, 1, :], ALU.add, [b_lt], [b_lt])
    p.tt("dve", sm[:], sm[:], ex[:, 2, :], ALU.add, [b_lt], [b_lt])
    p.tt("dve", sm[:], sm[:], ex[:, 3, :], ALU.add, [b_lt], [b_lt])
    num = p.sbuf("lb_num", [128, 8], F32)
    p.memset("dve", num[:], 0.0, [b_lt])
    for j in range(1, layer + 1):
        p.tt("dve", num[:], num[:], ex[:, j, :], ALU.add, [b_lt], [b_lt])
    lbc = p.sbuf("lbc", [128, 3, 8], F32)
    b_lbc = p.buf("lbc")
    p.add("dve", lambda e: e.reciprocal(sm[:], sm[:]), [b_lt], [b_lt])
    p.tt("dve", lbc[:, 0, :], num[:], sm[:], ALU.mult, [b_lt], [b_lbc])
    p.ts("dve", lbc[:, 1, :], lbc[:, 0, :], -1.0, 1.0, ALU.mult, ALU.add, [b_lbc], [b_lbc])
    p.ts("dve", lbc[:, 2, :], lbc[:, 0, :], -1.0, None, ALU.add, None, [b_lbc], [b_lbc])
    return lbc, b_lbc


def build_hgrn(layer, j, stage, cx=None, io=None):
    own = cx is None
    if own:
        cx = Ctx(bass.Bass("TRN2", target_bir_lowering=False))
    nc = cx.nc
    p = cx.p

    def dt_(name, shape, kind="Internal"):
        if io is not None and name in io:
            return io[name]
        return nc.dram_tensor(name, shape, F32, kind=kind).ap()
    full = stage == "B"
    x = dt_("x", [TOK, D], "ExternalInput")
    w_in = dt_("w_in", [D, 4 * D], "ExternalInput")
    lb_logits = dt_("lb_logits", [4, D], "ExternalInput")
    if full:
        norm_w = dt_("norm_w", [1, 128], "ExternalInput")
        w_out = dt_("w_out", [D, D], "ExternalInput")
        ln_g = dt_("ln_g", [2, D], "ExternalInput")
        ln_b = dt_("ln_b", [2, D], "ExternalInput")
        wgu = dt_("wgu", [D, 2 * FFN_H], "ExternalInput")
        wd = dt_("wd", [FFN_H, D], "ExternalInput")
        ssum = dt_("ssum", [NCORES, 128, 1032], "ExternalInput")
        cmask = dt_("cmask", [128, NCORES], "ExternalInput")
        y_out = dt_("y", [TOK, D], "ExternalOutput")
        x1s = dt_("x1s", [TOK, D])
    else:
        osum = dt_("osum", [128, 1032], "ExternalOutput") if (io is None or "bin" not in io) else None
    b_outd = p.dbuf("outd")

    load_ohot(cx, io, full)
    lbc, b_lbc = hgrn_lb_consts(cx, lb_logits, layer)
    cmk = p.sbuf("chunkmask", [128, 512], F32)
    b_cmk = p.buf("cmk")
    p.memset("pool", cmk[:], 1.0, [b_cmk])
    p.memset("pool", cmk[:, 0:512:64], 0.0, [b_cmk])
    if full:
        ones64 = p.sbuf("ones64", [64, 64], F32)
        maskT = p.sbuf("maskT", [64, 64], F32)
        b_mask = p.buf("mask")
        p.memset("pool", ones64[:], 1.0, [b_mask])
        p.add("pool", lambda e: e.affine_select(out=maskT[:], in_=ones64[:], pattern=[[1, 64]], compare_op=ALU.is_ge, fill=0.0,
                                                base=0, channel_multiplier=-1), [b_mask], [b_mask])
        nwb, b_nwb = cx.const_load("nw_bc", [64, 128], F32, norm_w.broadcast_to([64, 128]))
        nws = p.sbuf("nws", [64, 128], F32)
        b_nws = p.buf("nws")
        p.ts("dve", nws[:], nwb[:], math.sqrt(128.0), None, ALU.mult, None, [b_nwb], [b_nws])
        lng, lnb, b_gb = load_ln_consts(cx, ln_g, ln_b)
        cm, b_cm = cx.const_load("cmask_sb", [128, NCORES], F32, cmask)

    ARENA = 88 * 1024
    arena = p.sbuf("arena", [128, ARENA], BF16)
    off = 0

    def carve(n):
        nonlocal off
        v = arena[:, off:off + n]
        off += n
        assert off <= ARENA, off
        return v

    b_x1T = [p.buf(f"x1T{t}") for t in range(NTILE)]
    xT = carve(KT * TOK).rearrange("p (k t) -> p k t", k=KT)
    x1T = xT
    b_xT = [p.buf(f"xT{t}") for t in range(NTILE)]
    xb = [(carve(D), p.dbuf("xbf")) for _ in range(2)]
    load_xT(cx, x, xT, b_xT, xb)

    S = p.sbuf("S", [128, 8, 128], F32)
    b_S = [p.buf(f"S{h}") for h in range(8)]
    for h in range(8):
        p.memset("dve", S[:, h, :], 0.0, [b_S[h]])
    rdb = [io["b_bout"]] if (io is not None and "b_bout" in io) else []

    def combine():
        sBt = p.sbuf("sB_sb", [128, NCORES, 8], F32)
        b_sBt = p.dbuf("sBt")
        p.dma("sp", sBt[:], ssum[:, :, 1024:1032].rearrange("c d h -> d c h"), reads=rdb, writes=[b_sBt])
        ea = p.sbuf("ea", [128, NCORES, 8], F32)
        b_ea = p.buf("ea")
        p.act(ea[:], sBt[:], AF.Exp, [b_sBt], [b_ea])
        p.ts("dve", ea[:], ea[:], -1.0, None, ALU.add, None, [b_ea], [b_ea])
        p.tt("dve", ea[:], ea[:], cm[:].unsqueeze(2).to_broadcast([128, NCORES, 8]), ALU.mult, [b_ea, b_cm], [b_ea])
        p.ts("dve", ea[:], ea[:], 1.0, None, ALU.add, None, [b_ea], [b_ea])
        for c in range(NCORES):
            sc, b_scc = sc_bufs[c % 2]
            p.dma("sp", sc, ssum[c, :, 0:1024].rearrange("d (h e) -> d h e", h=8), reads=rdb, writes=[b_scc])
            p.ts("dve", sc, sc, cm[:, c:c + 1], None, ALU.mult, None, [b_scc, b_cm], [b_scc])
            for hh in range(8):
                p.stt(S[:, hh, :], S[:, hh, :], ea[:, c, hh:hh + 1], sc[:, hh, :], ALU.mult, ALU.add, [b_S[hh], b_ea, b_scc], [b_S[hh]])

    if full:
        sc_bufs = [(carve(2048).bitcast(F32).rearrange("p (h e) -> p h e", h=8), p.dbuf("sSc")) for _ in range(2)]

    wh_bufs = [(carve(KT * 512).rearrange("p (k c) -> p k c", k=KT), p.dbuf("wh")) for _ in range(2)]
    qT = carve(TOK)
    kT = carve(TOK)
    b_qT = p.buf("qT")
    b_kT = p.buf("kT")
    bcum = carve(2 * TOK).bitcast(F32)
    b_bc = p.buf("bcum")
    t1s = [(carve(1024).bitcast(F32), p.buf("t1")) for _ in range(1)]
    t2s = [(carve(1024).bitcast(F32), p.buf("t2")) for _ in range(1)]
    t3s = [(carve(1024).bitcast(F32), p.buf("t3")) for _ in range(1)]
    t4s = [(carve(1024).bitcast(F32), p.buf("t4")) for _ in range(1)]
    t5s = [(carve(1024).bitcast(F32), p.buf("t5")) for _ in range(1)]
    t6s = [(carve(1024).bitcast(F32), p.buf("t6")) for _ in range(1)]
    NCH = TOK // 64
    sc1 = p.sbuf("sc1", [128, NCH], F32)
    sc2 = p.sbuf("sc2", [128, NCH], F32)
    sc3 = p.sbuf("sc3", [128, NCH], F32)
    b_sc = p.buf("sc")
    v_ch = carve(NCH * 128).rearrange("p (c e) -> p c e", c=NCH)
    b_vch = p.buf("vch")
    if full:
        sgw = carve(NCH * 128).rearrange("p (c e) -> p c e", c=NCH)
        b_sgw = p.buf("sgw")
        ONT = carve(KT * TOK).rearrange("p (h t) -> p h t", h=8)
        b_ONT = [p.buf(f"ONT{t}") for t in range(NTILE)]
        wo_sb = carve(KT * D).rearrange("p (k c) -> p k c", k=KT)
        b_wo = p.dbuf("wo")
        p.dma("pool", wo_sb, w_out.rearrange("(kt p) c -> p kt c", p=128), writes=[b_wo])
    sgt_bufs = [(carve(256).bitcast(F32), p.buf("sgt")) for _ in range(2)]
    ktok_bufs = [(carve(128), p.buf("ktok")) for _ in range(2)]
    attm_bufs = [(carve(64), p.buf("attm")) for _ in range(2)]
    Sbf_bufs = [(carve(128), p.buf("Sbf")) for _ in range(2)]
    kvt_bufs = [(carve(256).bitcast(F32), p.buf("kvt")) for _ in range(2)]
    on_bufs = [(carve(128), p.buf("on")) for _ in range(4)]
    ss_l = [p.sbuf(p.uid("ss"), [64, 4], F32) for _ in range(2)]
    b_ss_l = [p.buf("ss") for _ in range(2)]
    junk_l = [carve(256).bitcast(F32) for _ in range(2)]
    b_junk_l = [p.buf("junk") for _ in range(2)]
    btot = p.sbuf("btot", [128, 8], F32)
    b_btot = p.buf("btot")

    BB = cx.bank_bufs
    bq = [(cx.banks[0], BB[0]), (cx.banks[2], BB[2])]
    bf_ = [(cx.banks[1], BB[1]), (cx.banks[3], BB[3])]
    vg_slots = [(cx.banks[4][0:64, 0:256], BB[4]), (cx.banks[5][0:64, 0:256], BB[5])]
    ktr_slots = [(cx.banks[0][:].bitcast(BF16)[0:64, 0:128], BB[0]), (cx.banks[1][:].bitcast(BF16)[0:64, 0:128], BB[1])]
    kv_slots = [(cx.banks[0][:, 0:128], BB[0]), (cx.banks[1][:, 0:128], BB[1])]
    att_slots = [(cx.banks[2][0:64, 0:64], BB[2]), (cx.banks[3][0:64, 0:64], BB[3])]
    o_slots = [(cx.banks[4][0:64, 0:128], BB[4]), (cx.banks[5][0:64, 0:128], BB[5])]
    ontr_slots = [(cx.banks[6][:].bitcast(BF16)[:, 0:64], BB[6]), (cx.banks[7][:].bitcast(BF16)[:, 0:64], BB[7])]

    w_in_v = w_in.rearrange("(kt p) (s h j) -> p kt s h j", p=128, s=4, h=8)
    import os
    if not full and os.environ.get("HG_SLOWA") is None:
        kkf = carve(2 * TOK).bitcast(F32)
        b_kk = p.buf("kkf")
        kdT = carve(TOK)
        b_kd = p.buf("kdT")
        vtok = carve(NTILE * 128).rearrange("p (t e) -> p t e", t=NTILE)
        b_vt = p.buf("vtok")
        ktk = carve(NTILE * 128).rearrange("p (t e) -> p t e", t=NTILE)
        b_ktk = p.buf("ktk")
        lf_bufs = [(carve(1024).bitcast(F32), p.buf("lf")) for _ in range(2)]
        sg_bufs2 = [(carve(1024).bitcast(F32), p.buf("sg2")) for _ in range(2)]
        ones512 = p.sbuf("ones512", [128, 512], F32)
        b_on5 = p.buf("ones512")
        p.memset("pool", ones512[:], 1.0, [b_on5])
        for h in range(8):
            wh, b_wh = wh_bufs[h % 2]
            for s_ in (1, 2):
                p.dma("pool", wh[:, :, s_ * 128:(s_ + 1) * 128], w_in_v[:, :, s_, h, :], writes=[b_wh])
            lb_h, oml_h, noml_h = lbc[:, 0, h:h + 1], lbc[:, 1, h:h + 1], lbc[:, 2, h:h + 1]
            for tb in range(TOK // 512):
                tok0 = tb * 512
                rd = [b_xT[tok0 // 128 + jj] for jj in range(4)]
                fps, b_fps = bf_[tb % 2]
                for kt in range(KT):
                    p.mm(fps[:], wh[:, kt, 128:256], xT[:, kt, tok0:tok0 + 512], kt == 0, kt == KT - 1, [b_wh] + rd, [b_fps])
                sg_, b_sg_ = sg_bufs2[tb % 2]
                lf_, b_lf_ = lf_bufs[tb % 2]
                p.act(sg_, fps[:], AF.Sigmoid, [], [b_fps, b_sg_])
                p.act(lf_, sg_, AF.Ln, [b_sg_, b_lbc], [b_lf_], scale=oml_h, bias=lb_h)
                p.ts("dve", kkf[:, tok0:tok0 + 512], sg_, 1.0, noml_h, ALU.subtract, ALU.mult, [b_sg_, b_lbc], [b_kk])
                init = 0.0 if tb == 0 else bcum[:, tok0 - 1:tok0]
                p.add("dve", lambda e, tok0=tok0, lf_=lf_, init=init: e.tensor_tensor_scan(out=bcum[:, tok0:tok0 + 512], data0=ones512[:], data1=lf_, initial=init,
                                                                                        op0=ALU.mult, op1=ALU.add), [b_on5, b_lf_, b_bc], [b_bc])
            p.copy("dve", btot[:, h:h + 1], bcum[:, TOK - 1:TOK], [b_bc], [b_btot])
            for tb in range(TOK // 512):
                tok0 = tb * 512
                lf_, b_lf_ = lf_bufs[tb % 2]
                p.act(lf_, bcum[:, tok0:tok0 + 512], AF.Exp, [b_bc, b_btot], [b_lf_], scale=-1.0, bias=btot[:, h:h + 1])
                p.tt("dve", kdT[:, tok0:tok0 + 512], kkf[:, tok0:tok0 + 512], lf_, ALU.mult, [b_kk, b_lf_], [b_kd])
            for t in range(NTILE):
                bk = 4 + (t % 2)
                vps = cx.banks[bk][:, 0:128]
                for kt in range(KT):
                    p.mm(vps, xT[:, kt, t * 128:(t + 1) * 128], wh[:, kt, 256:384], kt == 0, kt == KT - 1, [b_wh, b_xT[t]], [BB[bk]])
                p.copy("act", vtok[:, t, :], vps, [], [BB[bk], b_vt])
                bk2 = 6 + (t % 2)
                kps = cx.banks[bk2][:].bitcast(BF16)[:, 0:128]
                p.tr(kps, kdT[:, t * 128:(t + 1) * 128], cx.ident_bf[:], [b_kd, cx.b_ident], [BB[bk2]])
                p.copy("dve", ktk[:, t, :], kps, [], [BB[bk2], b_ktk])
            sps = cx.banks[h % 2][:, 0:128]
            for t in range(NTILE):
                p.mm(sps, ktk[:, t, :], vtok[:, t, :], t == 0, t == NTILE - 1, [b_ktk, b_vt], [BB[h % 2]])
            p.copy("act", S[:, h, :], sps, [], [BB[h % 2], b_S[h]])
        emit_summary(cx, io, [(S[:].rearrange("p h e -> p (h e)"), b_S, 0, 1024), (btot[:], [b_btot], 1024, 8)], osum, b_outd)
        return cx.done(own, [b_outd])
    STOP = int(os.environ.get("HG_STOP", "99"))
    NH = int(os.environ.get("HG_NH", "8"))
    for h in range(NH if STOP > 0 else 0):
        wh, b_wh = wh_bufs[h % 2]
        for s_ in range(4):
            if not full and s_ in (0, 3):
                continue
            p.dma("pool", wh[:, :, s_ * 128:(s_ + 1) * 128], w_in_v[:, :, s_, h, :], writes=[b_wh])
        lb_h, oml_h, noml_h = lbc[:, 0, h:h + 1], lbc[:, 1, h:h + 1], lbc[:, 2, h:h + 1]
        for tb in range(TOK // 512):
            tok0 = tb * 512
            rd = [b_xT[tok0 // 128 + jj] for jj in range(4)]
            (qps, b_qps), (fps, b_fps) = bq[tb % 2], bf_[tb % 2]
            if full:
                for kt in range(KT):
                    p.mm(qps[:], wh[:, kt, 0:128], xT[:, kt, tok0:tok0 + 512], kt == 0, kt == KT - 1, [b_wh] + rd, [b_qps])
            for kt in range(KT):
                p.mm(fps[:], wh[:, kt, 128:256], xT[:, kt, tok0:tok0 + 512], kt == 0, kt == KT - 1, [b_wh] + rd, [b_fps])
            (t1, b_t1), (t2, b_t2), (t3, b_t3) = t1s[0], t2s[0], t3s[0]
            (t4, b_t4), (t5, b_t5), (t6, b_t6) = t4s[0], t5s[0], t6s[0]
            p.act(t1, fps[:], AF.Sigmoid, [], [b_fps, b_t1])
            p.act(t4, t1, AF.Ln, [b_t1, b_lbc], [b_t4], scale=oml_h, bias=lb_h)
            p.ts("dve", t2, t1, 1.0, noml_h, ALU.subtract, ALU.mult, [b_t1, b_lbc], [b_t2])
            if full:
                p.act(t3, qps[:], AF.Silu, [], [b_qps, b_t3])
            blk = bcum[:, tok0:tok0 + 512]
            p.add("dve", lambda e, blk=blk, t4=t4: e.tensor_tensor_scan(out=blk, data0=cmk[:], data1=t4, initial=0.0, op0=ALU.mult, op1=ALU.add),
                  [b_cmk, b_t4], [b_bc])
            b3 = blk.rearrange("p (c t) -> p c t", t=64)
            t4v = t4.rearrange("p (c t) -> p c t", t=64)
            p.tt("dve", t4v, b3, b3[:, :, 32:33].to_broadcast([128, 8, 64]), ALU.subtract, [b_bc], [b_t4])
            p.act(t6, t4, AF.Exp, [b_t4], [b_t6], scale=-1.0)
            p.tt("dve", kT[:, tok0:tok0 + 512], t2, t6, ALU.mult, [b_t2, b_t6], [b_kT])
            if full:
                p.act(t5, t4, AF.Exp, [b_t4], [b_t5])
                p.tt("dve", qT[:, tok0:tok0 + 512], t3, t5, ALU.mult, [b_t3, b_t5], [b_qT])
            c0 = tb * 8
            p.act(sc1[:, c0:c0 + 8], bcum[:, tok0 + 32:tok0 + 512:64], AF.Exp, [b_bc], [b_sc])
            p.act(sc3[:, c0:c0 + 8], bcum[:, tok0 + 63:tok0 + 512:64], AF.Exp, [b_bc], [b_sc])
            p.tt("dve", sc2[:, c0:c0 + 8], bcum[:, tok0 + 63:tok0 + 512:64], bcum[:, tok0 + 32:tok0 + 512:64], ALU.subtract, [b_bc], [b_sc])
            p.act(sc2[:, c0:c0 + 8], sc2[:, c0:c0 + 8], AF.Exp, [b_sc], [b_sc])
        if STOP <= 1:
            continue
        if not full and os.environ.get("HG_NORED") is None:
            p.add("dve", lambda e, h=h: e.tensor_reduce(out=btot[:, h:h + 1], in_=bcum[:, 63:TOK:64], axis=AX.X, op=ALU.add), [b_bc], [b_btot])
        for c in range(NCH):
            vg, b_vg = vg_slots[c % 2]
            rd = [b_xT[c // 2]]
            ncol = 256 if full else 128
            for kt in range(KT):
                p.mm(vg[:, 0:ncol], xT[:, kt, c * 64:(c + 1) * 64], wh[:, kt, 256:256 + ncol], kt == 0, kt == KT - 1, [b_wh] + rd, [b_vg])
            p.copy("act", v_ch[0:64, c, :], vg[:, 0:128], [], [b_vg, b_vch])
            if full:
                sgt, b_sgt = sgt_bufs[c % 2]
                p.act(sgt[0:64, :], vg[:, 128:256], AF.Silu, [], [b_vg, b_sgt])
                p.tt("dve", sgw[0:64, c, :], sgt[0:64, :], nws[:], ALU.mult, [b_sgt, b_nws], [b_sgw])
        if full and h == 0:
            combine()
        pend_norm, pend_tail = [], []
        for c0 in range(0, NCH if STOP > 2 else 0, 2):
            pair = (c0, c0 + 1)
            for c in pair:
                ktr, b_ktr = ktr_slots[c % 2]
                p.tr(ktr, kT[:, c * 64:(c + 1) * 64], cx.ident_bf[:], [b_kT, cx.b_ident], [b_ktr])
            for c in pair:
                ktr, b_ktr = ktr_slots[c % 2]
                ktok, b_ktok = ktok_bufs[c % 2]
                p.copy("act", ktok[0:64, :], ktr, [], [b_ktr, b_ktok])
            if full:
                for c in pair:
                    att, b_att = att_slots[c % 2]
                    cs = slice(c * 64, (c + 1) * 64)
                    p.mm(att, kT[:, cs], qT[:, cs], True, True, [b_kT, b_qT], [b_att])
                for c in pair:
                    att, b_att = att_slots[c % 2]
                    attm, b_attm = attm_bufs[c % 2]
                    p.tt("dve", attm[0:64, :], att, maskT[:], ALU.mult, [b_mask], [b_att, b_attm])
            for c in pair:
                kv, b_kv = kv_slots[c % 2]
                ktok, b_ktok = ktok_bufs[c % 2]
                p.mm(kv, ktok[0:64, :], v_ch[0:64, c, :], True, True, [b_ktok, b_vch], [b_kv])
            for c in pair:
                kv, b_kv = kv_slots[c % 2]
                kvt, b_kvt = kvt_bufs[c % 2]
                p.act(kvt, kv, AF.Identity, [b_sc], [b_kv, b_kvt], scale=sc2[:, c:c + 1])
            for c in pair:
                i2 = c % 2
                cs = slice(c * 64, (c + 1) * 64)
                kvt, b_kvt = kvt_bufs[i2]
                if full:
                    Sbf, b_Sbf = Sbf_bufs[i2]
                    p.act(Sbf, S[:, h, :], AF.Identity, [b_S[h], b_sc], [b_Sbf], scale=sc1[:, c:c + 1])
                    attm, b_attm = attm_bufs[i2]
                    o, b_o = o_slots[i2]
                    p.mm(o, attm[0:64, :], v_ch[0:64, c, :], True, False, [b_attm, b_vch], [b_o])
                    p.mm(o, qT[:, cs], Sbf, False, True, [b_qT, b_Sbf], [b_o])
                p.stt(S[:, h, :], S[:, h, :], sc3[:, c:c + 1], kvt, ALU.mult, ALU.add, [b_S[h], b_sc, b_kvt], [b_S[h]])
            if full:
                def norm_fn(pair=pair, h=h):
                    for c in pair:
                        i2 = c % 2
                        o, b_o = o_slots[i2]
                        ss, b_ss, junk, b_junk = ss_l[i2], b_ss_l[i2], junk_l[i2], b_junk_l[i2]
                        p.act(junk[0:64, :], o, AF.Square, [], [b_o, b_junk, b_ss], accum=ss[:, 0:1])
                    for c in pair:
                        ss, b_ss = ss_l[c % 2], b_ss_l[c % 2]
                        p.act(ss[:, 2:3], ss[:, 0:1], AF.Sqrt, [b_ss, cx.b_eps], [b_ss], bias=cx.eps_hn[0:64, 0:1])
                    for c in pair:
                        ss, b_ss = ss_l[c % 2], b_ss_l[c % 2]
                        p.add("dve", lambda e, ss=ss: e.reciprocal(ss[:, 1:2], ss[:, 2:3]), [b_ss], [b_ss])
                    for c in pair:
                        i2 = c % 2
                        o, b_o = o_slots[i2]
                        ss, b_ss = ss_l[i2], b_ss_l[i2]
                        on, b_on = on_bufs[c % 4]
                        p.stt(on[0:64, :], o, ss[:, 1:2], sgw[0:64, c, :], ALU.mult, ALU.mult, [b_ss, b_sgw], [b_o, b_on])

                def tail_fn(pair=pair, h=h):
                    for c in pair:
                        on, b_on = on_bufs[c % 4]
                        ontr, b_ontr = ontr_slots[c % 2]
                        p.tr(ontr, on[0:64, :], cx.ident_bf[0:64, 0:64], [b_on, cx.b_ident], [b_ontr])
                    for c in pair:
                        ontr, b_ontr = ontr_slots[c % 2]
                        p.copy("act", ONT[:, h, c * 64:(c + 1) * 64], ontr, [], [b_ontr, b_ONT[c // 2]])

                pend_norm.append(norm_fn)
                pend_tail.append(tail_fn)
                pend_norm.pop(0)()
                if len(pend_tail) > 1:
                    pend_tail.pop(0)()
        while full and pend_norm:
            pend_norm.pop(0)()
        while full and pend_tail:
            pend_tail.pop(0)()
    if not full:
        emit_summary(cx, io, [(S[:].rearrange("p h e -> p (h e)"), b_S, 0, 1024), (btot[:], [b_btot], 1024, 8)], osum, b_outd)
        return cx.done(own, [b_outd])

    p.barrier()
    off = KT * TOK
    xr_bufs = [(carve(2048).bitcast(F32), p.dbuf("xr")) for _ in range(2)]
    y_bufs = [(carve(2048).bitcast(F32), p.buf("y")) for _ in range(2)]
    o_bufs = [(carve(2048).bitcast(F32), p.buf("o")) for _ in range(2)]
    tmp = [carve(2048).bitcast(F32) for _ in range(2)]
    b_tmp = [p.buf("tmp") for _ in range(2)]
    xbf = carve(1024)
    b_xbf = p.buf("xbf")
    st = [p.sbuf(p.uid("st1_"), [128, 16], F32) for _ in range(2)]
    b_st = [p.buf("st1") for _ in range(2)]
    b_x1s = p.dbuf("x1s")
    assert off <= KT * TOK + 30720, off
    b_pstr = p.buf("pstr")
    hb = [BB[0], BB[1], BB[2], BB[3]]
    for t in range(NTILE):
        ts_ = slice(t * 128, (t + 1) * 128)
        hps = [cx.banks[(t % 2) * 2], cx.banks[(t % 2) * 2 + 1]]
        hbb = [hb[(t % 2) * 2], hb[(t % 2) * 2 + 1]]
        for half in range(2):
            for hh in range(8):
                p.mm(hps[half][:], ONT[:, hh, ts_], wo_sb[:, hh, half * 512:(half + 1) * 512], hh == 0, hh == 7, [b_ONT[t], b_wo], [hbb[half]])
        residual_ln1_tile(cx, t, [hps[0][:], hps[1][:]], hbb, x, xr_bufs, y_bufs, o_bufs, tmp, b_tmp, st, b_st,
                          lng[:, 0, :], lnb[:, 0, :], b_gb, x1s, b_x1s, x1T, b_x1T, xbf, b_xbf, b_pstr)
    p.barrier()
    ffn_phase(cx, x1T, b_x1T, x1s, b_x1s, wgu, wd, lng[:, 1, :], lnb[:, 1, :], b_gb, y_out, b_outd, arena, KT * TOK)
    return cx.done(own, [b_outd])


ML_W = 3088


def build_mlstm(layer, j, stage, cx=None, io=None):
    import os
    own = cx is None
    if own:
        cx = Ctx(bass.Bass("TRN2", target_bir_lowering=False))
    nc = cx.nc
    p = cx.p

    def dt_(name, shape, kind="Internal"):
        if io is not None and name in io:
            return io[name]
        return nc.dram_tensor(name, shape, F32, kind=kind).ap()
    BB = cx.bank_bufs
    full = stage == "B"
    x = dt_("x", [TOK, D], "ExternalInput")
    xh = dt_("xh", [3, D], "ExternalInput")
    w_in = dt_("w_in", [D, ML_W], "ExternalInput")
    conv_w = dt_("conv_w", [4, D], "ExternalInput")
    gate_b = dt_("gate_b", [16, 1], "ExternalInput")
    if full:
        norm_w = dt_("norm_w", [1, D], "ExternalInput")
        w_out = dt_("w_out", [D, D], "ExternalInput")
        ln_g = dt_("ln_g", [2, D], "ExternalInput")
        ln_b = dt_("ln_b", [2, D], "ExternalInput")
        wgu = dt_("wgu", [D, 2 * FFN_H], "ExternalInput")
        wd = dt_("wd", [FFN_H, D], "ExternalInput")
        ssum = dt_("ssum", [NCORES, 128, 520], "ExternalInput")
        cmask = dt_("cmask", [128, NCORES], "ExternalInput")
        y_out = dt_("y", [TOK, D], "ExternalOutput")
        x1s = dt_("x1s", [TOK, D])
    else:
        osum = dt_("osum", [128, 520], "ExternalOutput") if (io is None or "bin" not in io) else None
    b_outd = p.dbuf("outd")
    NCH = TOK // 64

    load_ohot(cx, io, full)
    cmk = p.sbuf("chunkmask", [128, 512], F32)
    b_cmk = p.buf("cmk")
    p.memset("pool", cmk[:], 1.0, [b_cmk])
    p.memset("pool", cmk[:, 0:512:64], 0.0, [b_cmk])
    cwl, b_cwl = cx.const_load("cw_l", [32, 128], F32, conv_w.rearrange("j (b d) -> (j b) d", d=128))
    cw = p.sbuf("cw", [128, 4, 8], F32)
    b_cw = p.buf("cw")
    p.tr(cx.banks[7][:, 0:32], cwl[:], cx.ident_f[0:32, 0:32], [b_cwl, cx.b_ident], [BB[7]])
    p.copy("dve", cw[:], cx.banks[7][:, 0:32].rearrange("p (j b) -> p j b", j=4), [], [BB[7], b_cw])
    gbi, b_gbi = cx.const_load("gbi", [8, 1], F32, gate_b[0:8, :])
    gbf, b_gbf = cx.const_load("gbf", [8, 1], F32, gate_b[8:16, :])
    ngbf = p.sbuf("ngbf", [8, 1], F32)
    b_ng = p.buf("ngbf")
    p.ts("dve", ngbf[:], gbf[:], -1.0, None, ALU.mult, None, [b_gbf], [b_ng])
    sel = p.sbuf("sel", [8, 4, 128], F32)
    b_sel = p.buf("sel")
    p.memset("pool", sel[:], 1.0, [b_sel])
    for blk in range(4):
        p.add("pool", lambda e, blk=blk: e.affine_select(out=sel[:, blk, :], in_=sel[:, blk, :], pattern=[[1, 128]], compare_op=ALU.is_ge,
                                                         fill=0.0, base=128 * blk, channel_multiplier=-64), [b_sel], [b_sel])
        p.add("pool", lambda e, blk=blk: e.affine_select(out=sel[:, blk, :], in_=sel[:, blk, :], pattern=[[-1, 128]], compare_op=ALU.is_ge,
                                                         fill=0.0, base=63 - 128 * blk, channel_multiplier=64), [b_sel], [b_sel])
    if full:
        ones64 = p.sbuf("ones64", [64, 64], F32)
        maskT = p.sbuf("maskT", [64, 64], F32)
        b_mask = p.buf("mask")
        p.memset("pool", ones64[:], 1.0, [b_mask])
        p.add("pool", lambda e: e.affine_select(out=maskT[:], in_=ones64[:], pattern=[[1, 64]], compare_op=ALU.is_ge, fill=0.0,
                                                base=0, channel_multiplier=-1), [b_mask], [b_mask])
        nwb, b_nwb = cx.const_load("nw_bc", [64, D], F32, norm_w.broadcast_to([64, D]))
        b_nws = p.buf("nws")
        p.ts("dve", nwb[:], nwb[:], math.sqrt(128.0), None, ALU.mult, None, [b_nwb], [b_nws])
        nws = nwb
        lng, lnb, b_gb = load_ln_consts(cx, ln_g, ln_b)
        cm, b_cm = cx.const_load("cmask_sb", [128, NCORES], F32, cmask)

    ARENA = 88 * 1024 + 512
    arena = p.sbuf("arena", [128, ARENA], BF16)
    off = 0

    def carve(n):
        nonlocal off
        v = arena[:, off:off + n]
        off += n
        assert off <= ARENA, off
        return v

    b_x1T = [p.buf(f"x1T{t}") for t in range(NTILE)]
    xT = carve(KT * TOK).rearrange("p (k t) -> p k t", k=KT)
    x1T = xT
    b_xT = [p.buf(f"xT{t}") for t in range(NTILE)]
    xb = [(carve(D), p.dbuf("xbf")) for _ in range(2)]
    load_xT(cx, x, xT, b_xT, xb)
    xhb = carve(D)
    b_xhb = p.dbuf("xhb")
    p.memset("dve", xhb, 0.0, [b_xhb])
    p.dma("pool", xhb[0:3, :], xh, writes=[b_xhb])
    xhT = carve(KT * 128).rearrange("p (k t) -> p k t", k=KT)
    b_xhT = p.buf("xhT")
    ps7 = cx.banks[7][:].bitcast(BF16)
    for kt in range(KT):
        p.tr(ps7[:, kt * 128:(kt + 1) * 128], xhb[:, kt * 128:(kt + 1) * 128], cx.ident_bf[:], [b_xhb, cx.b_ident], [BB[7]])
    p.copy("dve", xhT, ps7.rearrange("p (k t) -> p k t", k=KT), [], [BB[7], b_xhT])

    C = p.sbuf("Cst", [128, 4, 129], F32)
    b_C = [p.buf(f"C{b}") for b in range(4)]
    for b in range(4):
        p.memset("dve", C[:, b, :], 0.0, [b_C[b]])
    rdb = [io["b_bout"]] if (io is not None and "b_bout" in io) else []

    def combine():
        sBt = p.sbuf("sB_sb", [128, NCORES, 4], F32)
        b_sBt = p.dbuf("sBt")
        p.dma("sp", sBt[:], ssum[:, :, 516:520].rearrange("c p b -> p c b"), reads=rdb, writes=[b_sBt])
        ea = p.sbuf("ea", [128, NCORES, 4], F32)
        b_ea = p.buf("ea")
        p.act(ea[:], sBt[:], AF.Exp, [b_sBt], [b_ea])
        p.ts("dve", ea[:], ea[:], -1.0, None, ALU.add, None, [b_ea], [b_ea])
        p.tt("dve", ea[:], ea[:], cm[:].unsqueeze(2).to_broadcast([128, NCORES, 4]), ALU.mult, [b_ea, b_cm], [b_ea])
        p.ts("dve", ea[:], ea[:], 1.0, None, ALU.add, None, [b_ea], [b_ea])
        for c in range(NCORES):
            sc, b_scc = sc_bufs[c % 2]
            p.dma("sp", sc, ssum[c, :, 0:516].rearrange("p (b e) -> p b e", b=4), reads=rdb, writes=[b_scc])
            p.ts("dve", sc, sc, cm[:, c:c + 1], None, ALU.mult, None, [b_scc, b_cm], [b_scc])
            for b in range(4):
                p.stt(C[:, b, :], C[:, b, :], ea[:, c, b:b + 1], sc[:, b, :], ALU.mult, ALU.add, [b_C[b], b_ea, b_scc], [b_C[b]])

    if full:
        sc_bufs = [(carve(2 * 4 * 129 + 8)[:, 0:2 * 4 * 129].bitcast(F32).rearrange("p (b e) -> p b e", b=4), p.dbuf("sCc")) for _ in range(2)]

    wg_sb = carve(KT * 16).rearrange("p (k c) -> p k c", k=KT)
    b_wg = p.dbuf("wgate")
    p.dma("pool", wg_sb, w_in.rearrange("(kt p) c -> p kt c", p=128)[:, :, 3072:3088], writes=[b_wg])
    aT = carve(2 * TOK).bitcast(F32)
    cT = carve(2 * TOK).bitcast(F32)
    b_aT = p.buf("aT")
    b_cT = p.buf("cT")
    gt = [carve(1024).bitcast(F32) for _ in range(3)]
    b_gt = p.buf("gt")
    fastA = (not full) and os.environ.get("ML_SLOWA") is None
    ones8 = cmk
    if fastA:
        ones8 = p.sbuf("ones8", [8, 512], F32)
        p.memset("pool", ones8[:], 1.0, [b_cmk])
    for tb in range(TOK // 512):
        tok0 = tb * 512
        rd = [b_xT[tok0 // 128 + jj] for jj in range(4)]
        gi_ps, gf_ps = cx.banks[4], cx.banks[5]
        for kt in range(KT):
            p.mm(gi_ps[0:8, :], wg_sb[:, kt, 0:8], xT[:, kt, tok0:tok0 + 512], kt == 0, kt == KT - 1, [b_wg] + rd, [BB[4]])
        for kt in range(KT):
            p.mm(gf_ps[0:8, :], wg_sb[:, kt, 8:16], xT[:, kt, tok0:tok0 + 512], kt == 0, kt == KT - 1, [b_wg] + rd, [BB[5]])
        p.act(gt[0][0:8, :], gf_ps[0:8, :], AF.Exp, [b_ng], [BB[5], b_gt], scale=-1.0, bias=ngbf[:, 0:1])
        p.act(gt[0][0:8, :], gt[0][0:8, :], AF.Ln, [b_gt], [b_gt], bias=1.0)
        if fastA:
            init = 0.0 if tb == 0 else aT[0:8, tok0 - 1:tok0]
            p.add("dve", lambda e, tok0=tok0, init=init: e.tensor_tensor_scan(out=aT[0:8, tok0:tok0 + 512], data0=ones8[0:8, :], data1=gt[0][0:8, :], initial=init,
                                                                             op0=ALU.mult, op1=ALU.add), [b_cmk, b_gt, b_aT], [b_aT])
            p.stt(cT[0:8, tok0:tok0 + 512], gi_ps[0:8, :], gbi[:, 0:1], aT[0:8, tok0:tok0 + 512], ALU.add, ALU.add, [b_gbi, b_aT], [BB[4], b_cT])
            continue
        p.add("dve", lambda e, tb=tb: e.tensor_tensor_scan(out=gt[1][0:8, :], data0=cmk[0:8, :], data1=gt[0][0:8, :], initial=0.0, op0=ALU.mult, op1=ALU.add),
              [b_cmk, b_gt], [b_gt])
        p.act(aT[0:8, tok0:tok0 + 512], gt[1][0:8, :], AF.Exp, [b_gt], [b_aT], scale=-1.0)
        p.stt(gt[2][0:8, :], gi_ps[0:8, :], gbi[:, 0:1], gt[1][0:8, :], ALU.add, ALU.add, [b_gbi, b_gt], [BB[4], b_gt])
        p.act(cT[0:8, tok0:tok0 + 512], gt[2][0:8, :], AF.Exp, [b_gt], [b_cT])
    if not full:
        lt8 = p.sbuf("lt8", [8, NCH + 1], F32)
        b_lt8 = p.buf("lt8")
        if fastA:
            p.ts("dve", lt8[:, NCH:NCH + 1], aT[0:8, TOK - 1:TOK], -1.0, None, ALU.mult, None, [b_aT], [b_lt8])
            p.act(cT[0:8, :], cT[0:8, :], AF.Exp, [b_cT, b_lt8], [b_cT], bias=lt8[:, NCH:NCH + 1])
        else:
            p.act(lt8[:, 0:NCH], aT[0:8, 63:TOK:64], AF.Ln, [b_aT], [b_lt8])
            p.add("dve", lambda e: e.tensor_reduce(out=lt8[:, NCH:NCH + 1], in_=lt8[:, 0:NCH], axis=AX.X, op=ALU.add), [b_lt8], [b_lt8])

    wqk_bufs = [(carve(KT * 256).rearrange("p (k c) -> p k c", k=KT), p.dbuf("wqk")) for _ in range(1)]
    pre_q = carve(2 * (TOK + 4)).bitcast(F32)
    pre_k = carve(2 * (TOK + 4)).bitcast(F32)
    b_pq = p.buf("pre_q")
    b_pk = p.buf("pre_k")
    cvt = [(carve(1024).bitcast(F32), p.buf("cvt")) for _ in range(2)]
    qT = carve(TOK)
    kT = carve(TOK)
    b_qT = p.buf("qT")
    b_kT = p.buf("kT")
    wvo_bufs = [(carve(KT * 256).rearrange("p (k c) -> p k c", k=KT), p.dbuf("wvo")) for _ in range(2)]
    vaug = carve(NCH * 130).rearrange("p (c e) -> p c e", c=NCH)
    b_va = p.buf("vaug")
    p.memset("pool", vaug[0:64, :, 128:129], 1.0, [b_va])
    if full:
        sgw = carve(NCH * 128).rearrange("p (c e) -> p c e", c=NCH)
        b_sgw = p.buf("sgw")
        ONT = carve(KT * TOK).rearrange("p (h t) -> p h t", h=8)
        b_ONT = [p.buf(f"ONT{t}") for t in range(NTILE)]
        wo_sb = carve(KT * D).rearrange("p (k c) -> p k c", k=KT)
        b_wo = p.dbuf("wo")
        p.dma("pool", wo_sb, w_out.rearrange("(kt p) c -> p kt c", p=128), writes=[b_wo])
    sgt_bufs = [(carve(256).bitcast(F32), p.buf("sgt")) for _ in range(2)]
    ktok_bufs = [(carve(128), p.buf("ktok")) for _ in range(2)]
    sm_bufs = [(carve(64), p.buf("smk")) for _ in range(2)]
    akv_bufs = [(carve(264).bitcast(F32)[:, 0:129], p.buf("akv")) for _ in range(2)]
    Cbf_bufs = [(carve(136), p.buf("Cbf")) for _ in range(4)]
    for cb_, b_cb_ in Cbf_bufs:
        p.memset("dve", cb_, 0.0, [b_cb_])
    on_bufs = [(carve(128), p.buf("on")) for _ in range(4)]
    ss_l = [p.sbuf(p.uid("ss"), [64, 8], F32) for _ in range(2)]
    b_ss_l = [p.buf("ss") for _ in range(2)]
    junk_l = [carve(256).bitcast(F32) for _ in range(2)]
    b_junk_l = [p.buf("junk") for _ in range(2)]
    abc_sb = carve(2 * NCH).bitcast(F32)
    b_abc = p.buf("abc")

    w_v = w_in.rearrange("(kt p) c -> p kt c", p=128)
    NBLK = int(os.environ.get("ML_NBLK", "4"))
    if fastA:
        ktk = carve(NTILE * 128).rearrange("p (t e) -> p t e", t=NTILE)
        b_ktk = p.buf("ktk")
        va128 = carve(NTILE * 130).rearrange("p (t e) -> p t e", t=NTILE)
        p.memset("pool", va128[:, :, 128:129], 1.0, [b_va])
    for blk in range(NBLK):
        wqk, b_wqk = wqk_bufs[0]
        if full:
            p.dma("pool", wqk[:, :, 0:128], w_v[:, :, blk * 128:(blk + 1) * 128], writes=[b_wqk])
        p.dma("pool", wqk[:, :, 128:256], w_v[:, :, 512 + blk * 128:512 + (blk + 1) * 128], writes=[b_wqk])
        hq, hk = cx.banks[6], cx.banks[7]
        if full:
            for kt in range(KT):
                p.mm(hq[:, 0:4], wqk[:, kt, 0:128], xhT[:, kt, 0:4], kt == 0, kt == KT - 1, [b_wqk, b_xhT], [BB[6]])
            p.copy("dve", pre_q[:, 1:4], hq[:, 0:3], [], [BB[6], b_pq])
        for kt in range(KT):
            p.mm(hk[:, 0:4], wqk[:, kt, 128:256], xhT[:, kt, 0:4], kt == 0, kt == KT - 1, [b_wqk, b_xhT], [BB[7]])
        p.copy("dve", pre_k[:, 1:4], hk[:, 0:3], [], [BB[7], b_pk])
        for tb in range(TOK // 512):
            tok0 = tb * 512
            rd = [b_xT[tok0 // 128 + jj] for jj in range(4)]
            qps, kps = cx.banks[(tb % 2) * 2], cx.banks[(tb % 2) * 2 + 1]
            b_qps, b_kps = BB[(tb % 2) * 2], BB[(tb % 2) * 2 + 1]
            if full:
                for kt in range(KT):
                    p.mm(qps[:], wqk[:, kt, 0:128], xT[:, kt, tok0:tok0 + 512], kt == 0, kt == KT - 1, [b_wqk] + rd, [b_qps])
                p.copy("act", pre_q[:, 4 + tok0:4 + tok0 + 512], qps[:], [], [b_qps, b_pq])
            for kt in range(KT):
                p.mm(kps[:], wqk[:, kt, 128:256], xT[:, kt, tok0:tok0 + 512], kt == 0, kt == KT - 1, [b_wqk] + rd, [b_kps])
            p.copy("act", pre_k[:, 4 + tok0:4 + tok0 + 512], kps[:], [], [b_kps, b_pk])
            abc_ps, cbc_ps = cx.banks[4], cx.banks[5]
            p.mm(abc_ps[:], sel[:, blk, :], aT[0:8, tok0:tok0 + 512], True, True, [b_sel, b_aT], [BB[4]])
            p.mm(cbc_ps[:], sel[:, blk, :], cT[0:8, tok0:tok0 + 512], True, True, [b_sel, b_cT], [BB[5]])
            if not fastA:
                p.copy("dve", abc_sb[:, tb * 8:(tb + 1) * 8], abc_ps[:, 63:512:64], [], [BB[4], b_abc])
            for which in ((0, 1) if full else (1,)):
                pre, b_pre = (pre_q, b_pq) if which == 0 else (pre_k, b_pk)
                cb = blk if which == 0 else 4 + blk
                t0_, b_t0 = cvt[0]
                t1_, b_t1 = cvt[1]
                base = 4 + tok0
                p.ts("dve", t0_, pre[:, base - 3:base - 3 + 512], cw[:, 0, cb:cb + 1], None, ALU.mult, None, [b_pre, b_cw], [b_t0])
                for jj in (1, 2, 3):
                    p.stt(t0_, pre[:, base - 3 + jj:base - 3 + jj + 512], cw[:, jj, cb:cb + 1], t0_, ALU.mult, ALU.add, [b_pre, b_cw, b_t0], [b_t0])
                p.act(t1_, t0_, AF.Silu, [b_t0], [b_t1])
                if which == 0:
                    p.tt("dve", qT[:, tok0:tok0 + 512], t1_, abc_ps[:], ALU.mult, [b_t1], [BB[4], b_qT])
                else:
                    p.stt(kT[:, tok0:tok0 + 512], t1_, 0.125, cbc_ps[:], ALU.mult, ALU.mult, [b_t1], [BB[5], b_kT])
        if fastA:
            for t in range(NTILE):
                bk2 = 6 + (t % 2)
                kps = cx.banks[bk2][:].bitcast(BF16)[:, 0:128]
                p.tr(kps, kT[:, t * 128:(t + 1) * 128], cx.ident_bf[:], [b_kT, cx.b_ident], [BB[bk2]])
                p.copy("dve", ktk[:, t, :], kps, [], [BB[bk2], b_ktk])
            for hl in range(2):
                h = 2 * blk + hl
                rows = slice(64 * hl, 64 * hl + 64)
                wvo, b_wvo = wvo_bufs[h % 2]
                p.dma("pool", wvo[:, :, 0:128], w_v[:, :, 1024 + h * 128:1024 + (h + 1) * 128], writes=[b_wvo])
                for t in range(NTILE):
                    bk = 4 + (t % 2)
                    vps = cx.banks[bk][:, 0:128]
                    for kt in range(KT):
                        p.mm(vps, xT[:, kt, t * 128:(t + 1) * 128], wvo[:, kt, 0:128], kt == 0, kt == KT - 1, [b_wvo, b_xT[t]], [BB[bk]])
                    p.copy("act", va128[:, t, 0:128], vps, [], [BB[bk], b_va])
                cps = cx.banks[hl][:, 0:129]
                for t in range(NTILE):
                    p.mm(cps, ktk[:, t, :], va128[:, t, 0:129], t == 0, t == NTILE - 1, [b_ktk, b_va], [BB[hl]])
                p.copy("act", C[rows, blk, :], cps[rows, :], [], [BB[hl], b_C[blk]])
            continue
        for hl in range(2):
            h = 2 * blk + hl
            rows = slice(64 * hl, 64 * hl + 64)
            wvo, b_wvo = wvo_bufs[h % 2]
            p.dma("pool", wvo[:, :, 0:128], w_v[:, :, 1024 + h * 128:1024 + (h + 1) * 128], writes=[b_wvo])
            if full:
                p.dma("pool", wvo[:, :, 128:256], w_v[:, :, 2048 + h * 128:2048 + (h + 1) * 128], writes=[b_wvo])
            ncol = 256 if full else 128
            for c in range(NCH):
                bk = 4 + (c % 2)
                vg = cx.banks[bk][0:64, 0:256]
                for kt in range(KT):
                    p.mm(vg[:, 0:ncol], xT[:, kt, c * 64:(c + 1) * 64], wvo[:, kt, 0:ncol], kt == 0, kt == KT - 1, [b_wvo, b_xT[c // 2]], [BB[bk]])
                p.copy("act", vaug[0:64, c, 0:128], vg[:, 0:128], [], [BB[bk], b_va])
                if full:
                    sgt, b_sgt = sgt_bufs[c % 2]
                    p.act(sgt[0:64, :], vg[:, 128:256], AF.Sigmoid, [], [BB[bk], b_sgt])
                    p.tt("dve", sgw[0:64, c, :], sgt[0:64, :], nws[:, h * 128:(h + 1) * 128], ALU.mult, [b_sgt, b_nws], [b_sgw])
            if full and blk == 0 and hl == 0:
                combine()
            pend_tail = []
            for c0 in range(0, NCH, 2):
                pair = (c0, c0 + 1)
                for c in pair:
                    ktr = cx.banks[c % 2][:].bitcast(BF16)[0:64, 0:128]
                    p.tr(ktr, kT[:, c * 64:(c + 1) * 64], cx.ident_bf[:], [b_kT, cx.b_ident], [BB[c % 2]])
                for c in pair:
                    ktr = cx.banks[c % 2][:].bitcast(BF16)[0:64, 0:128]
                    ktok, b_ktok = ktok_bufs[c % 2]
                    p.copy("act", ktok[0:64, :], ktr, [], [BB[c % 2], b_ktok])
                if full:
                    for c in pair:
                        cs = slice(c * 64, (c + 1) * 64)
                        sT = cx.banks[2 + c % 2][0:64, 0:64]
                        p.mm(sT, kT[rows, cs], qT[rows, cs], True, True, [b_kT, b_qT], [BB[2 + c % 2]])
                    for c in pair:
                        sT = cx.banks[2 + c % 2][0:64, 0:64]
                        smk, b_smk = sm_bufs[c % 2]
                        p.tt("dve", smk[0:64, :], sT, maskT[:], ALU.mult, [b_mask], [BB[2 + c % 2], b_smk])
                for c in pair:
                    kv = cx.banks[c % 2][:, 256:256 + 129]
                    ktok, b_ktok = ktok_bufs[c % 2]
                    p.mm(kv, ktok[0:64, :], vaug[0:64, c, 0:129], True, True, [b_ktok, b_va], [BB[c % 2]])
                for c in pair:
                    i2 = c % 2
                    cs = slice(c * 64, (c + 1) * 64)
                    kv = cx.banks[i2][:, 256:256 + 129]
                    if full:
                        Cbf, b_Cbf = Cbf_bufs[hl * 2 + i2]
                        p.copy("act", Cbf[rows, 0:129], C[rows, blk, :], [b_C[blk]], [b_Cbf])
                        smk, b_smk = sm_bufs[i2]
                        num = cx.banks[4 + i2][0:64, 256:256 + 129]
                        p.mm(num, smk[0:64, :], vaug[0:64, c, 0:129], True, False, [b_smk, b_va], [BB[4 + i2]])
                        p.mm(num, qT[:, cs], Cbf[:, 0:129], False, True, [b_qT, b_Cbf], [BB[4 + i2]])
                    akv, b_akv = akv_bufs[i2]
                    p.act(akv[rows, :], kv[rows, :], AF.Identity, [b_abc], [BB[i2], b_akv], scale=abc_sb[rows, c:c + 1])
                    p.stt(C[rows, blk, :], C[rows, blk, :], abc_sb[rows, c:c + 1], akv[rows, :], ALU.mult, ALU.add, [b_C[blk], b_abc, b_akv], [b_C[blk]])
                if full:
                    for c in pair:
                        i2 = c % 2
                        num = cx.banks[4 + i2][0:64, 256:256 + 129]
                        ss, b_ss, junk, b_junk = ss_l[i2], b_ss_l[i2], junk_l[i2], b_junk_l[i2]
                        p.act(ss[:, 0:1], num[:, 128:129], AF.Square, [], [BB[4 + i2], b_ss])
                        p.act(junk[0:64, :], num[:, 0:128], AF.Square, [], [BB[4 + i2], b_junk, b_ss], accum=ss[:, 2:3])
                    for c in pair:
                        ss, b_ss = ss_l[c % 2], b_ss_l[c % 2]
                        p.ts("dve", ss[:, 1:2], ss[:, 0:1], 1.0, 128.0 * HN_EPS, ALU.max, ALU.mult, [b_ss], [b_ss])
                        p.tt("dve", ss[:, 3:4], ss[:, 1:2], ss[:, 2:3], ALU.add, [b_ss], [b_ss])
                    for c in pair:
                        ss, b_ss = ss_l[c % 2], b_ss_l[c % 2]
                        p.act(ss[:, 4:5], ss[:, 3:4], AF.Sqrt, [b_ss], [b_ss])
                    for c in pair:
                        ss, b_ss = ss_l[c % 2], b_ss_l[c % 2]
                        p.add("dve", lambda e, ss=ss: e.reciprocal(ss[:, 5:6], ss[:, 4:5]), [b_ss], [b_ss])
                    for c in pair:
                        i2 = c % 2
                        num = cx.banks[4 + i2][0:64, 256:256 + 129]
                        ss, b_ss = ss_l[i2], b_ss_l[i2]
                        on, b_on = on_bufs[c % 4]
                        p.stt(on[0:64, :], num[:, 0:128], ss[:, 5:6], sgw[0:64, c, :], ALU.mult, ALU.mult, [b_ss, b_sgw], [BB[4 + i2], b_on])

                    def tail_fn(pair=pair, h=h):
                        for c in pair:
                            on, b_on = on_bufs[c % 4]
                            ontr = cx.banks[6 + c % 2][:].bitcast(BF16)[:, 0:64]
                            p.tr(ontr, on[0:64, :], cx.ident_bf[0:64, 0:64], [b_on, cx.b_ident], [BB[6 + c % 2]])
                        for c in pair:
                            ontr = cx.banks[6 + c % 2][:].bitcast(BF16)[:, 0:64]
                            p.copy("act", ONT[:, h, c * 64:(c + 1) * 64], ontr, [], [BB[6 + c % 2], b_ONT[c // 2]])

                    pend_tail.append(tail_fn)
                    if len(pend_tail) > 1:
                        pend_tail.pop(0)()
            while full and pend_tail:
                pend_tail.pop(0)()
    if not full:
        ob = p.sbuf("ob", [128, 4], F32)
        b_ob = p.buf("ob")
        for blk in range(4):
            p.mm(cx.banks[4][:, 0:1], sel[:, blk, :], lt8[:, NCH:NCH + 1], True, True, [b_sel, b_lt8], [BB[4]])
            p.copy("dve", ob[:, blk:blk + 1], cx.banks[4][:, 0:1], [], [BB[4], b_ob])
        emit_summary(cx, io, [(C[:].rearrange("p b e -> p (b e)"), b_C, 0, 516), (ob[:], [b_ob], 516, 4)], osum, b_outd)
        return cx.done(own, [b_outd])

    p.barrier()
    off = KT * TOK
    xr_bufs = [(carve(2048).bitcast(F32), p.dbuf("xr")) for _ in range(2)]
    y_bufs = [(carve(2048).bitcast(F32), p.buf("y")) for _ in range(2)]
    o_bufs = [(carve(2048).bitcast(F32), p.buf("o")) for _ in range(2)]
    tmp = [carve(2048).bitcast(F32) for _ in range(2)]
    b_tmp = [p.buf("tmp") for _ in range(2)]
    xbf = carve(1024)
    b_xbf = p.buf("xbf")
    st = [p.sbuf(p.uid("st1_"), [128, 16], F32) for _ in range(2)]
    b_st = [p.buf("st1") for _ in range(2)]
    b_x1s = p.dbuf("x1s")
    hb = [BB[0], BB[1], BB[2], BB[3]]
    for t in range(NTILE):
        ts_ = slice(t * 128, (t + 1) * 128)
        hps = [cx.banks[(t % 2) * 2], cx.banks[(t % 2) * 2 + 1]]
        hbb = [hb[(t % 2) * 2], hb[(t % 2) * 2 + 1]]
        for half in range(2):
            for hh in range(8):
                p.mm(hps[half][:], ONT[:, hh, ts_], wo_sb[:, hh, half * 512:(half + 1) * 512], hh == 0, hh == 7, [b_ONT[t], b_wo], [hbb[half]])
        residual_ln1_tile(cx, t, [hps[0][:], hps[1][:]], hbb, x, xr_bufs, y_bufs, o_bufs, tmp, b_tmp, st, b_st,
                          lng[:, 0, :], lnb[:, 0, :], b_gb, x1s, b_x1s, x1T, b_x1T, xbf, b_xbf, None)
    p.barrier()
    ffn_phase(cx, x1T, b_x1T, x1s, b_x1s, wgu, wd, lng[:, 1, :], lnb[:, 1, :], b_gb, y_out, b_outd, arena, KT * TOK)
    return cx.done(own, [b_outd])


NCH8 = TOK // 8
TWO_PI = 2.0 * math.pi
I32 = mybir.dt.int32


def tile_rows_s5(ap, ti):
    half, t = ti // 8, ti % 8
    return ap[half * 1024 + t:half * 1024 + 1024:8, :]


def build_s5(layer, j, stage, cx=None, io=None):
    import os
    own = cx is None
    if own:
        cx = Ctx(bass.Bass("TRN2", target_bir_lowering=False))
    nc = cx.nc
    p = cx.p

    def dt_(name, shape, kind="Internal"):
        if io is not None and name in io:
            return io[name]
        return nc.dram_tensor(name, shape, F32, kind=kind).ap()
    BB = cx.bank_bufs
    full = stage == "B"
    x = dt_("x", [TOK, D], "ExternalInput")
    w_in = dt_("w_in", [D, D], "ExternalInput")
    a_re = dt_("a_re", [32, 128], "ExternalInput")
    a_im = dt_("a_im", [32, 128], "ExternalInput")
    ldt = dt_("ldt", [32, 128], "ExternalInput")
    b_re = dt_("b_re", [64, 64, 16], "ExternalInput")
    b_im = dt_("b_im", [64, 64, 16], "ExternalInput")
    if full:
        c_re = dt_("c_re", [64, 16, 64], "ExternalInput")
        c_im = dt_("c_im", [64, 16, 64], "ExternalInput")
        d_skip = dt_("d_skip", [8, 128], "ExternalInput")
        w_out = dt_("w_out", [D, 2 * D], "ExternalInput")
        ln_g = dt_("ln_g", [2, D], "ExternalInput")
        ln_b = dt_("ln_b", [2, D], "ExternalInput")
        wgu = dt_("wgu", [D, 2 * FFN_H], "ExternalInput")
        wd = dt_("wd", [FFN_H, D], "ExternalInput")
        ssum = dt_("ssum", [NCORES, 128, 64], "ExternalInput")
        cmask = dt_("cmask", [128, NCORES], "ExternalInput")
        y_out = dt_("y", [TOK, D], "ExternalOutput")
        x1s = dt_("x1s", [TOK, D])
    else:
        osum = dt_("osum", [128, 64], "ExternalOutput") if (io is None or "bin" not in io) else None
    b_outd = p.dbuf("outd")
    load_ohot(cx, io, full)

    ARENA = 80 * 1024 + 4608
    arena = p.sbuf("arena", [128, ARENA], BF16)
    R_XT, R_UT, R_HZ, R_HB = 0, 16384, 32768, 65536

    def region(off, n):
        assert off + n <= ARENA
        return arena[:, off:off + n]

    def sl_load(name, src):
        t, b_t = cx.const_load(name + "_l", [32, 128], F32, src)
        o = p.sbuf(name + "_sl", [128, 32], F32)
        b_o = p.buf(name)
        p.tr(cx.banks[7][:, 0:32], t[:], cx.ident_f[0:32, 0:32], [b_t, cx.b_ident], [BB[7]])
        p.copy("dve", o[:], cx.banks[7][:, 0:32], [], [BB[7], b_o])
        return o, b_o

    are, b_are = sl_load("are", a_re)
    aim, b_aim = sl_load("aim", a_im)
    ldt_sl, b_ldt = sl_load("ldt", ldt)
    if full:
        lng, lnb, b_gb = load_ln_consts(cx, ln_g, ln_b)
        cm, b_cm = cx.const_load("cmask_sb", [128, NCORES], F32, cmask)
        dl, b_dl = cx.const_load("d_l", [8, 128], F32, d_skip)
    prm = p.sbuf("prm", [128, 9, 32], F32)
    b_prm = p.buf("prm")
    p.act(prm[:, 0, :], ldt_sl[:], AF.Exp, [b_ldt], [b_prm])
    p.tt("dve", prm[:, 1, :], are[:], prm[:, 0, :], ALU.mult, [b_are, b_prm], [b_prm])
    p.tt("dve", prm[:, 2, :], aim[:], prm[:, 0, :], ALU.mult, [b_aim, b_prm], [b_prm])
    nvec = p.sbuf("nvec", [128, 9], F32)
    b_nv = p.buf("nvec")
    for n in range(9):
        p.memset("pool", nvec[:, n:n + 1], float(n), [b_nv])

    xT = region(R_XT, KT * TOK).rearrange("p (k t) -> p k t", k=KT)
    b_xT = [p.buf(f"xT{t}") for t in range(NTILE)]
    xb = [(region(R_UT + i * 1024, 1024), p.dbuf("xbf")) for i in range(2)]
    load_xT(cx, x, xT, b_xT, xb)
    p.barrier()
    uT = region(R_UT, KT * TOK).rearrange("p (k t) -> p k t", k=KT)
    b_uT = [p.buf(f"uT{b}") for b in range(8)]
    wi_sb = region(R_HZ, 4096).rearrange("p (k c) -> p k c", k=KT)
    b_wi = p.dbuf("wi")
    w_v = w_in.rearrange("(kt p) c -> p kt c", p=128)
    ib = 0
    for hf in range(2):
        p.dma("pool", wi_sb, w_v[:, :, hf * 512:(hf + 1) * 512], writes=[b_wi])
        for bl in range(4):
            blk = hf * 4 + bl
            for tb in range(TOK // 512):
                bk = ib % 4
                ib += 1
                for kt in range(KT):
                    p.mm(cx.banks[bk][:], wi_sb[:, kt, bl * 128:(bl + 1) * 128], xT[:, kt, tb * 512:(tb + 1) * 512], kt == 0, kt == KT - 1,
                         [b_wi] + [b_xT[tb * 4 + jj] for jj in range(4)], [BB[bk]])
                p.copy("act" if ib % 2 == 0 else "dve", uT[:, blk, tb * 512:(tb + 1) * 512], cx.banks[bk][:], [], [BB[bk], b_uT[blk]])
    p.barrier()

    toff = R_HZ

    def tcarve(n):
        nonlocal toff
        v = arena[:, toff:toff + n]
        toff += n
        assert toff <= R_HB
        return v

    b_T = p.buf("preptmp")
    sh9 = [128, 32, 9]
    nth = tcarve(576).bitcast(F32).rearrange("p (j n) -> p j n", n=9)
    nar = tcarve(576).bitcast(F32).rearrange("p (j n) -> p j n", n=9)
    kint = tcarve(576).bitcast(I32).rearrange("p (j n) -> p j n", n=9)
    kf = tcarve(576).bitcast(F32).rearrange("p (j n) -> p j n", n=9)
    sinv = tcarve(576).bitcast(F32).rearrange("p (j n) -> p j n", n=9)
    cosv = tcarve(576).bitcast(F32).rearrange("p (j n) -> p j n", n=9)
    Pre = p.sbuf("Pre", sh9, F32)
    Pim = p.sbuf("Pim", sh9, F32)
    b_P = p.buf("P")
    nv_bc = nvec[:].unsqueeze(1).to_broadcast(sh9)
    p.tt("dve", nth, prm[:, 2, :].unsqueeze(2).to_broadcast(sh9), nv_bc, ALU.mult, [b_prm, b_nv], [b_T])
    p.tt("dve", nar, prm[:, 1, :].unsqueeze(2).to_broadcast(sh9), nv_bc, ALU.mult, [b_prm, b_nv], [b_T])
    p.act(nar, nar, AF.Exp, [b_T], [b_T])
    for which, outv, shift in ((0, sinv, 0.0), (1, cosv, 0.5 * math.pi)):
        src = nth
        if shift != 0.0:
            p.ts("dve", kf, nth, shift, None, ALU.add, None, [b_T], [b_T])
            p.copy("dve", outv, kf, [b_T], [b_T])
            src = outv
        p.ts("dve", kint, src, 1.0 / TWO_PI, None, ALU.mult, None, [b_T], [b_T])
        p.copy("dve", kf, kint, [b_T], [b_T])
        p.stt(outv, kf, -TWO_PI, src, ALU.mult, ALU.add, [b_T], [b_T])
        p.ts("dve", outv, outv, math.pi, -math.pi, ALU.min, ALU.max, [b_T], [b_T])
        p.act(outv, outv, AF.Sin, [b_T], [b_T])
    p.tt("dve", Pre[:], nar, cosv, ALU.mult, [b_T], [b_P])
    p.tt("dve", Pim[:], nar, sinv, ALU.mult, [b_T], [b_P])

    p.ts("dve", prm[:, 6, :], Pre[:, :, 1], -1.0, None, ALU.add, None, [b_P], [b_prm])
    p.tt("dve", prm[:, 7, :], are[:], are[:], ALU.mult, [b_are], [b_prm])
    p.tt("dve", prm[:, 8, :], aim[:], aim[:], ALU.mult, [b_aim], [b_prm])
    p.tt("dve", prm[:, 7, :], prm[:, 7, :], prm[:, 8, :], ALU.add, [b_prm], [b_prm])
    p.add("dve", lambda e: e.reciprocal(prm[:, 3, :], prm[:, 7, :]), [b_prm], [b_prm])
    p.tt("dve", prm[:, 7, :], prm[:, 6, :], are[:], ALU.mult, [b_prm, b_are], [b_prm])
    p.tt("dve", prm[:, 8, :], Pim[:, :, 1], aim[:], ALU.mult, [b_P, b_aim], [b_prm])
    p.tt("dve", prm[:, 7, :], prm[:, 7, :], prm[:, 8, :], ALU.add, [b_prm], [b_prm])
    p.tt("dve", prm[:, 4, :], prm[:, 7, :], prm[:, 3, :], ALU.mult, [b_prm], [b_prm])
    p.tt("dve", prm[:, 7, :], Pim[:, :, 1], are[:], ALU.mult, [b_P, b_are], [b_prm])
    p.tt("dve", prm[:, 8, :], prm[:, 6, :], aim[:], ALU.mult, [b_prm, b_aim], [b_prm])
    p.tt("dve", prm[:, 7, :], prm[:, 7, :], prm[:, 8, :], ALU.subtract, [b_prm], [b_prm])
    p.tt("dve", prm[:, 5, :], prm[:, 7, :], prm[:, 3, :], ALU.mult, [b_prm], [b_prm])
    sh16 = [128, 32, 16]

    def f16():
        return tcarve(1024).bitcast(F32).rearrange("p (j c) -> p j c", c=16)

    bbr, bbi = f16(), f16()
    b_bb = p.dbuf("bb")
    for g2 in range(2):
        p.dma("sp", bbr[64 * g2:64 * g2 + 64], b_re.rearrange("(j g2) p c -> g2 p j c", g2=2)[g2], writes=[b_bb])
        p.dma("sp", bbi[64 * g2:64 * g2 + 64], b_im.rearrange("(j g2) p c -> g2 p j c", g2=2)[g2], writes=[b_bb])
    bre, bim = f16(), f16()
    t1, t2 = f16(), f16()
    b_bar = p.buf("bbar")
    cre_bc = prm[:, 4, :].unsqueeze(2).to_broadcast(sh16)
    cim_bc = prm[:, 5, :].unsqueeze(2).to_broadcast(sh16)
    p.tt("dve", t1, bbr, cre_bc, ALU.mult, [b_bb, b_prm], [b_T])
    p.tt("dve", t2, bbi, cim_bc, ALU.mult, [b_bb, b_prm], [b_T])
    p.tt("dve", bre, t1, t2, ALU.subtract, [b_T], [b_bar])
    p.tt("dve", t1, bbi, cre_bc, ALU.mult, [b_bb, b_prm], [b_T])
    p.tt("dve", t2, bbr, cim_bc, ALU.mult, [b_bb, b_prm], [b_T])
    p.tt("dve", bim, t1, t2, ALU.add, [b_T], [b_bar])

    WZ = region(R_HB, 16384).rearrange("p (b s r m) -> p b s r m", b=8, s=8, r=2)
    b_WZ = p.buf("WZ")
    epads = []
    for i in range(2):
        ep = tcarve(2048)
        b_ep = p.buf(f"epad{i}")
        p.memset("dve", ep, 0.0, [b_ep])
        epads.append((ep, b_ep))
    t3, t4 = f16(), f16()
    for s in range(8):
        n = 7 - s
        pr_bc = Pre[:, :, n].unsqueeze(2).to_broadcast(sh16)
        pi_bc = Pim[:, :, n].unsqueeze(2).to_broadcast(sh16)
        ep, b_ep = epads[s % 2]
        p.tt("dve", t1, bre, pr_bc, ALU.mult, [b_bar, b_P], [b_T])
        p.tt("dve", t2, bim, pi_bc, ALU.mult, [b_bar, b_P], [b_T])
        p.tt("dve", t3, bim, pr_bc, ALU.mult, [b_bar, b_P], [b_T])
        p.tt("dve", t4, bre, pi_bc, ALU.mult, [b_bar, b_P], [b_T])
        for half in range(2):
            rows = slice(64 * half, 64 * half + 64)
            for ri, (ta, tb_, op) in enumerate(((t1, t2, ALU.subtract), (t3, t4, ALU.add))):
                dst = ep.rearrange("p (b r q h c) -> p b r q h c", b=8, r=2, q=4, h=2)[rows, :, ri, :, half, :]
                p.tt("dve", dst, ta[rows].rearrange("p (b q) c -> p b q c", q=4), tb_[rows].rearrange("p (b q) c -> p b q c", q=4), op, [b_T], [b_ep])
        epv = ep.rearrange("p (b r m) -> p b r m", b=8, r=2)
        for b in range(8):
            bk = b % 4
            psb = cx.banks[bk][:].bitcast(BF16)
            for ri in range(2):
                p.tr(psb[:, ri * 128:(ri + 1) * 128], epv[:, b, ri, :], cx.ident_bf[:], [b_ep, cx.b_ident], [BB[bk]])
            p.copy("act" if b % 2 == 0 else "dve", WZ[:, b, s, :, :], psb[:, 0:256].rearrange("p (r m) -> p r m", r=2), [], [BB[bk], b_WZ])

    ArAr = p.sbuf("ArAr", [128, 2, 32], F32)
    AiN = p.sbuf("AiN", [128, 2, 32], F32)
    b_A8 = p.buf("A8")
    for r in range(2):
        p.copy("dve", ArAr[:, r, :], Pre[:, :, 8], [b_P], [b_A8])
    p.ts("dve", AiN[:, 0, :], Pim[:, :, 8], -1.0, None, ALU.mult, None, [b_P], [b_A8])
    p.copy("dve", AiN[:, 1, :], Pim[:, :, 8], [b_P], [b_A8])
    hin = p.sbuf("hin", [128, 2, 32], F32)
    b_hin = p.buf("hin")
    p.memset("pool", hin[:], 0.0, [b_hin])

    if full:
        Ct = [f16(), f16()]
        b_Ct = p.buf("Ct")
        xc_bufs = [(tcarve(256).bitcast(F32), p.dbuf("xc")) for _ in range(2)]
        ii = 0
        for ri, csrc in enumerate((c_re, c_im)):
            for tb4 in range(4):
                xc, b_xc = xc_bufs[ii % 2]
                ii += 1
                for jl in range(8):
                    jj = tb4 * 8 + jl
                    p.dma("sp", xc[jl * 16:(jl + 1) * 16, :].rearrange("c (g p) -> c g p", g=2),
                          csrc[2 * jj:2 * jj + 2].rearrange("g c p -> c g p"), writes=[b_xc])
                bk = 4 + (ii % 2)
                p.tr(cx.banks[bk][:, 0:128], xc, cx.ident_f[:], [b_xc, cx.b_ident], [BB[bk]])
                p.copy("dve", Ct[ri][:, tb4 * 8:(tb4 + 1) * 8, :], cx.banks[bk][:, 0:128].rearrange("p (j c) -> p j c", c=16), [], [BB[bk], b_Ct])
        if int(os.environ.get("S5_PSTOP", "99")) <= 1:
            cx.finish([b_outd])
            return nc
        G = region(R_XT, 9216).rearrange("p (j r n c) -> p j r n c", j=32, r=2, n=9)
        b_G = p.buf("G")
        for n in range(9):
            pr_bc = Pre[:, :, n].unsqueeze(2).to_broadcast(sh16)
            pi_bc = Pim[:, :, n].unsqueeze(2).to_broadcast(sh16)
            p.tt("dve", t1, Ct[0], pr_bc, ALU.mult, [b_Ct, b_P], [b_T])
            p.tt("dve", t2, Ct[1], pi_bc, ALU.mult, [b_Ct, b_P], [b_T])
            p.tt("dve", G[:, :, 0, n, :], t1, t2, ALU.subtract, [b_T], [b_G])
            p.tt("dve", t3, Ct[0], pi_bc, ALU.mult, [b_Ct, b_P], [b_T])
            p.tt("dve", t4, Ct[1], pr_bc, ALU.mult, [b_Ct, b_P], [b_T])
            p.stt(G[:, :, 1, n, :], t3, -1.0, t4, ALU.mult, ALU.subtract, [b_T], [b_G])
        if int(os.environ.get("S5_PSTOP", "99")) <= 2:
            cx.finish([b_outd])
            return nc
        WT = region(R_XT + 9216, 3840).rearrange("p (b g w) -> p b g w", b=8, g=2)
        b_WT = p.buf("WT")
        p.memset("dve", WT, 0.0, [b_WT])
        bbpad = tcarve(8192)
        b_bbp = p.buf("bbpad")
        p.memset("dve", bbpad, 0.0, [b_bbp])
        for ri, bsrc in enumerate((bre, bim)):
            for half in range(2):
                rows = slice(64 * half, 64 * half + 64)
                base = bbpad[rows, :]
                dst = bass.AP(base.tensor, base.offset + ri * 128 + 16 * half, [list(base.ap[0]), [1024, 8], [256 + 32, 4], [1, 16]])
                p.copy("dve", dst, bsrc[rows].rearrange("p (b q) c -> p b q c", q=4), [b_bar], [b_bbp])
        bbp = bbpad.rearrange("p (b q r m) -> p b q r m", b=8, q=4, r=2)
        mask16 = p.sbuf("mask16", [128, 16], F32)
        rowm = p.sbuf("rowm", [128, 2], F32)
        b_mk = p.buf("mk")
        p.add("dve", lambda e: e.tensor_reduce(out=mask16[:], in_=cx.ident_f[:].rearrange("p (k c) -> p c k", c=16), axis=AX.X, op=ALU.add), [cx.b_ident], [b_mk])
        for g2 in range(2):
            p.add("dve", lambda e, g2=g2: e.tensor_reduce(out=rowm[:, g2:g2 + 1], in_=cx.ident_f[:].rearrange("p (k h c) -> p h k c", h=2, c=16)[:, g2],
                                                          axis=AX.XY, op=ALU.add), [cx.b_ident], [b_mk])
        p.tr(cx.banks[7][:, 0:8], dl[:], cx.ident_f[0:8, 0:8], [b_dl, cx.b_ident], [BB[7]])
        dmk = p.sbuf("dmk", [128, 2, 8], F32)
        for g2 in range(2):
            p.ts("dve", dmk[:, g2, :], cx.banks[7][:, 0:8], rowm[:, g2:g2 + 1], None, ALU.mult, None, [b_mk], [BB[7], b_mk])
        if int(os.environ.get("S5_PSTOP", "99")) <= 3:
            cx.finish([b_outd])
            return nc
        for b in range(8):
            for g2 in range(2):
                bk = 4 + ((b * 2 + g2) % 2)
                rows = slice(64 * g2, 64 * g2 + 64)
                kps = cx.banks[bk][:, 0:128]
                i = 0
                for q in range(4):
                    for ri in range(2):
                        p.mm(kps, bbp[rows, b, q, ri, :], G[rows, 4 * b + q, ri, 0:8, :], i == 0, i == 7, [b_bbp, b_G], [BB[bk]])
                        i += 1
                p.stt(WT[:, b, g2, 112:128], mask16[:], dmk[:, g2, b:b + 1], kps[:, 0:16], ALU.mult, ALU.add, [b_mk], [BB[bk], b_WT])
                p.copy("act", WT[:, b, g2, 128:240], kps[:, 16:128], [], [BB[bk], b_WT])
        if int(os.environ.get("S5_PSTOP", "99")) <= 4:
            cx.finish([b_outd])
            return nc
        At = p.sbuf("Atot", [128, 2, 32], F32)
        tq = p.sbuf("tq", [128, 4, 32], F32)
        b_At = p.buf("At")
        p.copy("dve", At[:, 0, :], Pre[:, :, 8], [b_P], [b_At])
        p.copy("dve", At[:, 1, :], Pim[:, :, 8], [b_P], [b_At])
        for _ in range(8):
            p.tt("dve", tq[:, 0, :], At[:, 0, :], At[:, 0, :], ALU.mult, [b_At], [b_At])
            p.tt("dve", tq[:, 1, :], At[:, 1, :], At[:, 1, :], ALU.mult, [b_At], [b_At])
            p.tt("dve", tq[:, 2, :], At[:, 0, :], At[:, 1, :], ALU.mult, [b_At], [b_At])
            p.tt("dve", At[:, 0, :], tq[:, 0, :], tq[:, 1, :], ALU.subtract, [b_At], [b_At])
            p.ts("dve", At[:, 1, :], tq[:, 2, :], 2.0, None, ALU.mult, None, [b_At], [b_At])
        hn = p.sbuf("hnew", [128, 2, 32], F32)
        rdb = [io["b_bout"]] if (io is not None and "b_bout" in io) else []
        sHt = p.sbuf("sH_sb", [128, NCORES, 2, 32], F32)
        b_sHt = p.dbuf("sHt")
        p.dma("sp", sHt[:], ssum.rearrange("c p (r j) -> p c r j", r=2), reads=rdb, writes=[b_sHt])
        for c in range(NCORES):
            p.tt("dve", tq[:, 0, :], At[:, 0, :], hin[:, 0, :], ALU.mult, [b_At, b_hin], [b_At])
            p.tt("dve", tq[:, 1, :], At[:, 1, :], hin[:, 1, :], ALU.mult, [b_At, b_hin], [b_At])
            p.tt("dve", tq[:, 2, :], At[:, 0, :], hin[:, 1, :], ALU.mult, [b_At, b_hin], [b_At])
            p.tt("dve", tq[:, 3, :], At[:, 1, :], hin[:, 0, :], ALU.mult, [b_At, b_hin], [b_At])
            p.tt("dve", hn[:, 0, :], tq[:, 0, :], tq[:, 1, :], ALU.subtract, [b_At], [b_At])
            p.tt("dve", hn[:, 1, :], tq[:, 2, :], tq[:, 3, :], ALU.add, [b_At], [b_At])
            p.tt("dve", hn[:], hn[:], sHt[:, c], ALU.add, [b_At, b_sHt], [b_At])
            p.tt("dve", hn[:], hn[:], hin[:], ALU.subtract, [b_At, b_hin], [b_At])
            p.stt(hin[:], hn[:], cm[:, c:c + 1], hin[:], ALU.mult, ALU.add, [b_At, b_cm, b_hin], [b_hin])

    if full and int(os.environ.get("S5_STOP", "99")) <= 1:
        return cx.done(own, [b_outd])
    p.barrier()
    HZ = region(R_HZ, 32768).bitcast(F32).rearrange("p (r j k) -> p r j k", r=2, j=32)
    b_HZ = p.buf("HZ")
    ib = 0
    for jt in range(32):
        b, q = jt // 4, jt % 4
        rows = slice(32 * q, 32 * q + 32)
        for ri in range(2):
            bk = ib % 8
            ib += 1
            zps = cx.banks[bk][:, 0:NCH8]
            for s in range(8):
                p.mm(zps, WZ[rows, b, s, ri, :], uT[rows, b, s:TOK:8], s == 0, s == 7, [b_WZ, b_uT[b]], [BB[bk]], tp=(32 * q, 0))
            p.copy("act" if ib % 2 == 0 else "dve", HZ[:, ri, jt, :], zps, [], [BB[bk], b_HZ])
    p.barrier()

    AA4 = p.sbuf("AA4", [128, 2, 2, 32], F32)
    b_AA4 = p.buf("AA4")
    for r in range(2):
        p.copy("dve", AA4[:, 0, r, :], ArAr[:, r, :], [b_A8], [b_AA4])
    p.ts("dve", AA4[:, 1, 0, :], AiN[:, 0, :], -1.0, None, ALU.mult, None, [b_A8], [b_AA4])
    p.copy("dve", AA4[:, 1, 1, :], AiN[:, 0, :], [b_A8], [b_AA4])
    sx_ = p.sbuf("scanX", [128, 2, 2, 32], F32)
    st_ = p.sbuf("scanT", [128, 2, 32], F32)
    b_sc = p.buf("scan")
    xb_ = sx_[:, 1, 1, :]
    x1rev = bass.AP(xb_.tensor, xb_.offset, [list(xb_.ap[0]), [-32, 2], [1, 32]])
    NSTEP = int(os.environ.get("S5_NSTEP", str(NCH8)))
    for k in range(NSTEP):
        if k == 0:
            base = hin[:, 0, :]
            prev4 = bass.AP(base.tensor, base.offset, [list(base.ap[0]), [0, 2], [32, 2], [1, 32]])
        else:
            base = HZ[:, 0, :, k - 1]
            prev4 = bass.AP(base.tensor, base.offset, [list(base.ap[0]), [0, 2], [32 * NCH8, 2], [NCH8, 32]])
        p.tt("dve", sx_[:], prev4, AA4[:], ALU.mult, [b_HZ, b_hin, b_AA4], [b_sc])
        p.tt("dve", st_[:], sx_[:, 0, :, :], x1rev, ALU.add, [b_sc], [b_sc])
        p.tt("dve", HZ[:, :, :, k], HZ[:, :, :, k], st_[:], ALU.add, [b_sc], [b_HZ])
    if not full:
        fin = p.sbuf("fin", [128, 2, 32], F32)
        b_fin = p.buf("fin")
        p.copy("dve", fin[:], HZ[:, :, :, NCH8 - 1], [b_HZ], [b_fin])
        emit_summary(cx, io, [(fin[:].rearrange("p r j -> p (r j)"), [b_fin], 0, 64)], osum, b_outd)
        return cx.done(own, [b_outd])
    if full and int(os.environ.get("S5_STOP", "99")) <= 3:
        return cx.done(own, [b_outd])
    p.barrier()
    Hbf = region(R_HB, 16384).rearrange("p (r j k) -> p r j k", r=2, j=32)
    b_Hbf = p.buf("Hbf")
    p.copy("dve", Hbf[:, :, :, 0], hin[:], [b_hin], [b_Hbf])
    for r in range(2):
        p.copy("act" if r == 0 else "dve", Hbf[:, r, :, 1:NCH8], HZ[:, r, :, 0:NCH8 - 1], [b_HZ], [b_Hbf])
    p.barrier()

    if full and int(os.environ.get("S5_STOP", "99")) <= 4:
        return cx.done(own, [b_outd])
    y8b = region(R_HZ, 16384).rearrange("p (h t c) -> p h t c", h=2, t=8)
    b_y8 = [p.buf(f"y8_{h}") for h in range(2)]
    W4 = R_XT + 13056
    it_bufs = [(region(W4 + i * 512, 512).bitcast(F32), p.buf("itmp")) for i in range(2)]
    ys_bufs = [(region(W4 + 1024 + i * 512, 512).bitcast(F32), p.buf("ysum")) for i in range(2)]
    g1_bufs = [(region(W4 + 2048 + i * 512, 512).bitcast(F32), p.buf("g1")) for i in range(2)]
    GC = 2.0 * math.sqrt(2.0 / math.pi)
    ib = 0
    for half in range(2):
        for jt in range(32):
            b, q = jt // 4, jt % 4
            rows = slice(32 * q, 32 * q + 32)
            tbk = ib % 2
            i0, i1 = 2 + (ib % 2) * 2, 3 + (ib % 2) * 2
            tps = cx.banks[tbk][:, 0:256]
            for g2 in range(2):
                for s in range(8):
                    c0 = half * 1024 + s
                    p.mm(tps[:, g2 * 128:(g2 + 1) * 128], uT[rows, b, c0:(half + 1) * 1024:8], WT[rows, b, g2, (7 - s) * 16:(7 - s) * 16 + 128], s == 0, s == 7,
                         [b_uT[b], b_WT], [BB[tbk]], tp=(32 * q, 0))
            for g2 in range(2):
                ibk = i0 if g2 == 0 else i1
                hr = slice(64 * g2, 64 * g2 + 64)
                for ri in range(2):
                    p.mm(cx.banks[ibk][:, 0:128], Hbf[hr, ri, jt, half * 128:(half + 1) * 128], G[hr, jt, ri, 1:9, :], ri == 0, ri == 1, [b_Hbf, b_G], [BB[ibk]])
            it, b_it = it_bufs[ib % 2]
            ys, b_ys = ys_bufs[ib % 2]
            g1, b_g1 = g1_bufs[ib % 2]
            p.copy("act", it[:, 0:128], cx.banks[i0][:, 0:128], [], [BB[i0], b_it])
            p.copy("act", it[:, 128:256], cx.banks[i1][:, 0:128], [], [BB[i1], b_it])
            p.tt("dve", ys, tps, it, ALU.add, [b_it], [BB[tbk], b_ys])
            p.act(g1, ys, AF.Square, [b_ys], [b_g1])
            p.ts("dve", g1, g1, 0.044715, 1.0, ALU.mult, ALU.add, [b_g1], [b_g1])
            p.tt("dve", g1, g1, ys, ALU.mult, [b_g1, b_ys], [b_g1])
            p.act(g1, g1, AF.Sigmoid, [b_g1], [b_g1], scale=GC)
            dst = y8b[:, half, :, jt * 32:(jt + 1) * 32].rearrange("p t (g c) -> p g t c", g=2)
            p.tt("dve", dst, g1.rearrange("p (g t c) -> p g t c", g=2, t=8), ys.rearrange("p (g t c) -> p g t c", g=2, t=8), ALU.mult, [b_g1, b_ys], [b_y8[half]])
            ib += 1
    p.barrier()

    if full and int(os.environ.get("S5_STOP", "99")) <= 5:
        return cx.done(own, [b_outd])
    yT = region(R_HZ + 16384, KT * TOK).rearrange("p (k t) -> p k t", k=KT)
    b_yT = [p.buf(f"yT{t}") for t in range(NTILE)]
    wo_sb = region(R_UT, 16384).rearrange("p (k c) -> p k c", k=KT)
    b_wo = p.dbuf("wo")
    wo_v = w_out.rearrange("(kt p) c -> p kt c", p=128)
    for qd in range(4):
        p.dma("pool", wo_sb[:, :, qd * 512:(qd + 1) * 512], wo_v[:, :, qd * 512:(qd + 1) * 512], writes=[b_wo])
    for ti in range(NTILE):
        half, t = ti // 8, ti % 8
        bk = 6 + (ti % 2)
        psb = cx.banks[bk][:].bitcast(BF16)
        for blk in range(8):
            p.tr(psb[:, blk * 128:(blk + 1) * 128], y8b[:, half, t, blk * 128:(blk + 1) * 128], cx.ident_bf[:], [b_y8[half], cx.b_ident], [BB[bk]])
        p.copy("act" if ti % 2 == 0 else "dve", yT[:, :, ti * 128:(ti + 1) * 128], psb.rearrange("p (k t) -> p k t", k=KT), [], [BB[bk], b_yT[ti]])
    off = R_HB
    def carve(n):
        nonlocal off
        v = arena[:, off:off + n]
        off += n
        assert off <= ARENA
        return v
    xr_bufs = [(carve(2048).bitcast(F32), p.dbuf("xr")) for _ in range(2)]
    y_bufs = [(carve(2048).bitcast(F32), p.buf("y")) for _ in range(2)]
    o_bufs = [(carve(2048).bitcast(F32), p.buf("o")) for _ in range(2)]
    tmp = carve(2048).bitcast(F32)
    b_tmp = p.buf("tmp")
    xbf = carve(1024)
    b_xbf = p.buf("xbf")
    hs = carve(2048).bitcast(F32)
    b_hs = p.buf("hs")
    sg = carve(2048).bitcast(F32)
    b_sg = p.buf("sg")
    st = p.sbuf("st1", [128, 16], F32)
    b_st = p.buf("st1")
    b_x1s = p.dbuf("x1s")
    x1T = region(R_XT, KT * TOK).rearrange("p (k t) -> p k t", k=KT)
    b_x1T = [p.buf(f"x1T{t}") for t in range(NTILE)]
    for ti in range(NTILE):
        ts_ = slice(ti * 128, (ti + 1) * 128)
        for qd in range(4):
            for blk in range(8):
                p.mm(cx.banks[qd][:], yT[:, blk, ts_], wo_sb[:, blk, qd * 512:(qd + 1) * 512], blk == 0, blk == 7, [b_yT[ti], b_wo], [BB[qd]])
        for hh in range(2):
            p.act(sg[:, hh * 512:(hh + 1) * 512], cx.banks[2 + hh][:], AF.Sigmoid, [], [BB[2 + hh], b_sg])
            p.tt("dve", hs[:, hh * 512:(hh + 1) * 512], sg[:, hh * 512:(hh + 1) * 512], cx.banks[hh][:], ALU.mult, [b_sg], [BB[hh], b_hs])
        residual_ln1_tile(cx, ti, [hs[:, 0:512], hs[:, 512:1024]], [b_hs, b_hs], x, xr_bufs, y_bufs, o_bufs, tmp, b_tmp, st, b_st,
                          lng[:, 0, :], lnb[:, 0, :], b_gb, x1s, b_x1s, x1T, b_x1T, xbf, b_xbf, None, row_fn=tile_rows_s5)
    p.barrier()
    ffn_phase(cx, x1T, b_x1T, x1s, b_x1s, wgu, wd, lng[:, 1, :], lnb[:, 1, :], b_gb, y_out, b_outd, arena, KT * TOK, row_fn=tile_rows_s5)
    return cx.done(own, [b_outd])


POOL_UNITS = 106400
SUMW = {0: 1032, 1: 520, 2: 64}


def halo_exchange(cx, x_in, io):
    p = cx.p
    oh, b_oh = cx.const_load("ohot_h", [128, NCORES], F32, io["ohot"])
    ph, b_ph = cx.const_load("phot_h", [128, NCORES], F32, io["phot"])
    last = p.sbuf("hl_last", [3, D], F32)
    b_last = p.dbuf("hl_last")
    p.dma("sp", last[:], x_in[TOK - 3:TOK, :], writes=[b_last])
    b_hin = p.dbuf("hl_in")
    tmps = [(p.sbuf(p.uid("hlt"), [3, D], F32), p.buf("hlt")) for _ in range(2)]
    hin3 = io["hin2d"].rearrange("(c r) d -> c r d", r=3)
    for r in range(NCORES):
        t, b_t = tmps[r % 2]
        p.ts("dve", t[:], last[:], oh[0:3, r:r + 1], None, ALU.mult, None, [b_last, b_oh], [b_t])
        p.dma("sp", hin3[r], t[:], reads=[b_t], writes=[b_hin])
    b_hout = p.dbuf("hl_out", unit=1)
    in2d, out2d = io["hin2d"], io["hout2d"]
    p.coll(lambda e: e.collective_compute("AllReduce", ALU.add, replica_groups=[list(range(NCORES))], ins=[in2d.opt()], outs=[out2d.opt()]),
           [b_hin], [b_hout])
    g = p.sbuf("hl_g", [3, NCORES, D], F32)
    b_g = p.dbuf("hl_g")
    p.dma("sp", g[:], io["hout2d"].rearrange("(c r) d -> r c d", r=3), reads=[b_hout], writes=[b_g])
    acc = p.sbuf("hl_acc", [3, D], F32)
    b_acc = p.buf("hl_acc")
    p.memset("dve", acc[:], 0.0, [b_acc])
    for r in range(NCORES):
        p.stt(acc[:], g[:, r, :], ph[0:3, r:r + 1], acc[:], ALU.mult, ALU.add, [b_g, b_ph, b_acc], [b_acc])
    b_xh = p.dbuf("xh_d")
    p.dma("sp", io["xh"], acc[:], reads=[b_acc], writes=[b_xh])


def build_fused():
    nc = bass.Bass("TRN2", target_bir_lowering=False)
    cx = Ctx(nc, pool_units=POOL_UNITS)
    cx.mark_persistent()
    p = cx.p

    def ein(name, shape):
        return nc.dram_tensor(name, shape, F32, kind="ExternalInput").ap()

    x = ein("x", [TOK, D])
    hg_w_in = ein("hgrn_w_in", [2, D, 4 * D])
    hg_nw = ein("hgrn_norm_w", [2, 128])
    hg_wo = ein("hgrn_w_out", [2, D, D])
    hg_lb = ein("hgrn_lb_logits", [4, D])
    ml_w_in = ein("mlstm_w_in", [1, D, ML_W])
    ml_cw = ein("mlstm_conv_w", [1, 4, D])
    ml_gb = ein("mlstm_gate_b", [16, 1])
    ml_nw = ein("mlstm_norm_w", [1, D])
    ml_wo = ein("mlstm_w_out", [1, D, D])
    s5_wi = ein("s5_w_in", [1, D, D])
    s5_are = ein("s5_a_re", [32, 128])
    s5_aim = ein("s5_a_im", [32, 128])
    s5_ldt = ein("s5_ldt", [32, 128])
    s5_bre = ein("s5_b_re", [64, 64, 16])
    s5_bim = ein("s5_b_im", [64, 64, 16])
    s5_cre = ein("s5_c_re", [64, 16, 64])
    s5_cim = ein("s5_c_im", [64, 16, 64])
    s5_d = ein("s5_d", [8, 128])
    s5_wo = ein("s5_w_out", [1, D, 2 * D])
    wgu = ein("ffn_w_gate_up", [DEPTH, D, 2 * FFN_H])
    wd = ein("ffn_w_down", [DEPTH, FFN_H, D])
    ln_g = ein("ln_g", [DEPTH, 2, D])
    ln_b = ein("ln_b", [DEPTH, 2, D])
    cmask = ein("cmask", [128, NCORES])
    ohot = ein("ohot", [128, NCORES])
    phot = ein("phot", [128, NCORES])
    y = nc.dram_tensor("y", [TOK, D], F32, kind="ExternalOutput").ap()
    xbuf = [nc.dram_tensor(f"xbuf{i}", [TOK, D], F32).ap() for i in range(2)]
    x1s = nc.dram_tensor("x1s", [TOK, D], F32).ap()
    xh = nc.dram_tensor("xh_scr", [3, D], F32).ap()
    hin2d = nc.dram_tensor("halo_in", [NCORES * 3, D], F32).ap()
    hout2d = nc.dram_tensor("halo_out", [NCORES * 3, D], F32).ap()

    first = True
    x_in = x
    for i in range(DEPTH):
        kind, j = i % 3, i // 3
        W = SUMW[kind]
        bin2d = nc.dram_tensor(f"bin{i}", [NCORES * 128, W], F32).ap()
        bout2d = nc.dram_tensor(f"bout{i}", [NCORES * 128, W], F32).ap()
        bin3 = bin2d.rearrange("(c p) w -> c p w", p=128)
        bout3 = bout2d.rearrange("(c p) w -> c p w", p=128)
        y_dst = y if i == DEPTH - 1 else xbuf[i % 2]
        ffn = {"ln_g": ln_g[i], "ln_b": ln_b[i], "wgu": wgu[i], "wd": wd[i], "x1s": x1s, "cmask": cmask, "ssum": bout3, "y": y_dst}
        if kind == 0:
            base = {"x": x_in, "w_in": hg_w_in[j], "lb_logits": hg_lb}
            extra = {"norm_w": hg_nw[j:j + 1, :], "w_out": hg_wo[j]}
            fn = build_hgrn
        elif kind == 1:
            if not first:
                cx.new_stage()
            first = False
            halo_exchange(cx, x_in, {"ohot": ohot, "phot": phot, "hin2d": hin2d, "hout2d": hout2d, "xh": xh})
            base = {"x": x_in, "xh": xh, "w_in": ml_w_in[j], "conv_w": ml_cw[j], "gate_b": ml_gb}
            extra = {"norm_w": ml_nw[j:j + 1, :], "w_out": ml_wo[j]}
            fn = build_mlstm
        else:
            base = {"x": x_in, "w_in": s5_wi[j], "a_re": s5_are, "a_im": s5_aim, "ldt": s5_ldt, "b_re": s5_bre, "b_im": s5_bim}
            extra = {"c_re": s5_cre, "c_im": s5_cim, "d_skip": s5_d, "w_out": s5_wo[j]}
            fn = build_s5
        if not first:
            cx.new_stage()
        first = False
        b_bout = p.dbuf("bout", unit=1)
        fn(i, j, "A", cx=cx, io=dict(base, ohot=ohot, bin=bin3, bin2d=bin2d, bout2d=bout2d, b_bout=b_bout))
        cx.new_stage()
        fn(i, j, "B", cx=cx, io=dict(base, **extra, **ffn, b_bout=b_bout))
        x_in = y_dst
    p.barrier()
    cx.finish(cx.pending_out)
    return nc


_NC_CACHE = {}


def _masks(c):
    cm = np.zeros((128, NCORES), np.float32)
    cm[:, :c] = 1.0
    oh = np.zeros((128, NCORES), np.float32)
    oh[:, c] = 1.0
    ph = np.zeros((128, NCORES), np.float32)
    if c > 0:
        ph[:, c - 1] = 1.0
    return cm, oh, ph


def kernel(x, hgrn_w_in, hgrn_norm_w, hgrn_w_out, hgrn_lb_logits, mlstm_w_in, mlstm_conv_w, mlstm_gate_b, mlstm_norm_w,
           mlstm_w_out, s5_w_in, s5_a_re, s5_a_im, s5_log_dt, s5_b_re, s5_b_im, s5_c_re, s5_c_im, s5_d, s5_w_out,
           ffn_w_gate_up, ffn_w_down, ln_g, ln_b):
    f = lambda a: np.ascontiguousarray(np.asarray(a, dtype=np.float32))
    if "fused" not in _NC_CACHE:
        _NC_CACHE["fused"] = build_fused()
    nc = _NC_CACHE["fused"]
    ldt = np.ascontiguousarray(np.broadcast_to(f(s5_log_dt)[0].reshape(32, 2, 1), (32, 2, 64)).reshape(32, 128))
    shared = {
        "hgrn_w_in": f(hgrn_w_in), "hgrn_norm_w": f(hgrn_norm_w), "hgrn_w_out": f(hgrn_w_out), "hgrn_lb_logits": f(hgrn_lb_logits),
        "mlstm_w_in": f(mlstm_w_in), "mlstm_conv_w": f(mlstm_conv_w), "mlstm_gate_b": f(mlstm_gate_b).reshape(16, 1),
        "mlstm_norm_w": f(mlstm_norm_w), "mlstm_w_out": f(mlstm_w_out),
        "s5_w_in": f(s5_w_in), "s5_a_re": f(s5_a_re).reshape(32, 128), "s5_a_im": f(s5_a_im).reshape(32, 128), "s5_ldt": ldt,
        "s5_b_re": f(s5_b_re)[0], "s5_b_im": f(s5_b_im)[0], "s5_c_re": f(s5_c_re)[0], "s5_c_im": f(s5_c_im)[0],
        "s5_d": f(s5_d).reshape(8, 128), "s5_w_out": f(s5_w_out),
        "ffn_w_gate_up": f(ffn_w_gate_up), "ffn_w_down": f(ffn_w_down), "ln_g": f(ln_g), "ln_b": f(ln_b),
    }
    xf = f(x)[0]
    in_maps = []
    for c in range(NCORES):
        cm, oh, ph = _masks(c)
        in_maps.append(dict(shared, x=np.ascontiguousarray(xf[c * TOK:(c + 1) * TOK]), cmask=cm, ohot=oh, phot=ph))
    res = run_bass_kernel_spmd(nc, in_maps, core_ids=list(range(NCORES)))
    return np.concatenate([res.results[c]["y"] for c in range(NCORES)], axis=0)[None].astype(np.float32)
```
